# Optimizing a Trainium2 kernel written in Bass

```python
import jax, jax.numpy as jnp
from jax import lax
import numpy as np

D_MODEL = 1024
BATCH = 8
SEQ = 4096
DEPTH = 2

N_EVEN = (DEPTH + 1) // 2
N_ODD = DEPTH // 2
NORM_EPS = 1e-6

RG_WIDTH = D_MODEL
RG_HEADS = 8
RG_HEAD_DIM = RG_WIDTH // RG_HEADS
RG_CONV = 4
RG_C = 8.0
RG_CONV_LEFT = 2
SC_WIDTH = D_MODEL
SC_CONV = 3
SC_CONV_LEFT = 1
EVEN_IN = 2 * RG_WIDTH + 4 * SC_WIDTH
EVEN_MIX = RG_WIDTH + SC_WIDTH
GLA_HEADS = 4
GLA_KEY = D_MODEL // 2
GLA_VAL = D_MODEL
GLA_DK = GLA_KEY // GLA_HEADS
GLA_DV = GLA_VAL // GLA_HEADS
GLA_RANK = 16
GLA_NORMALIZER = 16.0
GLA_CHUNK = 64
ODD_IN = 2 * GLA_KEY + 2 * GLA_VAL + 2 * GLA_RANK

kernel_name = 'hybrid_rglru_shortconv_gla_encoder'


def rmsnorm(x, g):
    xf = x.astype(jnp.float32)
    y = xf * lax.rsqrt(jnp.mean(xf * xf, axis=-1, keepdims=True) + NORM_EPS)
    return (y * g.astype(jnp.float32)).astype(x.dtype)


def centred_dwconv(u, w, left):
    seq = u.shape[1]
    width = w.shape[0]
    up = jnp.pad(u, ((0, 0), (left, width - 1 - left), (0, 0)))
    out = up[:, 0:seq] * w[0]
    for k in range(1, width):
        out = out + up[:, k:k + seq] * w[k]
    return out


def rg_lru(u, gate_w, gate_b, lam, reverse):
    bsz, seq, width = u.shape
    uf = u.astype(jnp.float32)
    uh = uf.reshape(bsz, seq, RG_HEADS, RG_HEAD_DIM)
    gates = jax.nn.sigmoid(
        jnp.einsum('bshi,ghij->gbshj', uh, gate_w.astype(jnp.float32))
        + gate_b[:, None, None].astype(jnp.float32))
    r = gates[0].reshape(bsz, seq, width)
    i = gates[1].reshape(bsz, seq, width)
    log_a = -RG_C * r * jax.nn.softplus(-lam.astype(jnp.float32))
    a = jnp.exp(log_a)
    b = jnp.sqrt(-jnp.expm1(2.0 * log_a)) * (i * uf)

    def combine(left, right):
        a_l, b_l = left
        a_r, b_r = right
        return (a_l * a_r, a_r * b_l + b_r)

    _, h = lax.associative_scan(combine, (a, b), reverse=reverse, axis=1)
    return h


def gla_chunked(q, k, v, g):
    bsz, nh, seq, dk = q.shape
    dv = v.shape[-1]
    nc = seq // GLA_CHUNK
    qc = q.astype(jnp.float32).reshape(bsz, nh, nc, GLA_CHUNK, dk) * (dk ** -0.5)
    kc = k.astype(jnp.float32).reshape(bsz, nh, nc, GLA_CHUNK, dk)
    vc = v.astype(jnp.float32).reshape(bsz, nh, nc, GLA_CHUNK, dv)
    gc = g.reshape(bsz, nh, nc, GLA_CHUNK, dk)
    bcum = jnp.cumsum(gc, axis=3)
    btot = bcum[..., -1:, :]
    q_in = qc * jnp.exp(bcum)
    k_in = kc * jnp.exp(-bcum)
    k_st = kc * jnp.exp(btot - bcum)
    mask = jnp.tril(jnp.ones((GLA_CHUNK, GLA_CHUNK), dtype=bool))
    scores = jnp.einsum('bhnid,bhnjd->bhnij', q_in, k_in)
    scores = jnp.where(mask, scores, 0.0)
    o_intra = jnp.einsum('bhnij,bhnje->bhnie', scores, vc)
    decay = jnp.exp(btot[..., 0, :])

    def step(state, xs):
        q_n, k_n, v_n, dec_n = xs
        o_n = jnp.einsum('bhcd,bhde->bhce', q_n, state)
        state = dec_n[..., None] * state + jnp.einsum('bhcd,bhce->bhde', k_n, v_n)
        return state, o_n

    xs = (jnp.moveaxis(q_in, 2, 0), jnp.moveaxis(k_st, 2, 0),
          jnp.moveaxis(vc, 2, 0), jnp.moveaxis(decay, 2, 0))
    state0 = jnp.zeros((bsz, nh, dk, dv), jnp.float32)
    _, o_inter = lax.scan(step, state0, xs)
    o_inter = jnp.moveaxis(o_inter, 0, 2)
    return (o_intra + o_inter).reshape(bsz, nh, seq, dv)


def even_layer(x, norm_pre, norm_post, w_in, conv_w, conv_b, gate_w, gate_b, lam, sc_w, w_out):
    h = rmsnorm(x, norm_pre)
    proj = h @ w_in
    xa, za, xb, gb, gc, zb = jnp.split(
        proj, [RG_WIDTH, 2 * RG_WIDTH, 2 * RG_WIDTH + SC_WIDTH,
               2 * RG_WIDTH + 2 * SC_WIDTH, 2 * RG_WIDTH + 3 * SC_WIDTH], axis=-1)
    ua = centred_dwconv(xa, conv_w, RG_CONV_LEFT) + conv_b
    ya = rg_lru(ua, gate_w[0], gate_b[0], lam[0], False) + rg_lru(ua, gate_w[1], gate_b[1], lam[1], True)
    ya = ya * jax.nn.silu(za.astype(jnp.float32))
    yb = gb * centred_dwconv(gc * xb, sc_w, SC_CONV_LEFT)
    yb = yb * jax.nn.silu(zb)
    y = jnp.concatenate([ya, yb.astype(jnp.float32)], axis=-1) @ w_out
    return (x + rmsnorm(y, norm_post)).astype(x.dtype)


def odd_layer(x, norm_pre, norm_post, w_in, w_gate_lr, b_gate, head_norm_g, w_out):
    bsz, seq, _ = x.shape
    h = rmsnorm(x, norm_pre)
    proj = h @ w_in
    q, k, v, r, lr = jnp.split(
        proj, [GLA_KEY, 2 * GLA_KEY, 2 * GLA_KEY + GLA_VAL, 2 * GLA_KEY + 2 * GLA_VAL], axis=-1)
    lr = lr.reshape(bsz, seq, 2, GLA_RANK).astype(jnp.float32)
    z = jnp.einsum('bsdr,drk->dbsk', lr, w_gate_lr.astype(jnp.float32)) + b_gate[:, None, None].astype(jnp.float32)
    log_alpha = jax.nn.log_sigmoid(z) / GLA_NORMALIZER

    def heads(t, dh):
        return jnp.transpose(t.reshape(bsz, seq, GLA_HEADS, dh), (0, 2, 1, 3))

    qh, kh, vh = heads(q, GLA_DK), heads(k, GLA_DK), heads(v, GLA_DV)
    g_f, g_b = heads(log_alpha[0], GLA_DK), heads(log_alpha[1], GLA_DK)
    o_f = gla_chunked(qh, kh, vh, g_f)
    o_b = jnp.flip(gla_chunked(jnp.flip(qh, 2), jnp.flip(kh, 2), jnp.flip(vh, 2), jnp.flip(g_b, 2)), 2)
    o = rmsnorm(o_f + o_b, head_norm_g)
    o = jnp.transpose(o, (0, 2, 1, 3)).reshape(bsz, seq, GLA_VAL)
    y = (o * jax.nn.silu(r.astype(jnp.float32))) @ w_out
    return (x + rmsnorm(y, norm_post)).astype(x.dtype)


def setup_inputs(seed: int = 0) -> dict:
    key = jax.random.key(seed)
    ks = jax.random.split(key, 20)
    nrm = jax.random.normal
    f32 = jnp.float32
    lam_a = jax.random.uniform(ks[8], (N_EVEN, 2, RG_WIDTH), f32, minval=0.9, maxval=0.999)
    lam_s = lam_a ** (1.0 / RG_C)
    rg_lambda = jnp.log(lam_s) - jnp.log1p(-lam_s)
    return {
        'x': nrm(ks[0], (BATCH, SEQ, D_MODEL), f32),
        'even_norm_pre': 1.0 + 0.05 * nrm(ks[1], (N_EVEN, D_MODEL), f32),
        'even_norm_post': 1.0 + 0.05 * nrm(ks[2], (N_EVEN, D_MODEL), f32),
        'even_w_in': nrm(ks[3], (N_EVEN, D_MODEL, EVEN_IN), f32) * D_MODEL ** -0.5,
        'rg_conv_w': nrm(ks[4], (N_EVEN, RG_CONV, RG_WIDTH), f32) * RG_CONV ** -0.5,
        'rg_conv_b': 0.01 * nrm(ks[5], (N_EVEN, RG_WIDTH), f32),
        'rg_gate_w': nrm(ks[6], (N_EVEN, 2, 2, RG_HEADS, RG_HEAD_DIM, RG_HEAD_DIM), f32) * RG_HEAD_DIM ** -0.5,
        'rg_gate_b': 0.01 * nrm(ks[7], (N_EVEN, 2, 2, RG_HEADS, RG_HEAD_DIM), f32),
        'rg_lambda': rg_lambda,
        'sc_conv_w': nrm(ks[9], (N_EVEN, SC_CONV, SC_WIDTH), f32) * SC_CONV ** -0.5,
        'even_w_out': nrm(ks[10], (N_EVEN, EVEN_MIX, D_MODEL), f32) * EVEN_MIX ** -0.5,
        'odd_norm_pre': 1.0 + 0.05 * nrm(ks[11], (N_ODD, D_MODEL), f32),
        'odd_norm_post': 1.0 + 0.05 * nrm(ks[12], (N_ODD, D_MODEL), f32),
        'odd_w_in': nrm(ks[13], (N_ODD, D_MODEL, ODD_IN), f32) * D_MODEL ** -0.5,
        'gla_w_gate_lr': nrm(ks[14], (N_ODD, 2, GLA_RANK, GLA_KEY), f32) * GLA_RANK ** -0.5,
        'gla_b_gate': 1.0 + 0.3 * nrm(ks[15], (N_ODD, 2, GLA_KEY), f32),
        'gla_norm_g': 1.0 + 0.05 * nrm(ks[16], (N_ODD, GLA_DV), f32),
        'odd_w_out': nrm(ks[17], (N_ODD, GLA_VAL, D_MODEL), f32) * GLA_VAL ** -0.5,
    }


def reference(x, even_norm_pre, even_norm_post, even_w_in, rg_conv_w, rg_conv_b, rg_gate_w,
              rg_gate_b, rg_lambda, sc_conv_w, even_w_out, odd_norm_pre, odd_norm_post,
              odd_w_in, gla_w_gate_lr, gla_b_gate, gla_norm_g, odd_w_out):
    for layer in range(DEPTH):
        j = layer // 2
        if layer % 2 == 0:
            x = even_layer(x, even_norm_pre[j], even_norm_post[j], even_w_in[j], rg_conv_w[j],
                           rg_conv_b[j], rg_gate_w[j], rg_gate_b[j], rg_lambda[j],
                           sc_conv_w[j], even_w_out[j])
        else:
            x = odd_layer(x, odd_norm_pre[j], odd_norm_post[j], odd_w_in[j], gla_w_gate_lr[j],
                          gla_b_gate[j], gla_norm_g[j], odd_w_out[j])
    return x
```

```python
import numpy as np
from contextlib import ExitStack
import concourse.bass as bass
import concourse.mybir as mybir
from concourse.bass_utils import run_bass_kernel_spmd
from concourse.ap import AP

F32 = mybir.dt.float32
BF16 = mybir.dt.bfloat16
AF = mybir.ActivationFunctionType
ALU = mybir.AluOpType

D = 1024
T = 512
EPS = 1e-6
NV = 18
LAYERS = 2
SBUF_LEFT = []


def rev(ap):
    pairs = [list(p) for p in ap.ap]
    step, cnt = pairs[-1]
    pairs[-1] = [-step, cnt]
    return AP(ap.tensor, ap.offset + step * (cnt - 1), pairs)


class _Op:
    __slots__ = ("eng", "fn", "deps", "dma_key", "signal", "val", "idx", "semkey")


class Prog:
    def __init__(self):
        self.ops = []
        self.lastw = {}
        self.rd_eng = {}
        self.rd_dma = {}
        self.last_eng = {}
        self.last_dma = {}
        self.fence = []
        self.passed = set()

    def barrier(self):
        self.fence = list(self.last_eng.values()) + list(self.last_dma.values())
        self.passed = set()

    def add(self, eng, fn, r=(), w=(), dma_key=None):
        op = _Op()
        op.eng, op.fn, op.dma_key = eng, fn, dma_key
        op.signal = dma_key is not None
        op.val = 0
        op.semkey = None
        op.idx = len(self.ops)
        deps = {}
        if self.fence and eng not in self.passed:
            for p in self.fence:
                deps[p.idx] = p
            self.passed.add(eng)
        for k in r:
            p = self.lastw.get(k)
            if p is not None:
                deps[p.idx] = p
        for k in w:
            p = self.lastw.get(k)
            if p is not None:
                deps[p.idx] = p
            for q in self.rd_eng.get(k, {}).values():
                deps[q.idx] = q
            for q in self.rd_dma.get(k, ()):
                deps[q.idx] = q
        op.deps = [p for p in deps.values() if (p.dma_key is not None) or (p.eng != eng) or (eng != "pe")]
        for k in r:
            if dma_key is not None:
                self.rd_dma.setdefault(k, []).append(op)
            else:
                self.rd_eng.setdefault(k, {})[eng] = op
        for k in w:
            self.lastw[k] = op
            self.rd_eng[k] = {}
            self.rd_dma[k] = []
        if dma_key is not None:
            self.last_dma[dma_key] = op
        else:
            self.last_eng[eng] = op
        self.ops.append(op)
        return op

    def emit(self, nc, es):
        for op in self.ops:
            for p in op.deps:
                p.signal = True
        cnt = {}
        for op in self.ops:
            if op.signal:
                key = ("dma", op.dma_key) if op.dma_key is not None else ("eng", op.eng)
                cnt[key] = cnt.get(key, 0) + (16 if op.dma_key is not None else 1)
                op.semkey, op.val = key, cnt[key]
        sems = {}
        for i, key in enumerate(cnt):
            sems[key] = es.enter_context(nc.semaphore("s%d" % i))
        self.nsem = len(sems)
        block = es.enter_context(nc.Block())
        ops = self.ops

        def run(engname, final=False):
            def body(e):
                waited = {}
                for op in ops:
                    if op.eng != engname:
                        continue
                    for p in op.deps:
                        if waited.get(p.semkey, 0) < p.val:
                            e.wait_ge(sems[p.semkey], p.val)
                            waited[p.semkey] = p.val
                    ins = op.fn(e)
                    if op.signal:
                        ins.then_inc(sems[op.semkey], 16 if op.dma_key is not None else 1)
                if final:
                    for key, v in cnt.items():
                        if waited.get(key, 0) < v:
                            e.wait_ge(sems[key], v)
            return body

        block.sync(run("sp", final=True))
        block.tensor(run("pe"))
        block.scalar(run("act"))
        block.vector(run("dve"))
        block.gpsimd(run("pool"))


def build_nc(SEQ, layers=LAYERS):
    NT = SEQ // T
    NCH = SEQ // 128
    BW = max(SEQ, 8 * T)
    nc = bass.Bass("TRN2", target_bir_lowering=False)
    din = lambda n, s, dt=F32: nc.dram_tensor(n, s, dt, kind="ExternalInput").ap()
    xT = din("xT", [D, SEQ])
    w_in0 = din("w_in0", [D, 6144])
    gate_w = din("gate_w", [2, 2, 8, 128, 128])
    w_out0 = din("w_out0", [2048, D])
    w_in1 = din("w_in1", [D, 3104])
    wg = din("wg", [2, 16, 512])
    w_out1 = din("w_out1", [D, D])
    pvec = din("pvec", [128, NV, 8])
    pg = din("pg", [128, 2, 4])
    png = din("png", [128, 2])
    cm01 = din("cm01", [128, T])
    ctri = din("ctri", [128, 2, T])
    cones = din("cones", [128, 128])
    cident = din("cident", [128, 128])
    outT = nc.dram_tensor("outT", [D, SEQ], F32, kind="ExternalOutput").ap()
    mixed0 = nc.dram_tensor("mixed0", [2048, SEQ], BF16, kind="Internal").ap()
    x1T = nc.dram_tensor("x1T", [D, SEQ], F32, kind="Internal").ap()
    qTs = nc.dram_tensor("qTs", [512, SEQ], F32, kind="Internal").ap()
    kTs = nc.dram_tensor("kTs", [512, SEQ], F32, kind="Internal").ap()
    Vs = nc.dram_tensor("Vs", [SEQ, D], BF16, kind="Internal").ap()
    SRs = nc.dram_tensor("SRs", [D, SEQ], BF16, kind="Internal").ap()

    es = ExitStack()
    P = Prog()
    sb = lambda n, s, dt=F32: es.enter_context(nc.sbuf_tensor(n, s, dt))
    ps = lambda n, s, dt=F32: es.enter_context(nc.psum_tensor(n, s, dt))

    hT = sb("hT", [128, 8, SEQ], BF16)
    pv = sb("pv", [128, NV, 8])
    pgt = sb("pgt", [128, 2, 4])
    pngt = sb("pngt", [128, 2])
    m01 = sb("m01", [128, T], BF16)
    tri = sb("tri", [128, 2, T])
    ones = sb("ones", [128, 128], BF16)
    ident = sb("ident", [128, 128], BF16)
    der = sb("der", [128, 12, 8])
    negb = sb("negb", [128, 2, 4])
    gwh = [sb("gwh%d" % i, [128, 4, 128], BF16) for i in range(2)]
    lrw = sb("lrw", [128, 8, 32], BF16)
    wgt = sb("wgt", [16, 2, 512], BF16)
    lrt = [sb("lrt%d" % d, [16, T], BF16) for d in range(2)]
    rs = [sb("rs%d" % i, [128, T]) for i in range(2)]
    wb = [sb("wb%d" % i, [128, 8, 1024], BF16) for i in range(2)]
    wbf = [wb[i][:].rearrange("p k n -> p (k n)").bitcast(F32) for i in range(2)]
    big = [sb("big%d" % i, [128, BW + 4]) for i in range(4)]
    bigh = [sb("bigh%d" % i, [128, BW], BF16) for i in range(2)]
    tmph = [[sb("tmph%d_%d" % (i, j), [128, T], BF16) for j in range(2)] for i in range(3)]
    zero1 = sb("zero1", [128, 1])
    Sst = [sb("Sst%d" % d, [128, 256]) for d in range(2)]
    Sbf = [sb("Sbf%d" % d, [128, 256], BF16) for d in range(2)]
    DEC = [sb("DEC%d" % d, [128, NT, 1]) for d in range(2)]
    KTt = [sb("KTt%d" % d, [128, 4, 128], BF16) for d in range(2)]
    xt = [big[2 + i][:, 0:8 * T].rearrange("p (k t) -> p k t", t=T) for i in range(2)]
    sqs = sb("sqs", [128, 2, T], BF16)

    SBUF_LEFT.append(nc.sbuf_bytes_remaining)
    pb = [ps("pb%d" % i, [128, T]) for i in range(7)]
    pbt = ps("pbt", [128, 1024], BF16)
    pbx = [pb[i][:] for i in range(7)] + [pbt[:].bitcast(F32)]
    pb6h = pb[6][:].bitcast(BF16)

    def kb(i, lo, hi):
        return [("big", i, g) for g in range(lo // T, (hi - 1) // T + 1)]

    def kh(i, lo, hi):
        return [("bigh", i, g) for g in range(lo // T, (hi - 1) // T + 1)]

    def khT(k, lo, hi):
        return [("hT", k, g) for g in range(lo // T, (hi - 1) // T + 1)]

    def bslot(i, n):
        return big[i][:, n * T:(n + 1) * T], [("big", i, n)]

    def wslot(i, n):
        return wbf[i][:, n * T:(n + 1) * T], [("wbf", i, n)]

    def bcast_last(ap, n):
        pairs = [list(p) for p in ap.ap]
        assert pairs[-1][1] == 1
        pairs[-1] = [0, n]
        return AP(ap.tensor, ap.offset, pairs)

    def dma(eng, out, in_, r, w, key):
        P.add(eng, lambda e: e.dma_start(out=out, in_=in_), r=r, w=w, dma_key=key)

    def act(out, in_, func, r, w, bias=0.0, scale=1.0):
        P.add("act", lambda e: e.activation(out=out, in_=in_, func=func, bias=bias, scale=scale), r=r, w=w)

    def tt(eng, out, in0, in1, op, r, w):
        P.add(eng, lambda e: e.tensor_tensor(out=out, in0=in0, in1=in1, op=op), r=r, w=w)

    def ts(eng, out, in0, s1, s2, op0, op1, r, w):
        if op1 is None:
            P.add(eng, lambda e: e.tensor_scalar(out=out, in0=in0, scalar1=s1, scalar2=None, op0=op0), r=r, w=w)
        else:
            P.add(eng, lambda e: e.tensor_scalar(out=out, in0=in0, scalar1=s1, scalar2=s2, op0=op0, op1=op1), r=r, w=w)

    def stt(eng, out, in0, s, in1, op0, op1, r, w):
        P.add(eng, lambda e: e.scalar_tensor_tensor(out=out, in0=in0, scalar=s, in1=in1, op0=op0, op1=op1), r=r, w=w)

    def mm(out, lhsT, rhs, start, stop, r, w):
        P.add("pe", lambda e: e.matmul(out, lhsT=lhsT, rhs=rhs, start=start, stop=stop), r=r, w=w)

    def scan(out, d0, d1, init, reverse, r, w):
        if reverse:
            P.add("dve", lambda e: e.tensor_tensor_scan(out=rev(out), data0=rev(d0), data1=rev(d1), initial=init,
                                                        op0=ALU.mult, op1=ALU.add), r=r, w=w)
        else:
            P.add("dve", lambda e: e.tensor_tensor_scan(out=out, data0=d0, data1=d1, initial=init,
                                                        op0=ALU.mult, op1=ALU.add), r=r, w=w)

    def tsl(j):
        return slice(j * T, (j + 1) * T)

    xtk = [kb(2 + i, 0, 8 * T) for i in range(2)]

    dma("sp", pv[:], pvec[:, :, :], [], ["pv"], "c0")
    dma("sp", pgt[:], pg[:, :, :], [], ["pgt"], "c1")
    dma("sp", pngt[:], png[:, :], [], ["pngt"], "c2")
    dma("pool", m01[:], cm01[:, :], [], ["m01"], "c3")
    dma("sp", tri[:], ctri[:, :, :], [], ["tri"], "c4")
    dma("pool", ones[:], cones[:, :], [], ["ones"], "c5")
    dma("pool", ident[:], cident[:, :], [], ["ident"], "c6")
    P.add("dve", lambda e: e.memset(zero1[:], 0.0), w=["zero1"])
    ts("dve", der[:, 0:4, :], pv[:, 7:11, :], 0.5, None, ALU.mult, None, ["pv"], ["der"])
    act(der[:, 8:10, :], pv[:, 11:13, :], AF.Exp, ["pv"], ["der8"], scale=-1.0)
    act(der[:, 10:12, :], der[:, 8:10, :], AF.Ln, ["der8"], ["der10"], bias=1.0)
    ts("dve", der[:, 4:6, :], der[:, 10:12, :], -4.0, None, ALU.mult, None, ["der10"], ["der"])
    ts("dve", der[:, 6:8, :], der[:, 10:12, :], -8.0, None, ALU.mult, None, ["der10"], ["der"])
    ts("dve", negb[:], pgt[:], -1.0, None, ALU.mult, None, ["pgt"], ["negb"])

    def rstd_from(psum_ap, b, n, mul=1.0):
        act(rs[b][:], psum_ap, AF.Ln, [("pb", 6)], [("rs", b)], bias=EPS, scale=1.0 / n)
        act(rs[b][:], rs[b][:], AF.Exp, [("rs", b)], [("rs", b)], scale=-0.5, bias=float(np.log(mul)))

    def norm_phase(src, gidx, tag):
        srcv = src.rearrange("(k p) t -> p k t", p=128)
        sq = bigh[1][:, 0:8 * T].rearrange("p (k t) -> p k t", t=T)
        sqk = kh(1, 0, 8 * T)
        for j in range(NT):
            b = j % 2
            dma("sp", xt[b], srcv[:, :, tsl(j)], [("src", tag, j)], xtk[b], "xt%d" % b)
            act(sq, xt[b], AF.Square, xtk[b], sqk)
            for k in range(8):
                mm(pb[6][:], ones[:], sq[:, k, :], k == 0, k == 7, ["ones"] + sqk, [("pb", 6)])
            rstd_from(pb[6][:], b, D)
            for k in range(8):
                stt("dve", hT[:, k, tsl(j)], xt[b][:, k, :], pv[:, gidx, k:k + 1], rs[b][:], ALU.mult, ALU.mult,
                    xtk[b] + [("rs", b), "pv"], khT(k, j * T, (j + 1) * T))

    def out_phase(mt_get, nk, wsrc, gidx, resid, dst, tag, sq, sqk):
        wo = wsrc.rearrange("(k p) n -> p k n", p=128)
        for k0 in range(0, nk, 4):
            s, q = k0 // 8, (k0 % 8) // 4
            dma("pool", wb[s][:, q * 4:q * 4 + 4, :], wo[:, k0:k0 + 4, :], [], [("wb", s, q)], "wb%d_%d" % (s, q))
        resv = resid.rearrange("(k p) t -> p k t", p=128)
        dstv = dst.rearrange("(k p) t -> p k t", p=128)
        ysb = [big[i][:, 0:8 * T].rearrange("p (k t) -> p k t", t=T) for i in range(2)]
        for j in range(NT):
            b = j % 2
            ys = ysb[b]
            mget, mkeys = mt_get(j)
            dma("sp", xt[b], resv[:, :, tsl(j)], [("src", tag, j)], xtk[b], "xt%d" % b)
            for o in range(8):
                bank = o % 4
                for k in range(nk):
                    mm(pb[bank][:], wb[k // 8][:, k % 8, o * 128:(o + 1) * 128], mget(k), k == 0, k == nk - 1,
                       [("wb", k // 8, (k % 8) // 4)] + mkeys, [("pb", bank)])
                act(ys[:, o, :], pb[bank][:], AF.Copy, [("pb", bank)], [("big", b, o)])
                act(sq[:, o, :], pb[bank][:], AF.Square, [("pb", bank)], sqk)
            for o in range(8):
                mm(pb[6][:], ones[:], sq[:, o, :], o == 0, o == 7, ["ones"] + sqk, [("pb", 6)])
            rstd_from(pb[6][:], b, D)
            for o in range(8):
                stt("dve", ys[:, o, :], ys[:, o, :], pv[:, gidx, o:o + 1], rs[b][:], ALU.mult, ALU.mult,
                    [("big", b, o), ("rs", b), "pv"], [("big", b, o)])
                tt("pool", ys[:, o, :], ys[:, o, :], xt[b][:, o, :], ALU.add, [("big", b, o)] + xtk[b], [("big", b, o)])
            dma("sp", dstv[:, :, tsl(j)], ys, [("big", b, o) for o in range(8)], [("src", tag + "o", j)], "xo%d" % b)

    norm_phase(xT, 0, "x0")
    w0 = w_in0.rearrange("(k p) n -> p k n", p=128)
    XA, UA, HB, CVb = big[0], big[1], big[2], big[3]
    UAb, SZ = bigh[0], bigh[1]
    P.add("pool", lambda e: e.memset(XA[:, 0:2], 0.0), w=kb(0, 0, 2))
    P.add("pool", lambda e: e.memset(XA[:, SEQ + 2:SEQ + 4], 0.0), w=kb(0, SEQ + 2, SEQ + 4))

    def rgA_front(c):
        s = c % 2
        wA = wb[0][:, :, s * 256:(s + 1) * 256]
        dma("pool", wA[:, :, 0:128], w0[:, :, c * 128:(c + 1) * 128], [], [("wA", s, 0)], "wb%d_0" % s)
        dma("pool", wA[:, :, 128:256], w0[:, :, 1024 + c * 128:1024 + (c + 1) * 128], [], [("wA", s, 1)], "wb%d_1" % s)
        dma("pool", gwh[s][:], gate_w[:, :, c].rearrange("d g i j -> i (d g) j"), [], [("gwh", s)], "gw%d" % s)
        yield
        for j in range(NT):
            bank = 6 + (j % 2)
            for k in range(8):
                mm(pbx[bank], wA[:, k, 0:128], hT[:, k, tsl(j)], k == 0, k == 7, [("wA", s, 0)] + khT(k, j * T, j * T + T), [("pb", bank)])
            yield
            act(XA[:, 2 + j * T:2 + (j + 1) * T], pbx[bank], AF.Copy, [("pb", bank)], kb(0, 2 + j * T, 2 + (j + 1) * T))
            yield

    def rgA_conv(c):
        allXA = kb(0, 0, SEQ + 4)
        allUA = kb(1, 0, SEQ)
        ts("dve", UA[:, 0:SEQ], XA[:, 0:SEQ], pv[:, 2, c:c + 1], pv[:, 6, c:c + 1], ALU.mult, ALU.add, allXA + ["pv"], allUA)
        for kk in range(1, 4):
            stt("dve", UA[:, 0:SEQ], XA[:, kk:kk + SEQ], pv[:, 2 + kk, c:c + 1], UA[:, 0:SEQ], ALU.mult, ALU.add, allXA + allUA + ["pv"], allUA)
        hlf = SEQ // 2
        act(UAb[:, 0:hlf], UA[:, 0:hlf], AF.Copy, kb(1, 0, hlf), kh(0, 0, hlf))
        P.add("pool", lambda e: e.tensor_copy(out=UAb[:, hlf:SEQ], in_=UA[:, hlf:SEQ]), r=kb(1, hlf, SEQ), w=kh(0, hlf, SEQ))

    def rg_dir(d, c):
        s = c % 2
        cs = slice(c * 128, (c + 1) * 128)
        wA = wb[0][:, :, s * 256:(s + 1) * 256]

        def slots(b):
            if d == 1:
                return [bslot(3, i * 2 + b) for i in range(4)]
            return [wslot(1, i * 2 + b) for i in range(4)]

        br, bi = (2, 3) if d == 1 else (0, 1)
        g_r = gwh[s][:, d * 2 + 0, :]
        g_i = gwh[s][:, d * 2 + 1, :]
        prev = None
        pend = []
        for t in range(NT):
            j = t if d == 0 else NT - 1 - t
            t_other = NT - 1 - j if d == 0 else j
            store = (t < t_other) or (t == t_other and d == 1)
            b = j % 2
            uk = kh(0, j * T, j * T + T)
            (THr, k0), (THi, k1), (A, k2), (Ht, k3) = slots(b)
            mm(pbx[br], g_r, UAb[:, tsl(j)], True, True, [("gwh", s)] + uk, [("pb", br)])
            act(THr, pbx[br], AF.Tanh, [("pb", br), "der"], k0, bias=der[:, d * 2 + 0, c:c + 1], scale=0.5)
            mm(pbx[bi], g_i, UAb[:, tsl(j)], True, True, [("gwh", s)] + uk, [("pb", bi)])
            act(THi, pbx[bi], AF.Tanh, [("pb", bi), "der"], k1, bias=der[:, d * 2 + 1, c:c + 1], scale=0.5)
            act(A, THr, AF.Exp, k0 + ["der"], k2, bias=der[:, 4 + d, c:c + 1], scale=der[:, 4 + d, c:c + 1])
            tt("pool", THr, A, A, ALU.mult, k2, k0)
            ts("pool", THr, THr, -1.0 / 16, 1.0 / 16, ALU.mult, ALU.add, k0, k0)
            stt("dve", THi, THi, 1.0, UA[:, tsl(j)], ALU.add, ALU.mult, k1 + kb(1, j * T, j * T + T), k1)
            pend.append((j, store))
            if t % 2 == 0 and t + 1 < NT:
                continue
            yield
            for (jj, _) in pend:
                (THr2, k02) = slots(jj % 2)[0]
                act(THr2, THr2, AF.Sqrt, k02, k02)
            yield
            for (jj, st) in pend:
                bb = jj % 2
                (THr2, k02), (THi2, k12), (A2, k22), (Ht2, k32) = slots(bb)
                hk2 = kb(2, jj * T, jj * T + T)
                tt("pool", THi2, THr2, THi2, ALU.mult, k02 + k12, k12)
                if prev is None:
                    init, rk = zero1[:, 0:1], ["zero1"]
                elif prev[1]:
                    col = prev[0] * T + (T - 1 if d == 0 else 0)
                    init, rk = HB[:, col:col + 1], kb(2, col, col + 1)
                else:
                    pslot, pk = slots(prev[0] % 2)[3]
                    init, rk = (pslot[:, T - 1:T] if d == 0 else pslot[:, 0:1]), pk
                if st:
                    scan(HB[:, tsl(jj)], A2, THi2, init, d == 1, k22 + k12 + rk, hk2)
                else:
                    zb = 4 + bb
                    for k in range(8):
                        mm(pbx[zb], wA[:, k, 128:256], hT[:, k, tsl(jj)], k == 0, k == 7,
                           [("wA", s, 1)] + khT(k, jj * T, jj * T + T), [("pb", zb)])
                    scan(Ht2, A2, THi2, init, d == 1, k22 + k12 + rk, k32)
                    act(THr2, pbx[zb], AF.Tanh, [("pb", zb)], k02, scale=0.5)
                    stt("dve", THr2, THr2, 1.0, pbx[zb], ALU.add, ALU.mult, k02 + [("pb", zb)], k02)
                    tt("dve", A2, Ht2, HB[:, tsl(jj)], ALU.add, k32 + hk2, k22)
                    ob, okk = tmph[d][bb], [("th%d" % d, bb)]
                    tt("pool", ob[:], A2, THr2, ALU.mult, k22 + k02, okk)
                    dma("sp", mixed0[cs, tsl(jj)], ob[:], okk, [("mixed0", c, jj)], "ya%d_%d" % (d, bb))
                prev = (jj, st)
            del pend[:]
            yield

    def run_rr(gens):
        live = list(gens)
        while live:
            for g in list(live):
                try:
                    next(g)
                except StopIteration:
                    live.remove(g)

    run_rr([rgA_front(0)])
    rgA_conv(0)
    for c in range(8):
        gens = [rg_dir(1, c), rg_dir(0, c)]
        if c + 1 < 8:
            gens.append(rgA_front(c + 1))
        run_rr(gens)
        if c + 1 < 8:
            rgA_conv(c + 1)

    Pb, GB = big[0], big[1]
    SZB, YB = bigh[0], bigh[1]
    P.add("pool", lambda e: e.memset(Pb[:, 0:1], 0.0), w=kb(0, 0, 1))
    P.add("pool", lambda e: e.memset(Pb[:, SEQ + 1:SEQ + 2], 0.0), w=kb(0, SEQ + 1, SEQ + 2))
    for c in range(8):
        s = c % 2
        wB = wb[1][:, :, s * 512:(s + 1) * 512]
        for q in range(4):
            dma("pool", wB[:, :, q * 128:(q + 1) * 128], w0[:, :, (2 + q) * 1024 + c * 128:(2 + q) * 1024 + (c + 1) * 128],
                [], [("wB", s, q)] + [("wbf", 1, n) for n in range(8)], "wB%d_%d" % (s, q))
        for j in range(NT):
            b = j % 2
            for q in range(4):
                for k in range(8):
                    mm(pb[q][:], wB[:, k, q * 128:(q + 1) * 128], hT[:, k, tsl(j)], k == 0, k == 7,
                       [("wB", s, q)] + khT(k, j * T, j * T + T), [("pb", q)])
            xbs, xbk = bslot(2, b)
            act(xbs, pb[0][:], AF.Copy, [("pb", 0)], xbk)
            tt("dve", Pb[:, 1 + j * T:1 + (j + 1) * T], pb[2][:], xbs, ALU.mult, [("pb", 2)] + xbk, kb(0, 1 + j * T, 1 + (j + 1) * T))
            act(GB[:, tsl(j)], pb[1][:], AF.Copy, [("pb", 1)], kb(1, j * T, j * T + T))
            szk = kh(0, j * T, j * T + T)
            act(SZB[:, tsl(j)], pb[3][:], AF.Tanh, [("pb", 3)], szk, scale=0.5)
            stt("dve", SZB[:, tsl(j)], SZB[:, tsl(j)], 1.0, pb[3][:], ALU.add, ALU.mult, szk + [("pb", 3)], szk)
        allP = kb(0, 0, SEQ + 4)
        allCV = kb(3, 0, SEQ)
        allGB = kb(1, 0, SEQ)
        allSZB = kh(0, 0, SEQ)
        allYB = kh(1, 0, SEQ)
        ts("dve", CVb[:, 0:SEQ], Pb[:, 0:SEQ], pv[:, 13, c:c + 1], None, ALU.mult, None, allP + ["pv"], allCV)
        for kk in range(1, 3):
            stt("dve", CVb[:, 0:SEQ], Pb[:, kk:kk + SEQ], pv[:, 13 + kk, c:c + 1], CVb[:, 0:SEQ], ALU.mult, ALU.add, allP + allCV + ["pv"], allCV)
        stt("dve", CVb[:, 0:SEQ], CVb[:, 0:SEQ], 0.5, GB[:, 0:SEQ], ALU.mult, ALU.mult, allCV + allGB, allCV)
        tt("pool", YB[:, 0:SEQ], CVb[:, 0:SEQ], SZB[:, 0:SEQ], ALU.mult, allCV + allSZB, allYB)
        dma("sp", mixed0[1024 + c * 128:1024 + (c + 1) * 128, :], YB[:, 0:SEQ], allYB, [("mixed0", 8 + c, j) for j in range(NT)], "yb")

    P.barrier()
    m0v = mixed0.rearrange("(k p) t -> p k t", p=128)
    mtb = [bigh[i][:, 0:8 * T].rearrange("p (k t) -> p k t", t=T) for i in range(2)]

    def mt_get0(j):
        b = j % 2
        lo = mtb[b]
        hi = hT[:, :, b * T:(b + 1) * T]
        rk = [("mixed0", c, j) for c in range(16)]
        dma("sp", lo, m0v[:, 0:8, tsl(j)], rk, [("mtlo", b)], "mt%d" % b)
        dma("sp", hi, m0v[:, 8:16, tsl(j)], rk, [("mthi", b)], "mth%d" % b)
        return (lambda k: lo[:, k, :] if k < 8 else hi[:, k - 8, :]), [("mtlo", b), ("mthi", b)]

    out_phase(mt_get0, 16, w_out0, 1, xT, x1T if layers > 1 else outT, "x0", hT[:, :, 2 * T:3 * T], ["sqh"])
    if layers == 1:
        P.emit(nc, es)
        es.close()
        return nc, P

    P.barrier()
    norm_phase(x1T, 16, "x0o")
    w1 = w_in1.rearrange("(k p) n -> p k n", p=128)
    LRs = nc.dram_tensor("LRs", [2, 16, SEQ], BF16, kind="Internal").ap()
    dma("pool", lrw[:], w1[:, :, 3072:3104], [], ["lrw"], "c8")
    dma("pool", wgt[:], wg.rearrange("d r k -> r d k"), [], ["wgt"], "c9")

    for q in range(2):
        dma("pool", wb[0][:, :, q * 512:(q + 1) * 512], w1[:, :, q * 512:(q + 1) * 512], [], [("wb", 0, q)], "wb0_%d" % q)
    for oc in range(8):
        dst = qTs if oc < 4 else kTs
        for j in range(NT):
            b = j % 2
            bank = (oc * NT + j) % 4
            for k in range(8):
                mm(pb[bank][:], wb[0][:, k, oc * 128:(oc + 1) * 128], hT[:, k, tsl(j)], k == 0, k == 7,
                   [("wb", 0, oc // 4)] + khT(k, j * T, j * T + T), [("pb", bank)])
            st, stk = bslot(bank, b)
            act(st, pb[bank][:], AF.Copy, [("pb", bank)], stk)
            dma("sp", dst[(oc % 4) * 128:(oc % 4 + 1) * 128, tsl(j)], st, stk, [("qk", oc, j)], "st%d_%d" % (bank, b))
    for q in range(2):
        dma("pool", wb[1][:, :, q * 512:(q + 1) * 512], w1[:, :, 1024 + q * 512:1024 + (q + 1) * 512], [], [("wb", 1, q)], "wb1_%d" % q)
    Vsv = Vs.rearrange("(m p) e -> p m e", p=128)
    Vt = [bigh[i][:, 0:4096].rearrange("p (m e) -> p m e", e=1024) for i in range(2)]
    for j in range(NT):
        b = j % 2
        for m in range(4):
            for half in range(2):
                bank = (m * 2 + half) % 4
                for k in range(8):
                    mm(pb[bank][:], hT[:, k, j * T + m * 128:j * T + (m + 1) * 128], wb[1][:, k, half * 512:(half + 1) * 512],
                       k == 0, k == 7, [("wb", 1, half)] + khT(k, j * T, j * T + T), [("pb", bank)])
                act(Vt[b][:, m, half * 512:(half + 1) * 512], pb[bank][:], AF.Copy, [("pb", bank)], [("Vt", b, m, half)])
        dma("sp", Vsv[:, j * 4:(j + 1) * 4, :], Vt[b], [("Vt", b, m, hf) for m in range(4) for hf in range(2)],
            [("Vs", j)], "vt%d" % b)
    for q in range(2):
        dma("pool", wb[0][:, :, q * 512:(q + 1) * 512], w1[:, :, 2048 + q * 512:2048 + (q + 1) * 512], [], [("wb", 0, q)], "wb0_%d" % q)
    for oc in range(8):
        for j in range(NT):
            b = j % 2
            bank = (oc * NT + j) % 4
            for k in range(8):
                mm(pb[bank][:], wb[0][:, k, oc * 128:(oc + 1) * 128], hT[:, k, tsl(j)], k == 0, k == 7,
                   [("wb", 0, oc // 4)] + khT(k, j * T, j * T + T), [("pb", bank)])
            stg = tmph[bank % 2][b]
            sk = [("th%d" % (bank % 2), b)]
            act(stg[:], pb[bank][:], AF.Tanh, [("pb", bank)], sk, scale=0.5)
            stt("dve", stg[:], stg[:], 1.0, pb[bank][:], ALU.add, ALU.mult, sk + [("pb", bank)], sk)
            dma("sp", SRs[oc * 128:(oc + 1) * 128, tsl(j)], stg[:], sk, [("SRs", oc, j)], "sr%d_%d" % (bank % 2, b))
    for d in range(2):
        for j in range(NT):
            b = j % 2
            bank = 4 + b
            for k in range(8):
                mm(pb[bank][0:16, :], lrw[:, k, d * 16:(d + 1) * 16], hT[:, k, tsl(j)], k == 0, k == 7,
                   ["lrw"] + khT(k, j * T, j * T + T), [("pb", bank)])
            act(tmph[2][b][0:16, :], pb[bank][0:16, :], AF.Copy, [("pb", bank)], [("th2", b)])
            dma("sp", LRs[d, :, tsl(j)], tmph[2][b][0:16, :], [("th2", b)], [("LRs", d, j)], "lrs%d" % b)

    P.barrier()
    SC = 128.0 ** -0.5
    qh, kh_, O = big[0], big[1], [big[2], big[3]]
    HC = NCH // 2
    Vhh = [bigh[i][:, 0:HC * 256].rearrange("p (m e) -> p m e", e=256) for i in range(2)]

    def Vn(n):
        return Vhh[n // HC][:, n % HC, :]

    sq1 = bigh[1][:, 0:8 * T].rearrange("p (k t) -> p k t", t=T)
    for h in range(4):
        dma("sp", qh[:, 0:SEQ], qTs[h * 128:(h + 1) * 128, :], [("qk", h, j) for j in range(NT)], ["qh"], "qh")
        dma("sp", kh_[:, 0:SEQ], kTs[h * 128:(h + 1) * 128, :], [("qk", 4 + h, j) for j in range(NT)], ["kh"], "kh")
        for i in range(2):
            dma("sp", Vhh[i], Vsv[:, i * HC:(i + 1) * HC, h * 256:(h + 1) * 256], [("Vs", j) for j in range(NT)], ["Vh"], "vh%d" % i)
        def hslot(k, g):
            return hT[:, k, g * T:(g + 1) * T], [("hT", k, g)]

        def gbuf(d, p):
            base = (d * 2 + p)
            QI, kq = hslot(0, base)
            KI, kk = hslot(1, base)
            KT, kt = hslot(2, base)
            SM = [hslot(3 + jc, base) for jc in range(4)]
            KV = (wbf[1][:, (2 + base) * T:(2 + base) * T + 256], [("wbf", 1, 2 + base)])
            return QI, kq, KI, kk, KT, kt, SM, KV

        def gla_prep(d, sidx, h=h):
            j = sidx if d == 0 else NT - 1 - sidx
            p = sidx % 2
            (Lt, k0), (Lc, k1), (E1, k2), (E2, k3) = [wslot(0, i * 2 + d) for i in range(4)]
            QI, kq, KI, kk, KT, kt, SM, (KV, kvk) = gbuf(d, p)
            KST, kst = hslot(7, d)
            bz, bk = 0 + d, 6 + d
            pz = pbx[bz]
            dma("sp", lrt[d][:], LRs[d, :, tsl(j)], [("LRs", d, j)], [("lrt", d)], "lrt%d" % d)
            mm(pz, wgt[0:16, d, h * 128:(h + 1) * 128], lrt[d][:], True, True, ["wgt", ("lrt", d)], [("pb", bz)])
            yield
            act(Lt, pz, AF.Exp, [("pb", bz), "negb"], k0, bias=negb[:, d, h:h + 1], scale=-1.0)
            act(Lt, Lt, AF.Ln, k0, k0, bias=1.0)
            yield
            scan(Lc, m01[:, :], Lt, 0.0, d == 1, k0 + ["m01"], k1)
            yield
            tot = Lc[:, T - 1:T] if d == 0 else Lc[:, 0:1]
            act(E1, Lc, AF.Exp, k1, k2, scale=-1.0 / 16)
            act(E2, Lc, AF.Exp, k1, k3, scale=1.0 / 16)
            act(DEC[d][:, j, :], tot, AF.Exp, k1, [("DEC", d, j)], scale=-1.0 / 16)
            yield
            stt("dve", QI, qh[:, tsl(j)], SC, E1, ALU.mult, ALU.mult, ["qh"] + k2, kq)
            tt("pool", KI, kh_[:, tsl(j)], E2, ALU.mult, ["kh"] + k3, kk)
            yield
            ts("dve", E1, Lc, tot, None, ALU.subtract, None, k1 + k2, k2)
            yield
            act(E1, E1, AF.Exp, k2, k2, scale=1.0 / 16)
            yield
            tt("pool", KST, kh_[:, tsl(j)], E1, ALU.mult, ["kh"] + k2, kst)
            yield
            kt_ps = (pb6h if d == 0 else pbt)[:, 0:T]
            kv_ps = pbx[bk][:, 256:512]
            jcs = list(range(4)) if d == 0 else list(range(3, -1, -1))
            for jc in range(4):
                P.add("pe", lambda e, o=kt_ps[:, jc * 128:(jc + 1) * 128], i=KST[:, jc * 128:(jc + 1) * 128]:
                      e.transpose(out=o, in_=i, identity=ident[:]), r=kst + ["ident"], w=[("pb", bk)])
            yield
            act(KT, kt_ps, AF.Copy, [("pb", bk)], kt)
            yield
            for jc in jcs:
                if d == 0:
                    isl, msl = slice(jc * 128, T), tri[:, 0, 0:(4 - jc) * 128]
                else:
                    isl, msl = slice(0, (jc + 1) * 128), tri[:, 1, (3 - jc) * 128:T]
                ncol = isl.stop - isl.start
                mm(pz[:, 0:ncol], KI[:, jc * 128:(jc + 1) * 128], QI[:, isl], True, True, kk + kq, [("pb", bz)])
                yield
                tt("dve", SM[jc][0][:, 0:ncol], pz[:, 0:ncol], msl, ALU.mult, [("pb", bz), "tri"], SM[jc][1])
                yield
            for jc in range(4):
                mm(kv_ps, KT[:, jc * 128:(jc + 1) * 128], Vn(j * 4 + jc), jc == 0, jc == 3, kt + ["Vh"], [("pb", bk)])
            yield
            act(KV, kv_ps, AF.Copy, [("pb", bk)], kvk)
            yield

        def gla_main(d, sidx, h=h):
            j = sidx if d == 0 else NT - 1 - sidx
            p = sidx % 2
            first = sidx == 0
            QI, kq, KI, kk, KT, kt, SM, (KV, kvk) = gbuf(d, p)
            bo0, bo1 = 2 + d, 4 + d
            jcs = list(range(4)) if d == 0 else list(range(3, -1, -1))
            for ec, bo in ((0, bo0), (1, bo1)):
                for idx, jc in enumerate(jcs):
                    isl = slice(jc * 128, T) if d == 0 else slice(0, (jc + 1) * 128)
                    ncol = isl.stop - isl.start
                    mm(pbx[bo][:, isl], Vn(j * 4 + jc)[:, ec * 128:(ec + 1) * 128], SM[jc][0][:, 0:ncol], idx == 0, first and idx == 3,
                       ["Vh"] + SM[jc][1], [("pb", bo)])
                if not first:
                    mm(pbx[bo], Sbf[d][:, ec * 128:(ec + 1) * 128], QI, False, True, [("Sb", d)] + kq, [("pb", bo)])
                yield
            if first:
                P.add("dve", lambda e, o=Sst[d][:], i=KV: e.tensor_copy(out=o, in_=i), r=kvk, w=[("S", d)])
            else:
                stt("dve", Sst[d][:], Sst[d][:], DEC[d][:, j, :], KV, ALU.mult, ALU.add, [("S", d), ("DEC", d, j)] + kvk, [("S", d)])
            yield
            act(Sbf[d][:], Sst[d][:], AF.Copy, [("S", d)], [("Sb", d)])
            firstO = (d == 0) == (2 * j <= NT - 1)
            okj = [[("O", ec, n) for n in range(j * 4, j * 4 + 4)] for ec in range(2)]
            for ec, bo in ((0, bo0), (1, bo1)):
                if firstO:
                    act(O[ec][:, tsl(j)], pbx[bo], AF.Copy, [("pb", bo)], okj[ec])
                else:
                    tt("dve", O[ec][:, tsl(j)], O[ec][:, tsl(j)], pbx[bo], ALU.add, [("pb", bo)] + okj[ec], okj[ec])
            yield

        def gla_dir(d):
            yield from gla_prep(d, 0)
            for sidx in range(NT):
                live2 = [gla_main(d, sidx)] + ([gla_prep(d, sidx + 1)] if sidx + 1 < NT else [])
                while live2:
                    for g in list(live2):
                        try:
                            next(g)
                            yield
                        except StopIteration:
                            live2.remove(g)

        run_rr([gla_dir(0), gla_dir(1)])

        for j in range(NT):
            b = j % 2
            okeys = [[("O", ec, n) for n in range(j * 4, j * 4 + 4)] for ec in range(2)]
            for ec in range(2):
                dma("sp", tmph[ec][b][:], SRs[(2 * h + ec) * 128:(2 * h + ec + 1) * 128, tsl(j)],
                    [("SRs", 2 * h + ec, j)], [("th%d" % ec, b)], "srl%d_%d" % (ec, b))
                act(sqs[:, ec, :], O[ec][:, tsl(j)], AF.Square, okeys[ec], [("sq1", ec)])
            for ec in range(2):
                mm(pb[6][:], ones[:], sqs[:, ec, :], ec == 0, ec == 1, ["ones", ("sq1", ec)], [("pb", 6)])
            rstd_from(pb[6][:], b, 256, mul=0.5)
            for ec in range(2):
                t4, t4k = wslot(1, ec)
                stt("dve", t4, O[ec][:, tsl(j)], pngt[:, ec:ec + 1], rs[b][:], ALU.mult, ALU.mult,
                    okeys[ec] + [("rs", b), "pngt"], t4k)
                mo, mok = tmph[2][ec], [("th2", ec)]
                tt("pool", mo[:], t4, tmph[ec][b][:], ALU.mult, t4k + [("th%d" % ec, b)], mok)
                dma("sp", mixed0[(2 * h + ec) * 128:(2 * h + ec + 1) * 128, tsl(j)], mo[:], mok, [("m1", 2 * h + ec, j)], "m1_%d" % ec)

    P.barrier()

    def mt_get1(j):
        b = j % 2
        lo = mtb[b]
        dma("sp", lo, m0v[:, 0:8, tsl(j)], [("m1", c, j) for c in range(8)], [("mtlo", b)], "mt%d" % b)
        return (lambda k: lo[:, k, :]), [("mtlo", b)]

    out_phase(mt_get1, 8, w_out1, 17, x1T, outT, "x0o", hT[:, :, 2 * T:3 * T], ["sqh"])
    P.emit(nc, es)
    es.close()
    return nc, P


def host_inputs(x_b, inp):
    f = lambda a: np.ascontiguousarray(np.asarray(a, dtype=np.float32))
    vecs = [inp["even_norm_pre"][0], inp["even_norm_post"][0]] + [inp["rg_conv_w"][0, i] for i in range(4)] + \
           [inp["rg_conv_b"][0]] + [inp["rg_gate_b"][0, d, g].reshape(-1) for d in range(2) for g in range(2)] + \
           [inp["rg_lambda"][0, d] for d in range(2)] + [inp["sc_conv_w"][0, i] for i in range(3)] + \
           [inp["odd_norm_pre"][0], inp["odd_norm_post"][0]]
    pvec = f(np.stack([np.asarray(v) for v in vecs], 0).reshape(NV, 8, 128).transpose(2, 0, 1))
    pg = f(np.asarray(inp["gla_b_gate"][0]).reshape(2, 4, 128).transpose(2, 0, 1))
    png = f(np.asarray(inp["gla_norm_g"][0]).reshape(2, 128).transpose(1, 0))
    cm01 = np.ones((128, T), np.float32)
    jj, ii = np.meshgrid(np.arange(128), np.arange(128), indexing="ij")
    mf = np.ones((128, T), np.float32)
    mf[:, 0:128] = (jj <= ii)
    mb = np.ones((128, T), np.float32)
    mb[:, T - 128:T] = (jj >= ii)
    ctri = f(np.stack([mf, mb], 1))
    return {
        "xT": f(np.asarray(x_b).T), "w_in0": f(inp["even_w_in"][0]), "gate_w": f(inp["rg_gate_w"][0]),
        "w_out0": f(inp["even_w_out"][0]), "w_in1": f(inp["odd_w_in"][0]), "wg": f(inp["gla_w_gate_lr"][0]),
        "w_out1": f(inp["odd_w_out"][0]), "pvec": pvec, "pg": pg, "png": png, "cm01": cm01, "ctri": ctri,
        "cones": np.ones((128, 128), np.float32), "cident": np.eye(128, dtype=np.float32),
    }


def kernel(**inputs):
    x = np.asarray(inputs["x"])
    B, SEQ, _ = x.shape
    nc, _ = build_nc(SEQ)
    shared = None
    in_maps = []
    for b in range(B):
        m = host_inputs(x[b], inputs)
        if shared is None:
            shared = m
        else:
            for k in m:
                if k != "xT":
                    m[k] = shared[k]
        in_maps.append(m)
    res = run_bass_kernel_spmd(nc, in_maps, core_ids=list(range(B)))
    out = np.stack([np.asarray(r["outT"]).T for r in res.results], 0)
    return np.ascontiguousarray(out.astype(np.float32))
```

```python
import numpy as np
from contextlib import ExitStack
import concourse.bass as bass
import concourse.mybir as mybir
from concourse.bass_utils import run_bass_kernel_spmd
from concourse.ap import AP

F32 = mybir.dt.float32
BF16 = mybir.dt.bfloat16
AF = mybir.ActivationFunctionType
ALU = mybir.AluOpType

D = 1024
T = 512
EPS = 1e-6
NV = 18
LAYERS = 2
SBUF_LEFT = []


def rev(ap):
    pairs = [list(p) for p in ap.ap]
    step, cnt = pairs[-1]
    pairs[-1] = [-step, cnt]
    return AP(ap.tensor, ap.offset + step * (cnt - 1), pairs)


class _Op:
    __slots__ = ("eng", "fn", "deps", "dma_key", "signal", "val", "idx", "semkey")


class Prog:
    def __init__(self):
        self.ops = []
        self.lastw = {}
        self.rd_eng = {}
        self.rd_dma = {}
        self.last_eng = {}
        self.last_dma = {}
        self.fence = []
        self.passed = set()

    def barrier(self):
        self.fence = list(self.last_eng.values()) + list(self.last_dma.values())
        self.passed = set()

    def add(self, eng, fn, r=(), w=(), dma_key=None):
        op = _Op()
        op.eng, op.fn, op.dma_key = eng, fn, dma_key
        op.signal = dma_key is not None
        op.val = 0
        op.semkey = None
        op.idx = len(self.ops)
        deps = {}
        if self.fence and eng not in self.passed:
            for p in self.fence:
                deps[p.idx] = p
            self.passed.add(eng)
        for k in r:
            p = self.lastw.get(k)
            if p is not None:
                deps[p.idx] = p
        for k in w:
            p = self.lastw.get(k)
            if p is not None:
                deps[p.idx] = p
            for q in self.rd_eng.get(k, {}).values():
                deps[q.idx] = q
            for q in self.rd_dma.get(k, ()):
                deps[q.idx] = q
        op.deps = [p for p in deps.values() if (p.dma_key is not None) or (p.eng != eng) or (eng != "pe")]
        for k in r:
            if dma_key is not None:
                self.rd_dma.setdefault(k, []).append(op)
            else:
                self.rd_eng.setdefault(k, {})[eng] = op
        for k in w:
            self.lastw[k] = op
            self.rd_eng[k] = {}
            self.rd_dma[k] = []
        if dma_key is not None:
            self.last_dma[dma_key] = op
        else:
            self.last_eng[eng] = op
        self.ops.append(op)
        return op

    def emit(self, nc, es):
        for op in self.ops:
            for p in op.deps:
                p.signal = True
        cnt = {}
        for op in self.ops:
            if op.signal:
                key = ("dma", op.dma_key) if op.dma_key is not None else ("eng", op.eng)
                cnt[key] = cnt.get(key, 0) + (16 if op.dma_key is not None else 1)
                op.semkey, op.val = key, cnt[key]
        sems = {}
        for i, key in enumerate(cnt):
            sems[key] = es.enter_context(nc.semaphore("s%d" % i))
        self.nsem = len(sems)
        block = es.enter_context(nc.Block())
        ops = self.ops

        def run(engname, final=False):
            def body(e):
                waited = {}
                for op in ops:
                    if op.eng != engname:
                        continue
                    for p in op.deps:
                        if waited.get(p.semkey, 0) < p.val:
                            e.wait_ge(sems[p.semkey], p.val)
                            waited[p.semkey] = p.val
                    ins = op.fn(e)
                    if op.signal:
                        ins.then_inc(sems[op.semkey], 16 if op.dma_key is not None else 1)
                if final:
                    for key, v in cnt.items():
                        if waited.get(key, 0) < v:
                            e.wait_ge(sems[key], v)
            return body

        block.sync(run("sp", final=True))
        block.tensor(run("pe"))
        block.scalar(run("act"))
        block.vector(run("dve"))
        block.gpsimd(run("pool"))


def build_nc(SEQ, layers=LAYERS):
    NT = SEQ // T
    NCH = SEQ // 128
    BW = max(SEQ, 8 * T)
    nc = bass.Bass("TRN2", target_bir_lowering=False)
    din = lambda n, s, dt=F32: nc.dram_tensor(n, s, dt, kind="ExternalInput").ap()
    xT = din("xT", [D, SEQ])
    w_in0 = din("w_in0", [D, 6144])
    gate_w = din("gate_w", [2, 2, 8, 128, 128])
    w_out0 = din("w_out0", [2048, D])
    w_in1 = din("w_in1", [D, 3104])
    wg = din("wg", [2, 16, 512])
    w_out1 = din("w_out1", [D, D])
    pvec = din("pvec", [128, NV, 8])
    pg = din("pg", [128, 2, 4])
    png = din("png", [128, 2])
    cm01 = din("cm01", [128, T])
    ctri = din("ctri", [128, 2, T])
    cones = din("cones", [128, 128])
    cident = din("cident", [128, 128])
    outT = nc.dram_tensor("outT", [D, SEQ], F32, kind="ExternalOutput").ap()
    mixed0 = nc.dram_tensor("mixed0", [2048, SEQ], BF16, kind="Internal").ap()
    x1T = nc.dram_tensor("x1T", [D, SEQ], F32, kind="Internal").ap()
    qTs = nc.dram_tensor("qTs", [512, SEQ], F32, kind="Internal").ap()
    kTs = nc.dram_tensor("kTs", [512, SEQ], F32, kind="Internal").ap()
    Vs = nc.dram_tensor("Vs", [SEQ, D], BF16, kind="Internal").ap()
    SRs = nc.dram_tensor("SRs", [D, SEQ], BF16, kind="Internal").ap()

    es = ExitStack()
    P = Prog()
    sb = lambda n, s, dt=F32: es.enter_context(nc.sbuf_tensor(n, s, dt))
    ps = lambda n, s, dt=F32: es.enter_context(nc.psum_tensor(n, s, dt))

    hT = sb("hT", [128, 8, SEQ], BF16)
    pv = sb("pv", [128, NV, 8])
    pgt = sb("pgt", [128, 2, 4])
    pngt = sb("pngt", [128, 2])
    m01 = sb("m01", [128, T], BF16)
    tri = sb("tri", [128, 2, T])
    ones = sb("ones", [128, 128], BF16)
    ident = sb("ident", [128, 128], BF16)
    der = sb("der", [128, 12, 8])
    negb = sb("negb", [128, 2, 4])
    gwh = [sb("gwh%d" % i, [128, 4, 128], BF16) for i in range(2)]
    lrw = sb("lrw", [128, 8, 32], BF16)
    wgt = sb("wgt", [16, 2, 512], BF16)
    lrt = [sb("lrt%d" % d, [16, T], BF16) for d in range(2)]
    rs = [sb("rs%d" % i, [128, T]) for i in range(2)]
    wb = [sb("wb%d" % i, [128, 8, 1024], BF16) for i in range(2)]
    wbf = [wb[i][:].rearrange("p k n -> p (k n)").bitcast(F32) for i in range(2)]
    big = [sb("big%d" % i, [128, BW + 4]) for i in range(4)]
    bigh = [sb("bigh%d" % i, [128, BW], BF16) for i in range(2)]
    tmph = [[sb("tmph%d_%d" % (i, j), [128, T], BF16) for j in range(2)] for i in range(3)]
    zero1 = sb("zero1", [128, 1])
    Sst = [sb("Sst%d" % d, [128, 256]) for d in range(2)]
    Sbf = [sb("Sbf%d" % d, [128, 256], BF16) for d in range(2)]
    DEC = [sb("DEC%d" % d, [128, NT, 1]) for d in range(2)]
    KTt = [sb("KTt%d" % d, [128, 4, 128], BF16) for d in range(2)]
    xt = [big[2 + i][:, 0:8 * T].rearrange("p (k t) -> p k t", t=T) for i in range(2)]
    sqs = sb("sqs", [128, 2, T], BF16)

    SBUF_LEFT.append(nc.sbuf_bytes_remaining)
    pb = [ps("pb%d" % i, [128, T]) for i in range(7)]
    pbt = ps("pbt", [128, 1024], BF16)
    pbx = [pb[i][:] for i in range(7)] + [pbt[:].bitcast(F32)]
    pb6h = pb[6][:].bitcast(BF16)

    def kb(i, lo, hi):
        return [("big", i, g) for g in range(lo // T, (hi - 1) // T + 1)]

    def kh(i, lo, hi):
        return [("bigh", i, g) for g in range(lo // T, (hi - 1) // T + 1)]

    def khT(k, lo, hi):
        return [("hT", k, g) for g in range(lo // T, (hi - 1) // T + 1)]

    def bslot(i, n):
        return big[i][:, n * T:(n + 1) * T], [("big", i, n)]

    def wslot(i, n):
        return wbf[i][:, n * T:(n + 1) * T], [("wbf", i, n)]

    def bcast_last(ap, n):
        pairs = [list(p) for p in ap.ap]
        assert pairs[-1][1] == 1
        pairs[-1] = [0, n]
        return AP(ap.tensor, ap.offset, pairs)

    def dma(eng, out, in_, r, w, key):
        P.add(eng, lambda e: e.dma_start(out=out, in_=in_), r=r, w=w, dma_key=key)

    def act(out, in_, func, r, w, bias=0.0, scale=1.0):
        P.add("act", lambda e: e.activation(out=out, in_=in_, func=func, bias=bias, scale=scale), r=r, w=w)

    def tt(eng, out, in0, in1, op, r, w):
        P.add(eng, lambda e: e.tensor_tensor(out=out, in0=in0, in1=in1, op=op), r=r, w=w)

    def ts(eng, out, in0, s1, s2, op0, op1, r, w):
        if op1 is None:
            P.add(eng, lambda e: e.tensor_scalar(out=out, in0=in0, scalar1=s1, scalar2=None, op0=op0), r=r, w=w)
        else:
            P.add(eng, lambda e: e.tensor_scalar(out=out, in0=in0, scalar1=s1, scalar2=s2, op0=op0, op1=op1), r=r, w=w)

    def stt(eng, out, in0, s, in1, op0, op1, r, w):
        P.add(eng, lambda e: e.scalar_tensor_tensor(out=out, in0=in0, scalar=s, in1=in1, op0=op0, op1=op1), r=r, w=w)

    def mm(out, lhsT, rhs, start, stop, r, w):
        P.add("pe", lambda e: e.matmul(out, lhsT=lhsT, rhs=rhs, start=start, stop=stop), r=r, w=w)

    def scan(out, d0, d1, init, reverse, r, w):
        if reverse:
            P.add("dve", lambda e: e.tensor_tensor_scan(out=rev(out), data0=rev(d0), data1=rev(d1), initial=init,
                                                        op0=ALU.mult, op1=ALU.add), r=r, w=w)
        else:
            P.add("dve", lambda e: e.tensor_tensor_scan(out=out, data0=d0, data1=d1, initial=init,
                                                        op0=ALU.mult, op1=ALU.add), r=r, w=w)

    def tsl(j):
        return slice(j * T, (j + 1) * T)

    xtk = [kb(2 + i, 0, 8 * T) for i in range(2)]

    dma("sp", pv[:], pvec[:, :, :], [], ["pv"], "c0")
    dma("sp", pgt[:], pg[:, :, :], [], ["pgt"], "c1")
    dma("sp", pngt[:], png[:, :], [], ["pngt"], "c2")
    dma("pool", m01[:], cm01[:, :], [], ["m01"], "c3")
    dma("sp", tri[:], ctri[:, :, :], [], ["tri"], "c4")
    dma("pool", ones[:], cones[:, :], [], ["ones"], "c5")
    dma("pool", ident[:], cident[:, :], [], ["ident"], "c6")
    P.add("dve", lambda e: e.memset(zero1[:], 0.0), w=["zero1"])
    ts("dve", der[:, 0:4, :], pv[:, 7:11, :], 0.5, None, ALU.mult, None, ["pv"], ["der"])
    act(der[:, 8:10, :], pv[:, 11:13, :], AF.Exp, ["pv"], ["der8"], scale=-1.0)
    act(der[:, 10:12, :], der[:, 8:10, :], AF.Ln, ["der8"], ["der10"], bias=1.0)
    ts("dve", der[:, 4:6, :], der[:, 10:12, :], -4.0, None, ALU.mult, None, ["der10"], ["der"])
    ts("dve", der[:, 6:8, :], der[:, 10:12, :], -8.0, None, ALU.mult, None, ["der10"], ["der"])
    ts("dve", negb[:], pgt[:], -1.0, None, ALU.mult, None, ["pgt"], ["negb"])

    def rstd_from(psum_ap, b, n, mul=1.0):
        act(rs[b][:], psum_ap, AF.Ln, [("pb", 6)], [("rs", b)], bias=EPS, scale=1.0 / n)
        act(rs[b][:], rs[b][:], AF.Exp, [("rs", b)], [("rs", b)], scale=-0.5, bias=float(np.log(mul)))

    def norm_phase(src, gidx, tag):
        srcv = src.rearrange("(k p) t -> p k t", p=128)
        sq = bigh[1][:, 0:8 * T].rearrange("p (k t) -> p k t", t=T)
        sqk = kh(1, 0, 8 * T)
        for j in range(NT):
            b = j % 2
            dma("sp", xt[b], srcv[:, :, tsl(j)], [("src", tag, j)], xtk[b], "xt%d" % b)
            act(sq, xt[b], AF.Square, xtk[b], sqk)
            for k in range(8):
                mm(pb[6][:], ones[:], sq[:, k, :], k == 0, k == 7, ["ones"] + sqk, [("pb", 6)])
            rstd_from(pb[6][:], b, D)
            for k in range(8):
                stt("dve", hT[:, k, tsl(j)], xt[b][:, k, :], pv[:, gidx, k:k + 1], rs[b][:], ALU.mult, ALU.mult,
                    xtk[b] + [("rs", b), "pv"], khT(k, j * T, (j + 1) * T))

    def out_phase(mt_get, nk, wsrc, gidx, resid, dst, tag, sq, sqk):
        wo = wsrc.rearrange("(k p) n -> p k n", p=128)
        for k0 in range(0, nk, 4):
            s, q = k0 // 8, (k0 % 8) // 4
            dma("pool", wb[s][:, q * 4:q * 4 + 4, :], wo[:, k0:k0 + 4, :], [], [("wb", s, q)], "wb%d_%d" % (s, q))
        resv = resid.rearrange("(k p) t -> p k t", p=128)
        dstv = dst.rearrange("(k p) t -> p k t", p=128)
        ysb = [big[i][:, 0:8 * T].rearrange("p (k t) -> p k t", t=T) for i in range(2)]
        for j in range(NT):
            b = j % 2
            ys = ysb[b]
            mget, mkeys = mt_get(j)
            dma("sp", xt[b], resv[:, :, tsl(j)], [("src", tag, j)], xtk[b], "xt%d" % b)
            for o in range(8):
                bank = o % 4
                for k in range(nk):
                    mm(pb[bank][:], wb[k // 8][:, k % 8, o * 128:(o + 1) * 128], mget(k), k == 0, k == nk - 1,
                       [("wb", k // 8, (k % 8) // 4)] + mkeys, [("pb", bank)])
                act(ys[:, o, :], pb[bank][:], AF.Copy, [("pb", bank)], [("big", b, o)])
                act(sq[:, o, :], pb[bank][:], AF.Square, [("pb", bank)], sqk)
            for o in range(8):
                mm(pb[6][:], ones[:], sq[:, o, :], o == 0, o == 7, ["ones"] + sqk, [("pb", 6)])
            rstd_from(pb[6][:], b, D)
            for o in range(8):
                stt("dve", ys[:, o, :], ys[:, o, :], pv[:, gidx, o:o + 1], rs[b][:], ALU.mult, ALU.mult,
                    [("big", b, o), ("rs", b), "pv"], [("big", b, o)])
                tt("pool", ys[:, o, :], ys[:, o, :], xt[b][:, o, :], ALU.add, [("big", b, o)] + xtk[b], [("big", b, o)])
            dma("sp", dstv[:, :, tsl(j)], ys, [("big", b, o) for o in range(8)], [("src", tag + "o", j)], "xo%d" % b)

    norm_phase(xT, 0, "x0")
    w0 = w_in0.rearrange("(k p) n -> p k n", p=128)
    XA, UA, HB, CVb = big[0], big[1], big[2], big[3]
    UAb, SZ = bigh[0], bigh[1]
    P.add("pool", lambda e: e.memset(XA[:, 0:2], 0.0), w=kb(0, 0, 2))
    P.add("pool", lambda e: e.memset(XA[:, SEQ + 2:SEQ + 4], 0.0), w=kb(0, SEQ + 2, SEQ + 4))

    def rgA_front(c):
        s = c % 2
        wA = wb[0][:, :, s * 256:(s + 1) * 256]
        dma("pool", wA[:, :, 0:128], w0[:, :, c * 128:(c + 1) * 128], [], [("wA", s, 0)], "wb%d_0" % s)
        dma("pool", wA[:, :, 128:256], w0[:, :, 1024 + c * 128:1024 + (c + 1) * 128], [], [("wA", s, 1)], "wb%d_1" % s)
        dma("pool", gwh[s][:], gate_w[:, :, c].rearrange("d g i j -> i (d g) j"), [], [("gwh", s)], "gw%d" % s)
        yield
        for j in range(NT):
            bank = 6 + (j % 2)
            for k in range(8):
                mm(pbx[bank], wA[:, k, 0:128], hT[:, k, tsl(j)], k == 0, k == 7, [("wA", s, 0)] + khT(k, j * T, j * T + T), [("pb", bank)])
            yield
            act(XA[:, 2 + j * T:2 + (j + 1) * T], pbx[bank], AF.Copy, [("pb", bank)], kb(0, 2 + j * T, 2 + (j + 1) * T))
            yield

    def rgA_conv(c):
        allXA = kb(0, 0, SEQ + 4)
        allUA = kb(1, 0, SEQ)
        ts("dve", UA[:, 0:SEQ], XA[:, 0:SEQ], pv[:, 2, c:c + 1], pv[:, 6, c:c + 1], ALU.mult, ALU.add, allXA + ["pv"], allUA)
        for kk in range(1, 4):
            stt("dve", UA[:, 0:SEQ], XA[:, kk:kk + SEQ], pv[:, 2 + kk, c:c + 1], UA[:, 0:SEQ], ALU.mult, ALU.add, allXA + allUA + ["pv"], allUA)
        hlf = SEQ // 2
        act(UAb[:, 0:hlf], UA[:, 0:hlf], AF.Copy, kb(1, 0, hlf), kh(0, 0, hlf))
        P.add("pool", lambda e: e.tensor_copy(out=UAb[:, hlf:SEQ], in_=UA[:, hlf:SEQ]), r=kb(1, hlf, SEQ), w=kh(0, hlf, SEQ))

    def rg_dir(d, c):
        s = c % 2
        cs = slice(c * 128, (c + 1) * 128)
        wA = wb[0][:, :, s * 256:(s + 1) * 256]

        def slots(b):
            if d == 1:
                return [bslot(3, i * 2 + b) for i in range(4)]
            return [wslot(1, i * 2 + b) for i in range(4)]

        br, bi = (2, 3) if d == 1 else (0, 1)
        g_r = gwh[s][:, d * 2 + 0, :]
        g_i = gwh[s][:, d * 2 + 1, :]
        prev = None
        pend = []
        for t in range(NT):
            j = t if d == 0 else NT - 1 - t
            t_other = NT - 1 - j if d == 0 else j
            store = (t < t_other) or (t == t_other and d == 1)
            b = j % 2
            uk = kh(0, j * T, j * T + T)
            (THr, k0), (THi, k1), (A, k2), (Ht, k3) = slots(b)
            mm(pbx[br], g_r, UAb[:, tsl(j)], True, True, [("gwh", s)] + uk, [("pb", br)])
            act(THr, pbx[br], AF.Tanh, [("pb", br), "der"], k0, bias=der[:, d * 2 + 0, c:c + 1], scale=0.5)
            mm(pbx[bi], g_i, UAb[:, tsl(j)], True, True, [("gwh", s)] + uk, [("pb", bi)])
            act(THi, pbx[bi], AF.Tanh, [("pb", bi), "der"], k1, bias=der[:, d * 2 + 1, c:c + 1], scale=0.5)
            act(A, THr, AF.Exp, k0 + ["der"], k2, bias=der[:, 4 + d, c:c + 1], scale=der[:, 4 + d, c:c + 1])
            act(THr, THr, AF.Exp, k0 + ["der"], k0, bias=der[:, 6 + d, c:c + 1], scale=der[:, 6 + d, c:c + 1])
            ts("pool", THr, THr, -1.0 / 16, 1.0 / 16, ALU.mult, ALU.add, k0, k0)
            stt("dve", THi, THi, 1.0, UA[:, tsl(j)], ALU.add, ALU.mult, k1 + kb(1, j * T, j * T + T), k1)
            pend.append((j, store))
            if t % 2 == 0 and t + 1 < NT:
                continue
            yield
            for (jj, _) in pend:
                (THr2, k02) = slots(jj % 2)[0]
                act(THr2, THr2, AF.Sqrt, k02, k02)
            yield
            for (jj, st) in pend:
                bb = jj % 2
                (THr2, k02), (THi2, k12), (A2, k22), (Ht2, k32) = slots(bb)
                hk2 = kb(2, jj * T, jj * T + T)
                tt("pool", THi2, THr2, THi2, ALU.mult, k02 + k12, k12)
                if prev is None:
                    init, rk = zero1[:, 0:1], ["zero1"]
                elif prev[1]:
                    col = prev[0] * T + (T - 1 if d == 0 else 0)
                    init, rk = HB[:, col:col + 1], kb(2, col, col + 1)
                else:
                    pslot, pk = slots(prev[0] % 2)[3]
                    init, rk = (pslot[:, T - 1:T] if d == 0 else pslot[:, 0:1]), pk
                if st:
                    scan(HB[:, tsl(jj)], A2, THi2, init, d == 1, k22 + k12 + rk, hk2)
                else:
                    zb = 4 + bb
                    for k in range(8):
                        mm(pbx[zb], wA[:, k, 128:256], hT[:, k, tsl(jj)], k == 0, k == 7,
                           [("wA", s, 1)] + khT(k, jj * T, jj * T + T), [("pb", zb)])
                    scan(Ht2, A2, THi2, init, d == 1, k22 + k12 + rk, k32)
                    act(THr2, pbx[zb], AF.Tanh, [("pb", zb)], k02, scale=0.5)
                    stt("dve", THr2, THr2, 1.0, pbx[zb], ALU.add, ALU.mult, k02 + [("pb", zb)], k02)
                    tt("dve", A2, Ht2, HB[:, tsl(jj)], ALU.add, k32 + hk2, k22)
                    ob, okk = tmph[d][bb], [("th%d" % d, bb)]
                    tt("pool", ob[:], A2, THr2, ALU.mult, k22 + k02, okk)
                    dma("sp", mixed0[cs, tsl(jj)], ob[:], okk, [("mixed0", c, jj)], "ya%d_%d" % (d, bb))
                prev = (jj, st)
            del pend[:]
            yield

    def run_rr(gens):
        live = list(gens)
        while live:
            for g in list(live):
                try:
                    next(g)
                except StopIteration:
                    live.remove(g)

    run_rr([rgA_front(0)])
    rgA_conv(0)
    for c in range(8):
        gens = [rg_dir(1, c), rg_dir(0, c)]
        if c + 1 < 8:
            gens.append(rgA_front(c + 1))
        run_rr(gens)
        if c + 1 < 8:
            rgA_conv(c + 1)

    Pb, GB = big[0], big[1]
    SZB, YB = bigh[0], bigh[1]
    P.add("pool", lambda e: e.memset(Pb[:, 0:1], 0.0), w=kb(0, 0, 1))
    P.add("pool", lambda e: e.memset(Pb[:, SEQ + 1:SEQ + 2], 0.0), w=kb(0, SEQ + 1, SEQ + 2))
    for c in range(8):
        s = c % 2
        wB = wb[1][:, :, s * 512:(s + 1) * 512]
        for q in range(4):
            dma("pool", wB[:, :, q * 128:(q + 1) * 128], w0[:, :, (2 + q) * 1024 + c * 128:(2 + q) * 1024 + (c + 1) * 128],
                [], [("wB", s, q)] + [("wbf", 1, n) for n in range(8)], "wB%d_%d" % (s, q))
        for j in range(NT):
            b = j % 2
            for q in range(4):
                for k in range(8):
                    mm(pb[q][:], wB[:, k, q * 128:(q + 1) * 128], hT[:, k, tsl(j)], k == 0, k == 7,
                       [("wB", s, q)] + khT(k, j * T, j * T + T), [("pb", q)])
            xbs, xbk = bslot(2, b)
            act(xbs, pb[0][:], AF.Copy, [("pb", 0)], xbk)
            tt("dve", Pb[:, 1 + j * T:1 + (j + 1) * T], pb[2][:], xbs, ALU.mult, [("pb", 2)] + xbk, kb(0, 1 + j * T, 1 + (j + 1) * T))
            act(GB[:, tsl(j)], pb[1][:], AF.Copy, [("pb", 1)], kb(1, j * T, j * T + T))
            szk = kh(0, j * T, j * T + T)
            act(SZB[:, tsl(j)], pb[3][:], AF.Tanh, [("pb", 3)], szk, scale=0.5)
            stt("dve", SZB[:, tsl(j)], SZB[:, tsl(j)], 1.0, pb[3][:], ALU.add, ALU.mult, szk + [("pb", 3)], szk)
        allP = kb(0, 0, SEQ + 4)
        allCV = kb(3, 0, SEQ)
        allGB = kb(1, 0, SEQ)
        allSZB = kh(0, 0, SEQ)
        allYB = kh(1, 0, SEQ)
        ts("dve", CVb[:, 0:SEQ], Pb[:, 0:SEQ], pv[:, 13, c:c + 1], None, ALU.mult, None, allP + ["pv"], allCV)
        for kk in range(1, 3):
            stt("dve", CVb[:, 0:SEQ], Pb[:, kk:kk + SEQ], pv[:, 13 + kk, c:c + 1], CVb[:, 0:SEQ], ALU.mult, ALU.add, allP + allCV + ["pv"], allCV)
        stt("dve", CVb[:, 0:SEQ], CVb[:, 0:SEQ], 0.5, GB[:, 0:SEQ], ALU.mult, ALU.mult, allCV + allGB, allCV)
        tt("pool", YB[:, 0:SEQ], CVb[:, 0:SEQ], SZB[:, 0:SEQ], ALU.mult, allCV + allSZB, allYB)
        dma("sp", mixed0[1024 + c * 128:1024 + (c + 1) * 128, :], YB[:, 0:SEQ], allYB, [("mixed0", 8 + c, j) for j in range(NT)], "yb")

    P.barrier()
    m0v = mixed0.rearrange("(k p) t -> p k t", p=128)
    mtb = [bigh[i][:, 0:8 * T].rearrange("p (k t) -> p k t", t=T) for i in range(2)]

    def mt_get0(j):
        b = j % 2
        lo = mtb[b]
        hi = hT[:, :, b * T:(b + 1) * T]
        rk = [("mixed0", c, j) for c in range(16)]
        dma("sp", lo, m0v[:, 0:8, tsl(j)], rk, [("mtlo", b)], "mt%d" % b)
        dma("sp", hi, m0v[:, 8:16, tsl(j)], rk, [("mthi", b)], "mth%d" % b)
        return (lambda k: lo[:, k, :] if k < 8 else hi[:, k - 8, :]), [("mtlo", b), ("mthi", b)]

    out_phase(mt_get0, 16, w_out0, 1, xT, x1T if layers > 1 else outT, "x0", hT[:, :, 2 * T:3 * T], ["sqh"])
    if layers == 1:
        P.emit(nc, es)
        es.close()
        return nc, P

    P.barrier()
    norm_phase(x1T, 16, "x0o")
    w1 = w_in1.rearrange("(k p) n -> p k n", p=128)
    LRs = nc.dram_tensor("LRs", [2, 16, SEQ], BF16, kind="Internal").ap()
    dma("pool", lrw[:], w1[:, :, 3072:3104], [], ["lrw"], "c8")
    dma("pool", wgt[:], wg.rearrange("d r k -> r d k"), [], ["wgt"], "c9")

    for q in range(2):
        dma("pool", wb[0][:, :, q * 512:(q + 1) * 512], w1[:, :, q * 512:(q + 1) * 512], [], [("wb", 0, q)], "wb0_%d" % q)
    for oc in range(8):
        dst = qTs if oc < 4 else kTs
        for j in range(NT):
            b = j % 2
            bank = (oc * NT + j) % 4
            for k in range(8):
                mm(pb[bank][:], wb[0][:, k, oc * 128:(oc + 1) * 128], hT[:, k, tsl(j)], k == 0, k == 7,
                   [("wb", 0, oc // 4)] + khT(k, j * T, j * T + T), [("pb", bank)])
            st, stk = bslot(bank, b)
            act(st, pb[bank][:], AF.Copy, [("pb", bank)], stk)
            dma("sp", dst[(oc % 4) * 128:(oc % 4 + 1) * 128, tsl(j)], st, stk, [("qk", oc, j)], "st%d_%d" % (bank, b))
    for q in range(2):
        dma("pool", wb[1][:, :, q * 512:(q + 1) * 512], w1[:, :, 1024 + q * 512:1024 + (q + 1) * 512], [], [("wb", 1, q)], "wb1_%d" % q)
    Vsv = Vs.rearrange("(m p) e -> p m e", p=128)
    Vt = [bigh[i][:, 0:4096].rearrange("p (m e) -> p m e", e=1024) for i in range(2)]
    for j in range(NT):
        b = j % 2
        for m in range(4):
            for half in range(2):
                bank = (m * 2 + half) % 4
                for k in range(8):
                    mm(pb[bank][:], hT[:, k, j * T + m * 128:j * T + (m + 1) * 128], wb[1][:, k, half * 512:(half + 1) * 512],
                       k == 0, k == 7, [("wb", 1, half)] + khT(k, j * T, j * T + T), [("pb", bank)])
                act(Vt[b][:, m, half * 512:(half + 1) * 512], pb[bank][:], AF.Copy, [("pb", bank)], [("Vt", b, m, half)])
        dma("sp", Vsv[:, j * 4:(j + 1) * 4, :], Vt[b], [("Vt", b, m, hf) for m in range(4) for hf in range(2)],
            [("Vs", j)], "vt%d" % b)
    for q in range(2):
        dma("pool", wb[0][:, :, q * 512:(q + 1) * 512], w1[:, :, 2048 + q * 512:2048 + (q + 1) * 512], [], [("wb", 0, q)], "wb0_%d" % q)
    for oc in range(8):
        for j in range(NT):
            b = j % 2
            bank = (oc * NT + j) % 4
            for k in range(8):
                mm(pb[bank][:], wb[0][:, k, oc * 128:(oc + 1) * 128], hT[:, k, tsl(j)], k == 0, k == 7,
                   [("wb", 0, oc // 4)] + khT(k, j * T, j * T + T), [("pb", bank)])
            stg = tmph[bank % 2][b]
            sk = [("th%d" % (bank % 2), b)]
            act(stg[:], pb[bank][:], AF.Tanh, [("pb", bank)], sk, scale=0.5)
            stt("dve", stg[:], stg[:], 1.0, pb[bank][:], ALU.add, ALU.mult, sk + [("pb", bank)], sk)
            dma("sp", SRs[oc * 128:(oc + 1) * 128, tsl(j)], stg[:], sk, [("SRs", oc, j)], "sr%d_%d" % (bank % 2, b))
    for d in range(2):
        for j in range(NT):
            b = j % 2
            bank = 4 + b
            for k in range(8):
                mm(pb[bank][0:16, :], lrw[:, k, d * 16:(d + 1) * 16], hT[:, k, tsl(j)], k == 0, k == 7,
                   ["lrw"] + khT(k, j * T, j * T + T), [("pb", bank)])
            act(tmph[2][b][0:16, :], pb[bank][0:16, :], AF.Copy, [("pb", bank)], [("th2", b)])
            dma("sp", LRs[d, :, tsl(j)], tmph[2][b][0:16, :], [("th2", b)], [("LRs", d, j)], "lrs%d" % b)

    P.barrier()
    SC = 128.0 ** -0.5
    qh, kh_, O = big[0], big[1], [big[2], big[3]]
    HC = NCH // 2
    Vhh = [bigh[i][:, 0:HC * 256].rearrange("p (m e) -> p m e", e=256) for i in range(2)]

    def Vn(n):
        return Vhh[n // HC][:, n % HC, :]

    sq1 = bigh[1][:, 0:8 * T].rearrange("p (k t) -> p k t", t=T)
    for h in range(4):
        dma("sp", qh[:, 0:SEQ], qTs[h * 128:(h + 1) * 128, :], [("qk", h, j) for j in range(NT)], ["qh"], "qh")
        dma("sp", kh_[:, 0:SEQ], kTs[h * 128:(h + 1) * 128, :], [("qk", 4 + h, j) for j in range(NT)], ["kh"], "kh")
        for i in range(2):
            dma("sp", Vhh[i], Vsv[:, i * HC:(i + 1) * HC, h * 256:(h + 1) * 256], [("Vs", j) for j in range(NT)], ["Vh"], "vh%d" % i)
        def gla_dir(d, h=h):
            first = True
            for sidx in range(NT):
                j = sidx if d == 0 else NT - 1 - sidx
                (Lt, k0), (Lc, k1), (E1, k2), (E2, k3) = [wslot(0, i * 2 + d) for i in range(4)]
                QI, KI, KST = tmph[0][d], tmph[1][d], tmph[2][d]
                bz, bo0, bo1, bk = 0 + d, 2 + d, 4 + d, 6 + d
                pz = pbx[bz]
                dma("sp", lrt[d][:], LRs[d, :, tsl(j)], [("LRs", d, j)], [("lrt", d)], "lrt%d" % d)
                mm(pz, wgt[0:16, d, h * 128:(h + 1) * 128], lrt[d][:], True, True, ["wgt", ("lrt", d)], [("pb", bz)])
                yield
                act(Lt, pz, AF.Exp, [("pb", bz), "negb"], k0, bias=negb[:, d, h:h + 1], scale=-1.0)
                act(Lt, Lt, AF.Ln, k0, k0, bias=1.0)
                yield
                scan(Lc, m01[:, :], Lt, 0.0, d == 1, k0 + ["m01"], k1)
                yield
                tot = Lc[:, T - 1:T] if d == 0 else Lc[:, 0:1]
                act(E1, Lc, AF.Exp, k1, k2, scale=-1.0 / 16)
                act(E2, Lc, AF.Exp, k1, k3, scale=1.0 / 16)
                act(DEC[d][:, j, :], tot, AF.Exp, k1, [("DEC", d, j)], scale=-1.0 / 16)
                yield
                stt("dve", QI[:], qh[:, tsl(j)], SC, E1, ALU.mult, ALU.mult, ["qh"] + k2, [("th0", d)])
                tt("pool", KI[:], kh_[:, tsl(j)], E2, ALU.mult, ["kh"] + k3, [("th1", d)])
                yield
                ts("dve", E1, Lc, tot, None, ALU.subtract, None, k1 + k2, k2)
                yield
                act(E1, E1, AF.Exp, k2, k2, scale=1.0 / 16)
                yield
                tt("pool", KST[:], kh_[:, tsl(j)], E1, ALU.mult, ["kh"] + k2, [("th2", d)])
                yield
                kt_ps = (pb6h if d == 0 else pbt)[:, 0:T]
                kv_ps = pbx[bk][:, 256:512]
                jcs = list(range(4)) if d == 0 else list(range(3, -1, -1))
                SM = [wb[1][:, 2 + (d * 4 + jc) // 2, ((d * 4 + jc) % 2) * T:((d * 4 + jc) % 2 + 1) * T] for jc in range(4)]
                for jc in range(4):
                    P.add("pe", lambda e, o=kt_ps[:, jc * 128:(jc + 1) * 128], i=KST[:, jc * 128:(jc + 1) * 128]:
                          e.transpose(out=o, in_=i, identity=ident[:]), r=[("th2", d), "ident"], w=[("pb", bk)])
                yield
                act(KTt[d][:].rearrange("p c t -> p (c t)"), kt_ps, AF.Copy, [("pb", bk)], [("KT", d)])
                yield
                for jc in jcs:
                    if d == 0:
                        isl, msl = slice(jc * 128, T), tri[:, 0, 0:(4 - jc) * 128]
                    else:
                        isl, msl = slice(0, (jc + 1) * 128), tri[:, 1, (3 - jc) * 128:T]
                    ncol = isl.stop - isl.start
                    mm(pz[:, 0:ncol], KI[:, jc * 128:(jc + 1) * 128], QI[:, isl], True, True, [("th1", d), ("th0", d)], [("pb", bz)])
                    yield
                    tt("dve", SM[jc][:, 0:ncol], pz[:, 0:ncol], msl, ALU.mult, [("pb", bz), "tri"], [("SM", d, jc)])
                    yield
                for jc in range(4):
                    mm(kv_ps, KTt[d][:, jc, :], Vn(j * 4 + jc), jc == 0, jc == 3, [("KT", d), "Vh"], [("pb", bk)])
                yield
                for ec, bo in ((0, bo0), (1, bo1)):
                    for idx, jc in enumerate(jcs):
                        isl = slice(jc * 128, T) if d == 0 else slice(0, (jc + 1) * 128)
                        ncol = isl.stop - isl.start
                        mm(pbx[bo][:, isl], Vn(j * 4 + jc)[:, ec * 128:(ec + 1) * 128], SM[jc][:, 0:ncol], idx == 0, first and idx == 3,
                           ["Vh", ("SM", d, jc)], [("pb", bo)])
                    if not first:
                        mm(pbx[bo], Sbf[d][:, ec * 128:(ec + 1) * 128], QI[:], False, True, [("Sb", d), ("th0", d)], [("pb", bo)])
                yield
                if first:
                    P.add("dve", lambda e, o=Sst[d][:], i=kv_ps: e.tensor_copy(out=o, in_=i), r=[("pb", bk)], w=[("S", d)])
                else:
                    stt("dve", Sst[d][:], Sst[d][:], DEC[d][:, j, :], kv_ps, ALU.mult, ALU.add,
                        [("S", d), ("DEC", d, j), ("pb", bk)], [("S", d)])
                yield
                act(Sbf[d][:], Sst[d][:], AF.Copy, [("S", d)], [("Sb", d)])
                firstO = (d == 0) == (2 * j <= NT - 1)
                okj = [[("O", ec, n) for n in range(j * 4, j * 4 + 4)] for ec in range(2)]
                for ec, bo in ((0, bo0), (1, bo1)):
                    if firstO:
                        act(O[ec][:, tsl(j)], pbx[bo], AF.Copy, [("pb", bo)], okj[ec])
                    else:
                        tt("dve", O[ec][:, tsl(j)], O[ec][:, tsl(j)], pbx[bo], ALU.add, [("pb", bo)] + okj[ec], okj[ec])
                first = False
                yield

        g0, g1 = gla_dir(0), gla_dir(1)
        for _ in range(11):
            next(g0)
        run_rr([g0, g1])
        for j in range(NT):
            b = j % 2
            okeys = [[("O", ec, n) for n in range(j * 4, j * 4 + 4)] for ec in range(2)]
            for ec in range(2):
                dma("sp", tmph[ec][b][:], SRs[(2 * h + ec) * 128:(2 * h + ec + 1) * 128, tsl(j)],
                    [("SRs", 2 * h + ec, j)], [("th%d" % ec, b)], "srl%d_%d" % (ec, b))
                act(sqs[:, ec, :], O[ec][:, tsl(j)], AF.Square, okeys[ec], [("sq1", ec)])
            for ec in range(2):
                mm(pb[6][:], ones[:], sqs[:, ec, :], ec == 0, ec == 1, ["ones", ("sq1", ec)], [("pb", 6)])
            rstd_from(pb[6][:], b, 256, mul=0.5)
            for ec in range(2):
                t4, t4k = wslot(1, ec)
                stt("dve", t4, O[ec][:, tsl(j)], pngt[:, ec:ec + 1], rs[b][:], ALU.mult, ALU.mult,
                    okeys[ec] + [("rs", b), "pngt"], t4k)
                tt("pool", hT[:, 2 * h + ec, tsl(j)], t4, tmph[ec][b][:], ALU.mult,
                   t4k + [("th%d" % ec, b)], khT(2 * h + ec, j * T, j * T + T))

    P.barrier()

    def mt_get1(j):
        return (lambda k: hT[:, k, tsl(j)]), [("hT", k, j) for k in range(8)]

    out_phase(mt_get1, 8, w_out1, 17, x1T, outT, "x0o", sq1, ["sq1all"])
    P.emit(nc, es)
    es.close()
    return nc, P


def host_inputs(x_b, inp):
    f = lambda a: np.ascontiguousarray(np.asarray(a, dtype=np.float32))
    vecs = [inp["even_norm_pre"][0], inp["even_norm_post"][0]] + [inp["rg_conv_w"][0, i] for i in range(4)] + \
           [inp["rg_conv_b"][0]] + [inp["rg_gate_b"][0, d, g].reshape(-1) for d in range(2) for g in range(2)] + \
           [inp["rg_lambda"][0, d] for d in range(2)] + [inp["sc_conv_w"][0, i] for i in range(3)] + \
           [inp["odd_norm_pre"][0], inp["odd_norm_post"][0]]
    pvec = f(np.stack([np.asarray(v) for v in vecs], 0).reshape(NV, 8, 128).transpose(2, 0, 1))
    pg = f(np.asarray(inp["gla_b_gate"][0]).reshape(2, 4, 128).transpose(2, 0, 1))
    png = f(np.asarray(inp["gla_norm_g"][0]).reshape(2, 128).transpose(1, 0))
    cm01 = np.ones((128, T), np.float32)
    jj, ii = np.meshgrid(np.arange(128), np.arange(128), indexing="ij")
    mf = np.ones((128, T), np.float32)
    mf[:, 0:128] = (jj <= ii)
    mb = np.ones((128, T), np.float32)
    mb[:, T - 128:T] = (jj >= ii)
    ctri = f(np.stack([mf, mb], 1))
    return {
        "xT": f(np.asarray(x_b).T), "w_in0": f(inp["even_w_in"][0]), "gate_w": f(inp["rg_gate_w"][0]),
        "w_out0": f(inp["even_w_out"][0]), "w_in1": f(inp["odd_w_in"][0]), "wg": f(inp["gla_w_gate_lr"][0]),
        "w_out1": f(inp["odd_w_out"][0]), "pvec": pvec, "pg": pg, "png": png, "cm01": cm01, "ctri": ctri,
        "cones": np.ones((128, 128), np.float32), "cident": np.eye(128, dtype=np.float32),
    }


def kernel(**inputs):
    x = np.asarray(inputs["x"])
    B, SEQ, _ = x.shape
    nc, _ = build_nc(SEQ)
    shared = None
    in_maps = []
    for b in range(B):
        m = host_inputs(x[b], inputs)
        if shared is None:
            shared = m
        else:
            for k in m:
                if k != "xT":
                    m[k] = shared[k]
        in_maps.append(m)
    res = run_bass_kernel_spmd(nc, in_maps, core_ids=list(range(B)))
    out = np.stack([np.asarray(r["outT"]).T for r in res.results], 0)
    return np.ascontiguousarray(out.astype(np.float32))
```

```python
import numpy as np
from contextlib import ExitStack
import concourse.bass as bass
import concourse.mybir as mybir
from concourse.bass_utils import run_bass_kernel_spmd
from concourse.ap import AP

F32 = mybir.dt.float32
BF16 = mybir.dt.bfloat16
AF = mybir.ActivationFunctionType
ALU = mybir.AluOpType

D = 1024
T = 512
EPS = 1e-6
NV = 18
LAYERS = 2
SBUF_LEFT = []


def rev(ap):
    pairs = [list(p) for p in ap.ap]
    step, cnt = pairs[-1]
    pairs[-1] = [-step, cnt]
    return AP(ap.tensor, ap.offset + step * (cnt - 1), pairs)


class _Op:
    __slots__ = ("eng", "fn", "deps", "dma_key", "signal", "val", "idx", "semkey")


class Prog:
    def __init__(self):
        self.ops = []
        self.lastw = {}
        self.rd_eng = {}
        self.rd_dma = {}
        self.last_eng = {}
        self.last_dma = {}
        self.fence = []
        self.passed = set()

    def barrier(self):
        self.fence = list(self.last_eng.values()) + list(self.last_dma.values())
        self.passed = set()

    def add(self, eng, fn, r=(), w=(), dma_key=None):
        op = _Op()
        op.eng, op.fn, op.dma_key = eng, fn, dma_key
        op.signal = dma_key is not None
        op.val = 0
        op.semkey = None
        op.idx = len(self.ops)
        deps = {}
        if self.fence and eng not in self.passed:
            for p in self.fence:
                deps[p.idx] = p
            self.passed.add(eng)
        for k in r:
            p = self.lastw.get(k)
            if p is not None:
                deps[p.idx] = p
        for k in w:
            p = self.lastw.get(k)
            if p is not None:
                deps[p.idx] = p
            for q in self.rd_eng.get(k, {}).values():
                deps[q.idx] = q
            for q in self.rd_dma.get(k, ()):
                deps[q.idx] = q
        op.deps = [p for p in deps.values() if (p.dma_key is not None) or (p.eng != eng) or (eng != "pe")]
        for k in r:
            if dma_key is not None:
                self.rd_dma.setdefault(k, []).append(op)
            else:
                self.rd_eng.setdefault(k, {})[eng] = op
        for k in w:
            self.lastw[k] = op
            self.rd_eng[k] = {}
            self.rd_dma[k] = []
        if dma_key is not None:
            self.last_dma[dma_key] = op
        else:
            self.last_eng[eng] = op
        self.ops.append(op)
        return op

    def emit(self, nc, es):
        for op in self.ops:
            for p in op.deps:
                p.signal = True
        cnt = {}
        for op in self.ops:
            if op.signal:
                key = ("dma", op.dma_key) if op.dma_key is not None else ("eng", op.eng)
                cnt[key] = cnt.get(key, 0) + (16 if op.dma_key is not None else 1)
                op.semkey, op.val = key, cnt[key]
        sems = {}
        for i, key in enumerate(cnt):
            sems[key] = es.enter_context(nc.semaphore("s%d" % i))
        self.nsem = len(sems)
        block = es.enter_context(nc.Block())
        ops = self.ops

        def run(engname, final=False):
            def body(e):
                waited = {}
                for op in ops:
                    if op.eng != engname:
                        continue
                    for p in op.deps:
                        if waited.get(p.semkey, 0) < p.val:
                            e.wait_ge(sems[p.semkey], p.val)
                            waited[p.semkey] = p.val
                    ins = op.fn(e)
                    if op.signal:
                        ins.then_inc(sems[op.semkey], 16 if op.dma_key is not None else 1)
                if final:
                    for key, v in cnt.items():
                        if waited.get(key, 0) < v:
                            e.wait_ge(sems[key], v)
            return body

        block.sync(run("sp", final=True))
        block.tensor(run("pe"))
        block.scalar(run("act"))
        block.vector(run("dve"))
        block.gpsimd(run("pool"))


def build_nc(SEQ, layers=LAYERS):
    NT = SEQ // T
    NCH = SEQ // 128
    BW = max(SEQ, 8 * T)
    nc = bass.Bass("TRN2", target_bir_lowering=False)
    din = lambda n, s, dt=F32: nc.dram_tensor(n, s, dt, kind="ExternalInput").ap()
    xT = din("xT", [D, SEQ])
    w_in0 = din("w_in0", [D, 6144])
    gate_w = din("gate_w", [2, 2, 8, 128, 128])
    w_out0 = din("w_out0", [2048, D])
    w_in1 = din("w_in1", [D, 3104])
    wg = din("wg", [2, 16, 512])
    w_out1 = din("w_out1", [D, D])
    pvec = din("pvec", [128, NV, 8])
    pg = din("pg", [128, 2, 4])
    png = din("png", [128, 2])
    cm01 = din("cm01", [128, T])
    ctri = din("ctri", [128, 2, T])
    cones = din("cones", [128, 128])
    cident = din("cident", [128, 128])
    outT = nc.dram_tensor("outT", [D, SEQ], F32, kind="ExternalOutput").ap()
    mixed0 = nc.dram_tensor("mixed0", [2048, SEQ], BF16, kind="Internal").ap()
    x1T = nc.dram_tensor("x1T", [D, SEQ], F32, kind="Internal").ap()
    qTs = nc.dram_tensor("qTs", [512, SEQ], F32, kind="Internal").ap()
    kTs = nc.dram_tensor("kTs", [512, SEQ], F32, kind="Internal").ap()
    Vs = nc.dram_tensor("Vs", [SEQ, D], BF16, kind="Internal").ap()
    SRs = nc.dram_tensor("SRs", [D, SEQ], BF16, kind="Internal").ap()

    es = ExitStack()
    P = Prog()
    sb = lambda n, s, dt=F32: es.enter_context(nc.sbuf_tensor(n, s, dt))
    ps = lambda n, s, dt=F32: es.enter_context(nc.psum_tensor(n, s, dt))

    hT = sb("hT", [128, 8, SEQ], BF16)
    pv = sb("pv", [128, NV, 8])
    pgt = sb("pgt", [128, 2, 4])
    pngt = sb("pngt", [128, 2])
    m01 = sb("m01", [128, T], BF16)
    tri = sb("tri", [128, 2, T])
    ones = sb("ones", [128, 128], BF16)
    ident = sb("ident", [128, 128], BF16)
    der = sb("der", [128, 12, 8])
    negb = sb("negb", [128, 2, 4])
    gwh = [sb("gwh%d" % i, [128, 4, 128], BF16) for i in range(2)]
    lrw = sb("lrw", [128, 8, 32], BF16)
    wgt = sb("wgt", [16, 2, 512], BF16)
    lrt = [sb("lrt%d" % d, [16, T], BF16) for d in range(2)]
    rs = [sb("rs%d" % i, [128, T]) for i in range(2)]
    wb = [sb("wb%d" % i, [128, 8, 1024], BF16) for i in range(2)]
    wbf = [wb[i][:].rearrange("p k n -> p (k n)").bitcast(F32) for i in range(2)]
    big = [sb("big%d" % i, [128, BW + 4]) for i in range(4)]
    bigh = [sb("bigh%d" % i, [128, BW], BF16) for i in range(2)]
    tmph = [[sb("tmph%d_%d" % (i, j), [128, T], BF16) for j in range(2)] for i in range(3)]
    zero1 = sb("zero1", [128, 1])
    Sst = [sb("Sst%d" % d, [128, 256]) for d in range(2)]
    Sbf = [sb("Sbf%d" % d, [128, 256], BF16) for d in range(2)]
    DEC = [sb("DEC%d" % d, [128, NT, 1]) for d in range(2)]
    KTt = [sb("KTt%d" % d, [128, 4, 128], BF16) for d in range(2)]
    xt = [big[2 + i][:, 0:8 * T].rearrange("p (k t) -> p k t", t=T) for i in range(2)]
    sqs = sb("sqs", [128, 2, T], BF16)

    SBUF_LEFT.append(nc.sbuf_bytes_remaining)
    pb = [ps("pb%d" % i, [128, T]) for i in range(7)]
    pbt = ps("pbt", [128, 1024], BF16)
    pbx = [pb[i][:] for i in range(7)] + [pbt[:].bitcast(F32)]
    pb6h = pb[6][:].bitcast(BF16)

    def kb(i, lo, hi):
        return [("big", i, g) for g in range(lo // T, (hi - 1) // T + 1)]

    def kh(i, lo, hi):
        return [("bigh", i, g) for g in range(lo // T, (hi - 1) // T + 1)]

    def khT(k, lo, hi):
        return [("hT", k, g) for g in range(lo // T, (hi - 1) // T + 1)]

    def bslot(i, n):
        return big[i][:, n * T:(n + 1) * T], [("big", i, n)]

    def wslot(i, n):
        return wbf[i][:, n * T:(n + 1) * T], [("wbf", i, n)]

    def bcast_last(ap, n):
        pairs = [list(p) for p in ap.ap]
        assert pairs[-1][1] == 1
        pairs[-1] = [0, n]
        return AP(ap.tensor, ap.offset, pairs)

    def dma(eng, out, in_, r, w, key):
        P.add(eng, lambda e: e.dma_start(out=out, in_=in_), r=r, w=w, dma_key=key)

    def act(out, in_, func, r, w, bias=0.0, scale=1.0):
        P.add("act", lambda e: e.activation(out=out, in_=in_, func=func, bias=bias, scale=scale), r=r, w=w)

    def tt(eng, out, in0, in1, op, r, w):
        P.add(eng, lambda e: e.tensor_tensor(out=out, in0=in0, in1=in1, op=op), r=r, w=w)

    def ts(eng, out, in0, s1, s2, op0, op1, r, w):
        if op1 is None:
            P.add(eng, lambda e: e.tensor_scalar(out=out, in0=in0, scalar1=s1, scalar2=None, op0=op0), r=r, w=w)
        else:
            P.add(eng, lambda e: e.tensor_scalar(out=out, in0=in0, scalar1=s1, scalar2=s2, op0=op0, op1=op1), r=r, w=w)

    def stt(eng, out, in0, s, in1, op0, op1, r, w):
        P.add(eng, lambda e: e.scalar_tensor_tensor(out=out, in0=in0, scalar=s, in1=in1, op0=op0, op1=op1), r=r, w=w)

    def mm(out, lhsT, rhs, start, stop, r, w):
        P.add("pe", lambda e: e.matmul(out, lhsT=lhsT, rhs=rhs, start=start, stop=stop), r=r, w=w)

    def scan(out, d0, d1, init, reverse, r, w):
        if reverse:
            P.add("dve", lambda e: e.tensor_tensor_scan(out=rev(out), data0=rev(d0), data1=rev(d1), initial=init,
                                                        op0=ALU.mult, op1=ALU.add), r=r, w=w)
        else:
            P.add("dve", lambda e: e.tensor_tensor_scan(out=out, data0=d0, data1=d1, initial=init,
                                                        op0=ALU.mult, op1=ALU.add), r=r, w=w)

    def tsl(j):
        return slice(j * T, (j + 1) * T)

    xtk = [kb(2 + i, 0, 8 * T) for i in range(2)]

    dma("sp", pv[:], pvec[:, :, :], [], ["pv"], "c0")
    dma("sp", pgt[:], pg[:, :, :], [], ["pgt"], "c1")
    dma("sp", pngt[:], png[:, :], [], ["pngt"], "c2")
    dma("pool", m01[:], cm01[:, :], [], ["m01"], "c3")
    dma("sp", tri[:], ctri[:, :, :], [], ["tri"], "c4")
    dma("pool", ones[:], cones[:, :], [], ["ones"], "c5")
    dma("pool", ident[:], cident[:, :], [], ["ident"], "c6")
    P.add("dve", lambda e: e.memset(zero1[:], 0.0), w=["zero1"])
    ts("dve", der[:, 0:4, :], pv[:, 7:11, :], 0.5, None, ALU.mult, None, ["pv"], ["der"])
    act(der[:, 8:10, :], pv[:, 11:13, :], AF.Exp, ["pv"], ["der8"], scale=-1.0)
    act(der[:, 10:12, :], der[:, 8:10, :], AF.Ln, ["der8"], ["der10"], bias=1.0)
    ts("dve", der[:, 4:6, :], der[:, 10:12, :], -4.0, None, ALU.mult, None, ["der10"], ["der"])
    ts("dve", der[:, 6:8, :], der[:, 10:12, :], -8.0, None, ALU.mult, None, ["der10"], ["der"])
    ts("dve", negb[:], pgt[:], -1.0, None, ALU.mult, None, ["pgt"], ["negb"])

    def rstd_from(psum_ap, b, n, mul=1.0):
        act(rs[b][:], psum_ap, AF.Ln, [("pb", 6)], [("rs", b)], bias=EPS, scale=1.0 / n)
        act(rs[b][:], rs[b][:], AF.Exp, [("rs", b)], [("rs", b)], scale=-0.5, bias=float(np.log(mul)))

    def norm_phase(src, gidx, tag):
        srcv = src.rearrange("(k p) t -> p k t", p=128)
        sq = bigh[1][:, 0:8 * T].rearrange("p (k t) -> p k t", t=T)
        sqk = kh(1, 0, 8 * T)
        for j in range(NT):
            b = j % 2
            dma("sp", xt[b], srcv[:, :, tsl(j)], [("src", tag, j)], xtk[b], "xt%d" % b)
            act(sq, xt[b], AF.Square, xtk[b], sqk)
            for k in range(8):
                mm(pb[6][:], ones[:], sq[:, k, :], k == 0, k == 7, ["ones"] + sqk, [("pb", 6)])
            rstd_from(pb[6][:], b, D)
            for k in range(8):
                stt("dve", hT[:, k, tsl(j)], xt[b][:, k, :], pv[:, gidx, k:k + 1], rs[b][:], ALU.mult, ALU.mult,
                    xtk[b] + [("rs", b), "pv"], khT(k, j * T, (j + 1) * T))

    def out_phase(mt_get, nk, wsrc, gidx, resid, dst, tag, sq, sqk):
        wo = wsrc.rearrange("(k p) n -> p k n", p=128)
        for k0 in range(0, nk, 4):
            s, q = k0 // 8, (k0 % 8) // 4
            dma("pool", wb[s][:, q * 4:q * 4 + 4, :], wo[:, k0:k0 + 4, :], [], [("wb", s, q)], "wb%d_%d" % (s, q))
        resv = resid.rearrange("(k p) t -> p k t", p=128)
        dstv = dst.rearrange("(k p) t -> p k t", p=128)
        ysb = [big[i][:, 0:8 * T].rearrange("p (k t) -> p k t", t=T) for i in range(2)]
        for j in range(NT):
            b = j % 2
            ys = ysb[b]
            mget, mkeys = mt_get(j)
            dma("sp", xt[b], resv[:, :, tsl(j)], [("src", tag, j)], xtk[b], "xt%d" % b)
            for o in range(8):
                bank = o % 4
                for k in range(nk):
                    mm(pb[bank][:], wb[k // 8][:, k % 8, o * 128:(o + 1) * 128], mget(k), k == 0, k == nk - 1,
                       [("wb", k // 8, (k % 8) // 4)] + mkeys, [("pb", bank)])
                act(ys[:, o, :], pb[bank][:], AF.Copy, [("pb", bank)], [("big", b, o)])
                act(sq[:, o, :], pb[bank][:], AF.Square, [("pb", bank)], sqk)
            for o in range(8):
                mm(pb[6][:], ones[:], sq[:, o, :], o == 0, o == 7, ["ones"] + sqk, [("pb", 6)])
            rstd_from(pb[6][:], b, D)
            for o in range(8):
                stt("dve", ys[:, o, :], ys[:, o, :], pv[:, gidx, o:o + 1], rs[b][:], ALU.mult, ALU.mult,
                    [("big", b, o), ("rs", b), "pv"], [("big", b, o)])
                tt("pool", ys[:, o, :], ys[:, o, :], xt[b][:, o, :], ALU.add, [("big", b, o)] + xtk[b], [("big", b, o)])
            dma("sp", dstv[:, :, tsl(j)], ys, [("big", b, o) for o in range(8)], [("src", tag + "o", j)], "xo%d" % b)

    norm_phase(xT, 0, "x0")
    w0 = w_in0.rearrange("(k p) n -> p k n", p=128)
    XA, UA, HB, CVb = big[0], big[1], big[2], big[3]
    UAb, SZ = bigh[0], bigh[1]
    P.add("pool", lambda e: e.memset(XA[:, 0:2], 0.0), w=kb(0, 0, 2))
    P.add("pool", lambda e: e.memset(XA[:, SEQ + 2:SEQ + 4], 0.0), w=kb(0, SEQ + 2, SEQ + 4))

    def rgA_front(c):
        s = c % 2
        wA = wb[0][:, :, s * 256:(s + 1) * 256]
        dma("pool", wA[:, :, 0:128], w0[:, :, c * 128:(c + 1) * 128], [], [("wA", s, 0)], "wb%d_0" % s)
        dma("pool", wA[:, :, 128:256], w0[:, :, 1024 + c * 128:1024 + (c + 1) * 128], [], [("wA", s, 1)], "wb%d_1" % s)
        dma("pool", gwh[s][:], gate_w[:, :, c].rearrange("d g i j -> i (d g) j"), [], [("gwh", s)], "gw%d" % s)
        yield
        for j in range(NT):
            bank = 6 + (j % 2)
            for k in range(8):
                mm(pbx[bank], wA[:, k, 0:128], hT[:, k, tsl(j)], k == 0, k == 7, [("wA", s, 0)] + khT(k, j * T, j * T + T), [("pb", bank)])
            yield
            act(XA[:, 2 + j * T:2 + (j + 1) * T], pbx[bank], AF.Copy, [("pb", bank)], kb(0, 2 + j * T, 2 + (j + 1) * T))
            yield

    def rgA_conv(c):
        allXA = kb(0, 0, SEQ + 4)
        allUA = kb(1, 0, SEQ)
        ts("dve", UA[:, 0:SEQ], XA[:, 0:SEQ], pv[:, 2, c:c + 1], pv[:, 6, c:c + 1], ALU.mult, ALU.add, allXA + ["pv"], allUA)
        for kk in range(1, 4):
            stt("dve", UA[:, 0:SEQ], XA[:, kk:kk + SEQ], pv[:, 2 + kk, c:c + 1], UA[:, 0:SEQ], ALU.mult, ALU.add, allXA + allUA + ["pv"], allUA)
        hlf = SEQ // 2
        act(UAb[:, 0:hlf], UA[:, 0:hlf], AF.Copy, kb(1, 0, hlf), kh(0, 0, hlf))
        P.add("pool", lambda e: e.tensor_copy(out=UAb[:, hlf:SEQ], in_=UA[:, hlf:SEQ]), r=kb(1, hlf, SEQ), w=kh(0, hlf, SEQ))

    def rg_dir(d, c):
        s = c % 2
        cs = slice(c * 128, (c + 1) * 128)
        wA = wb[0][:, :, s * 256:(s + 1) * 256]

        def slots(b):
            if d == 1:
                return [bslot(3, i * 2 + b) for i in range(4)]
            return [wslot(1, i * 2 + b) for i in range(4)]

        br, bi = (2, 3) if d == 1 else (0, 1)
        g_r = gwh[s][:, d * 2 + 0, :]
        g_i = gwh[s][:, d * 2 + 1, :]
        prev = None
        pend = []
        for t in range(NT):
            j = t if d == 0 else NT - 1 - t
            t_other = NT - 1 - j if d == 0 else j
            store = (t < t_other) or (t == t_other and d == 1)
            b = j % 2
            uk = kh(0, j * T, j * T + T)
            (THr, k0), (THi, k1), (A, k2), (Ht, k3) = slots(b)
            mm(pbx[br], g_r, UAb[:, tsl(j)], True, True, [("gwh", s)] + uk, [("pb", br)])
            act(THr, pbx[br], AF.Tanh, [("pb", br), "der"], k0, bias=der[:, d * 2 + 0, c:c + 1], scale=0.5)
            mm(pbx[bi], g_i, UAb[:, tsl(j)], True, True, [("gwh", s)] + uk, [("pb", bi)])
            act(THi, pbx[bi], AF.Tanh, [("pb", bi), "der"], k1, bias=der[:, d * 2 + 1, c:c + 1], scale=0.5)
            act(A, THr, AF.Exp, k0 + ["der"], k2, bias=der[:, 4 + d, c:c + 1], scale=der[:, 4 + d, c:c + 1])
            tt("pool", THr, A, A, ALU.mult, k2, k0)
            ts("pool", THr, THr, -1.0 / 16, 1.0 / 16, ALU.mult, ALU.add, k0, k0)
            stt("dve", THi, THi, 1.0, UA[:, tsl(j)], ALU.add, ALU.mult, k1 + kb(1, j * T, j * T + T), k1)
            pend.append((j, store))
            if t % 2 == 0 and t + 1 < NT:
                continue
            yield
            for (jj, _) in pend:
                (THr2, k02) = slots(jj % 2)[0]
                act(THr2, THr2, AF.Sqrt, k02, k02)
            yield
            for (jj, st) in pend:
                bb = jj % 2
                (THr2, k02), (THi2, k12), (A2, k22), (Ht2, k32) = slots(bb)
                hk2 = kb(2, jj * T, jj * T + T)
                tt("pool", THi2, THr2, THi2, ALU.mult, k02 + k12, k12)
                if prev is None:
                    init, rk = zero1[:, 0:1], ["zero1"]
                elif prev[1]:
                    col = prev[0] * T + (T - 1 if d == 0 else 0)
                    init, rk = HB[:, col:col + 1], kb(2, col, col + 1)
                else:
                    pslot, pk = slots(prev[0] % 2)[3]
                    init, rk = (pslot[:, T - 1:T] if d == 0 else pslot[:, 0:1]), pk
                if st:
                    scan(HB[:, tsl(jj)], A2, THi2, init, d == 1, k22 + k12 + rk, hk2)
                else:
                    zb = 4 + bb
                    for k in range(8):
                        mm(pbx[zb], wA[:, k, 128:256], hT[:, k, tsl(jj)], k == 0, k == 7,
                           [("wA", s, 1)] + khT(k, jj * T, jj * T + T), [("pb", zb)])
                    scan(Ht2, A2, THi2, init, d == 1, k22 + k12 + rk, k32)
                    act(THr2, pbx[zb], AF.Tanh, [("pb", zb)], k02, scale=0.5)
                    stt("dve", THr2, THr2, 1.0, pbx[zb], ALU.add, ALU.mult, k02 + [("pb", zb)], k02)
                    tt("dve", A2, Ht2, HB[:, tsl(jj)], ALU.add, k32 + hk2, k22)
                    ob, okk = tmph[d][bb], [("th%d" % d, bb)]
                    tt("pool", ob[:], A2, THr2, ALU.mult, k22 + k02, okk)
                    dma("sp", mixed0[cs, tsl(jj)], ob[:], okk, [("mixed0", c, jj)], "ya%d_%d" % (d, bb))
                prev = (jj, st)
            del pend[:]
            yield

    def run_rr(gens):
        live = list(gens)
        while live:
            for g in list(live):
                try:
                    next(g)
                except StopIteration:
                    live.remove(g)

    run_rr([rgA_front(0)])
    rgA_conv(0)
    for c in range(8):
        gens = [rg_dir(1, c), rg_dir(0, c)]
        if c + 1 < 8:
            gens.append(rgA_front(c + 1))
        run_rr(gens)
        if c + 1 < 8:
            rgA_conv(c + 1)

    Pb, GB = big[0], big[1]
    SZB, YB = bigh[0], bigh[1]
    P.add("pool", lambda e: e.memset(Pb[:, 0:1], 0.0), w=kb(0, 0, 1))
    P.add("pool", lambda e: e.memset(Pb[:, SEQ + 1:SEQ + 2], 0.0), w=kb(0, SEQ + 1, SEQ + 2))
    for c in range(8):
        s = c % 2
        wB = wb[1][:, :, s * 512:(s + 1) * 512]
        for q in range(4):
            dma("pool", wB[:, :, q * 128:(q + 1) * 128], w0[:, :, (2 + q) * 1024 + c * 128:(2 + q) * 1024 + (c + 1) * 128],
                [], [("wB", s, q)] + [("wbf", 1, n) for n in range(8)], "wB%d_%d" % (s, q))
        for j in range(NT):
            b = j % 2
            for q in range(4):
                for k in range(8):
                    mm(pb[q][:], wB[:, k, q * 128:(q + 1) * 128], hT[:, k, tsl(j)], k == 0, k == 7,
                       [("wB", s, q)] + khT(k, j * T, j * T + T), [("pb", q)])
            xbs, xbk = bslot(2, b)
            act(xbs, pb[0][:], AF.Copy, [("pb", 0)], xbk)
            tt("dve", Pb[:, 1 + j * T:1 + (j + 1) * T], pb[2][:], xbs, ALU.mult, [("pb", 2)] + xbk, kb(0, 1 + j * T, 1 + (j + 1) * T))
            act(GB[:, tsl(j)], pb[1][:], AF.Copy, [("pb", 1)], kb(1, j * T, j * T + T))
            szk = kh(0, j * T, j * T + T)
            act(SZB[:, tsl(j)], pb[3][:], AF.Tanh, [("pb", 3)], szk, scale=0.5)
            stt("dve", SZB[:, tsl(j)], SZB[:, tsl(j)], 1.0, pb[3][:], ALU.add, ALU.mult, szk + [("pb", 3)], szk)
            for jj in ([j - 1] if j >= 1 else []) + ([j] if j == NT - 1 else []):
                cvt, cvk = bslot(3, jj % 2)
                lo = jj * T
                ts("dve", cvt, Pb[:, lo:lo + T], pv[:, 13, c:c + 1], None, ALU.mult, None, kb(0, lo, lo + T) + ["pv"], cvk)
                for kk in range(1, 3):
                    stt("dve", cvt, Pb[:, lo + kk:lo + kk + T], pv[:, 13 + kk, c:c + 1], cvt, ALU.mult, ALU.add,
                        kb(0, lo + kk, lo + kk + T) + cvk + ["pv"], cvk)
                stt("dve", cvt, cvt, 0.5, GB[:, tsl(jj)], ALU.mult, ALU.mult, cvk + kb(1, lo, lo + T), cvk)
                tt("pool", YB[:, tsl(jj)], cvt, SZB[:, tsl(jj)], ALU.mult, cvk + kh(0, lo, lo + T), kh(1, lo, lo + T))
        dma("sp", mixed0[1024 + c * 128:1024 + (c + 1) * 128, :], YB[:, 0:SEQ], kh(1, 0, SEQ), [("mixed0", 8 + c, j) for j in range(NT)], "yb")

    P.barrier()
    m0v = mixed0.rearrange("(k p) t -> p k t", p=128)
    mtb = [bigh[i][:, 0:8 * T].rearrange("p (k t) -> p k t", t=T) for i in range(2)]

    def mt_get0(j):
        b = j % 2
        lo = mtb[b]
        hi = hT[:, :, b * T:(b + 1) * T]
        rk = [("mixed0", c, j) for c in range(16)]
        dma("sp", lo, m0v[:, 0:8, tsl(j)], rk, [("mtlo", b)], "mt%d" % b)
        dma("sp", hi, m0v[:, 8:16, tsl(j)], rk, [("mthi", b)], "mth%d" % b)
        return (lambda k: lo[:, k, :] if k < 8 else hi[:, k - 8, :]), [("mtlo", b), ("mthi", b)]

    out_phase(mt_get0, 16, w_out0, 1, xT, x1T if layers > 1 else outT, "x0", hT[:, :, 2 * T:3 * T], ["sqh"])
    if layers == 1:
        P.emit(nc, es)
        es.close()
        return nc, P

    P.barrier()
    norm_phase(x1T, 16, "x0o")
    w1 = w_in1.rearrange("(k p) n -> p k n", p=128)
    LRs = nc.dram_tensor("LRs", [2, 16, SEQ], BF16, kind="Internal").ap()
    dma("pool", lrw[:], w1[:, :, 3072:3104], [], ["lrw"], "c8")
    dma("pool", wgt[:], wg.rearrange("d r k -> r d k"), [], ["wgt"], "c9")

    for q in range(2):
        dma("pool", wb[0][:, :, q * 512:(q + 1) * 512], w1[:, :, q * 512:(q + 1) * 512], [], [("wb", 0, q)], "wb0_%d" % q)
    for oc in range(8):
        dst = qTs if oc < 4 else kTs
        for j in range(NT):
            b = j % 2
            bank = (oc * NT + j) % 4
            for k in range(8):
                mm(pb[bank][:], wb[0][:, k, oc * 128:(oc + 1) * 128], hT[:, k, tsl(j)], k == 0, k == 7,
                   [("wb", 0, oc // 4)] + khT(k, j * T, j * T + T), [("pb", bank)])
            st, stk = bslot(bank, b)
            act(st, pb[bank][:], AF.Copy, [("pb", bank)], stk)
            dma("sp", dst[(oc % 4) * 128:(oc % 4 + 1) * 128, tsl(j)], st, stk, [("qk", oc, j)], "st%d_%d" % (bank, b))
    for q in range(2):
        dma("pool", wb[1][:, :, q * 512:(q + 1) * 512], w1[:, :, 1024 + q * 512:1024 + (q + 1) * 512], [], [("wb", 1, q)], "wb1_%d" % q)
    Vsv = Vs.rearrange("(m p) e -> p m e", p=128)
    Vt = [bigh[i][:, 0:4096].rearrange("p (m e) -> p m e", e=1024) for i in range(2)]
    for j in range(NT):
        b = j % 2
        for m in range(4):
            for half in range(2):
                bank = (m * 2 + half) % 4
                for k in range(8):
                    mm(pb[bank][:], hT[:, k, j * T + m * 128:j * T + (m + 1) * 128], wb[1][:, k, half * 512:(half + 1) * 512],
                       k == 0, k == 7, [("wb", 1, half)] + khT(k, j * T, j * T + T), [("pb", bank)])
                act(Vt[b][:, m, half * 512:(half + 1) * 512], pb[bank][:], AF.Copy, [("pb", bank)], [("Vt", b, m, half)])
        dma("sp", Vsv[:, j * 4:(j + 1) * 4, :], Vt[b], [("Vt", b, m, hf) for m in range(4) for hf in range(2)],
            [("Vs", j)], "vt%d" % b)
    for q in range(2):
        dma("pool", wb[0][:, :, q * 512:(q + 1) * 512], w1[:, :, 2048 + q * 512:2048 + (q + 1) * 512], [], [("wb", 0, q)], "wb0_%d" % q)
    for oc in range(8):
        for j in range(NT):
            b = j % 2
            bank = (oc * NT + j) % 4
            for k in range(8):
                mm(pb[bank][:], wb[0][:, k, oc * 128:(oc + 1) * 128], hT[:, k, tsl(j)], k == 0, k == 7,
                   [("wb", 0, oc // 4)] + khT(k, j * T, j * T + T), [("pb", bank)])
            stg = tmph[bank % 2][b]
            sk = [("th%d" % (bank % 2), b)]
            act(stg[:], pb[bank][:], AF.Tanh, [("pb", bank)], sk, scale=0.5)
            stt("dve", stg[:], stg[:], 1.0, pb[bank][:], ALU.add, ALU.mult, sk + [("pb", bank)], sk)
            dma("sp", SRs[oc * 128:(oc + 1) * 128, tsl(j)], stg[:], sk, [("SRs", oc, j)], "sr%d_%d" % (bank % 2, b))
    for d in range(2):
        for j in range(NT):
            b = j % 2
            bank = 4 + b
            for k in range(8):
                mm(pb[bank][0:16, :], lrw[:, k, d * 16:(d + 1) * 16], hT[:, k, tsl(j)], k == 0, k == 7,
                   ["lrw"] + khT(k, j * T, j * T + T), [("pb", bank)])
            act(tmph[2][b][0:16, :], pb[bank][0:16, :], AF.Copy, [("pb", bank)], [("th2", b)])
            dma("sp", LRs[d, :, tsl(j)], tmph[2][b][0:16, :], [("th2", b)], [("LRs", d, j)], "lrs%d" % b)

    P.barrier()
    SC = 128.0 ** -0.5
    qh, kh_, O = big[0], big[1], [big[2], big[3]]
    HC = NCH // 2
    Vhh = [bigh[i][:, 0:HC * 256].rearrange("p (m e) -> p m e", e=256) for i in range(2)]

    def Vn(n):
        return Vhh[n // HC][:, n % HC, :]

    sq1 = bigh[1][:, 0:8 * T].rearrange("p (k t) -> p k t", t=T)
    for h in range(4):
        dma("sp", qh[:, 0:SEQ], qTs[h * 128:(h + 1) * 128, :], [("qk", h, j) for j in range(NT)], ["qh"], "qh")
        dma("sp", kh_[:, 0:SEQ], kTs[h * 128:(h + 1) * 128, :], [("qk", 4 + h, j) for j in range(NT)], ["kh"], "kh")
        for i in range(2):
            dma("sp", Vhh[i], Vsv[:, i * HC:(i + 1) * HC, h * 256:(h + 1) * 256], [("Vs", j) for j in range(NT)], ["Vh"], "vh%d" % i)
        def gla_dir(d, h=h):
            first = True
            for sidx in range(NT):
                j = sidx if d == 0 else NT - 1 - sidx
                (Lt, k0), (Lc, k1), (E1, k2), (E2, k3) = [wslot(0, i * 2 + d) for i in range(4)]
                QI, KI, KST = tmph[0][d], tmph[1][d], tmph[2][d]
                bz, bo0, bo1, bk = 0 + d, 2 + d, 4 + d, 6 + d
                pz = pbx[bz]
                dma("sp", lrt[d][:], LRs[d, :, tsl(j)], [("LRs", d, j)], [("lrt", d)], "lrt%d" % d)
                mm(pz, wgt[0:16, d, h * 128:(h + 1) * 128], lrt[d][:], True, True, ["wgt", ("lrt", d)], [("pb", bz)])
                yield
                act(Lt, pz, AF.Exp, [("pb", bz), "negb"], k0, bias=negb[:, d, h:h + 1], scale=-1.0)
                act(Lt, Lt, AF.Ln, k0, k0, bias=1.0)
                yield
                scan(Lc, m01[:, :], Lt, 0.0, d == 1, k0 + ["m01"], k1)
                yield
                tot = Lc[:, T - 1:T] if d == 0 else Lc[:, 0:1]
                act(E1, Lc, AF.Exp, k1, k2, scale=-1.0 / 16)
                act(E2, Lc, AF.Exp, k1, k3, scale=1.0 / 16)
                act(DEC[d][:, j, :], tot, AF.Exp, k1, [("DEC", d, j)], scale=-1.0 / 16)
                yield
                stt("dve", QI[:], qh[:, tsl(j)], SC, E1, ALU.mult, ALU.mult, ["qh"] + k2, [("th0", d)])
                tt("pool", KI[:], kh_[:, tsl(j)], E2, ALU.mult, ["kh"] + k3, [("th1", d)])
                yield
                ts("dve", E1, Lc, tot, None, ALU.subtract, None, k1 + k2, k2)
                yield
                act(E1, E1, AF.Exp, k2, k2, scale=1.0 / 16)
                yield
                tt("pool", KST[:], kh_[:, tsl(j)], E1, ALU.mult, ["kh"] + k2, [("th2", d)])
                yield
                kt_ps = (pb6h if d == 0 else pbt)[:, 0:T]
                kv_ps = pbx[bk][:, 256:512]
                jcs = list(range(4)) if d == 0 else list(range(3, -1, -1))
                SM = [wb[1][:, 2 + (d * 4 + jc) // 2, ((d * 4 + jc) % 2) * T:((d * 4 + jc) % 2 + 1) * T] for jc in range(4)]
                for jc in range(4):
                    P.add("pe", lambda e, o=kt_ps[:, jc * 128:(jc + 1) * 128], i=KST[:, jc * 128:(jc + 1) * 128]:
                          e.transpose(out=o, in_=i, identity=ident[:]), r=[("th2", d), "ident"], w=[("pb", bk)])
                yield
                act(KTt[d][:].rearrange("p c t -> p (c t)"), kt_ps, AF.Copy, [("pb", bk)], [("KT", d)])
                yield
                for jc in jcs:
                    if d == 0:
                        isl, msl = slice(jc * 128, T), tri[:, 0, 0:(4 - jc) * 128]
                    else:
                        isl, msl = slice(0, (jc + 1) * 128), tri[:, 1, (3 - jc) * 128:T]
                    ncol = isl.stop - isl.start
                    mm(pz[:, 0:ncol], KI[:, jc * 128:(jc + 1) * 128], QI[:, isl], True, True, [("th1", d), ("th0", d)], [("pb", bz)])
                    yield
                    tt("dve", SM[jc][:, 0:ncol], pz[:, 0:ncol], msl, ALU.mult, [("pb", bz), "tri"], [("SM", d, jc)])
                    yield
                for jc in range(4):
                    mm(kv_ps, KTt[d][:, jc, :], Vn(j * 4 + jc), jc == 0, jc == 3, [("KT", d), "Vh"], [("pb", bk)])
                yield
                for ec, bo in ((0, bo0), (1, bo1)):
                    for idx, jc in enumerate(jcs):
                        isl = slice(jc * 128, T) if d == 0 else slice(0, (jc + 1) * 128)
                        ncol = isl.stop - isl.start
                        mm(pbx[bo][:, isl], Vn(j * 4 + jc)[:, ec * 128:(ec + 1) * 128], SM[jc][:, 0:ncol], idx == 0, first and idx == 3,
                           ["Vh", ("SM", d, jc)], [("pb", bo)])
                    if not first:
                        mm(pbx[bo], Sbf[d][:, ec * 128:(ec + 1) * 128], QI[:], False, True, [("Sb", d), ("th0", d)], [("pb", bo)])
                yield
                if first:
                    P.add("dve", lambda e, o=Sst[d][:], i=kv_ps: e.tensor_copy(out=o, in_=i), r=[("pb", bk)], w=[("S", d)])
                else:
                    stt("dve", Sst[d][:], Sst[d][:], DEC[d][:, j, :], kv_ps, ALU.mult, ALU.add,
                        [("S", d), ("DEC", d, j), ("pb", bk)], [("S", d)])
                yield
                act(Sbf[d][:], Sst[d][:], AF.Copy, [("S", d)], [("Sb", d)])
                firstO = (d == 0) == (2 * j <= NT - 1)
                okj = [[("O", ec, n) for n in range(j * 4, j * 4 + 4)] for ec in range(2)]
                for ec, bo in ((0, bo0), (1, bo1)):
                    if firstO:
                        act(O[ec][:, tsl(j)], pbx[bo], AF.Copy, [("pb", bo)], okj[ec])
                    else:
                        tt("dve", O[ec][:, tsl(j)], O[ec][:, tsl(j)], pbx[bo], ALU.add, [("pb", bo)] + okj[ec], okj[ec])
                first = False
                yield

        g0, g1 = gla_dir(0), gla_dir(1)
        for _ in range(11):
            next(g0)
        run_rr([g0, g1])
        for j in range(NT):
            b = j % 2
            okeys = [[("O", ec, n) for n in range(j * 4, j * 4 + 4)] for ec in range(2)]
            for ec in range(2):
                dma("sp", tmph[ec][b][:], SRs[(2 * h + ec) * 128:(2 * h + ec + 1) * 128, tsl(j)],
                    [("SRs", 2 * h + ec, j)], [("th%d" % ec, b)], "srl%d_%d" % (ec, b))
                act(sqs[:, ec, :], O[ec][:, tsl(j)], AF.Square, okeys[ec], [("sq1", ec)])
            for ec in range(2):
                mm(pb[6][:], ones[:], sqs[:, ec, :], ec == 0, ec == 1, ["ones", ("sq1", ec)], [("pb", 6)])
            rstd_from(pb[6][:], b, 256, mul=0.5)
            for ec in range(2):
                t4, t4k = wslot(1, ec)
                stt("dve", t4, O[ec][:, tsl(j)], pngt[:, ec:ec + 1], rs[b][:], ALU.mult, ALU.mult,
                    okeys[ec] + [("rs", b), "pngt"], t4k)
                tt("pool", hT[:, 2 * h + ec, tsl(j)], t4, tmph[ec][b][:], ALU.mult,
                   t4k + [("th%d" % ec, b)], khT(2 * h + ec, j * T, j * T + T))

    P.barrier()

    def mt_get1(j):
        return (lambda k: hT[:, k, tsl(j)]), [("hT", k, j) for k in range(8)]

    out_phase(mt_get1, 8, w_out1, 17, x1T, outT, "x0o", sq1, ["sq1all"])
    P.emit(nc, es)
    es.close()
    return nc, P


def host_inputs(x_b, inp):
    f = lambda a: np.ascontiguousarray(np.asarray(a, dtype=np.float32))
    vecs = [inp["even_norm_pre"][0], inp["even_norm_post"][0]] + [inp["rg_conv_w"][0, i] for i in range(4)] + \
           [inp["rg_conv_b"][0]] + [inp["rg_gate_b"][0, d, g].reshape(-1) for d in range(2) for g in range(2)] + \
           [inp["rg_lambda"][0, d] for d in range(2)] + [inp["sc_conv_w"][0, i] for i in range(3)] + \
           [inp["odd_norm_pre"][0], inp["odd_norm_post"][0]]
    pvec = f(np.stack([np.asarray(v) for v in vecs], 0).reshape(NV, 8, 128).transpose(2, 0, 1))
    pg = f(np.asarray(inp["gla_b_gate"][0]).reshape(2, 4, 128).transpose(2, 0, 1))
    png = f(np.asarray(inp["gla_norm_g"][0]).reshape(2, 128).transpose(1, 0))
    cm01 = np.ones((128, T), np.float32)
    jj, ii = np.meshgrid(np.arange(128), np.arange(128), indexing="ij")
    mf = np.ones((128, T), np.float32)
    mf[:, 0:128] = (jj <= ii)
    mb = np.ones((128, T), np.float32)
    mb[:, T - 128:T] = (jj >= ii)
    ctri = f(np.stack([mf, mb], 1))
    return {
        "xT": f(np.asarray(x_b).T), "w_in0": f(inp["even_w_in"][0]), "gate_w": f(inp["rg_gate_w"][0]),
        "w_out0": f(inp["even_w_out"][0]), "w_in1": f(inp["odd_w_in"][0]), "wg": f(inp["gla_w_gate_lr"][0]),
        "w_out1": f(inp["odd_w_out"][0]), "pvec": pvec, "pg": pg, "png": png, "cm01": cm01, "ctri": ctri,
        "cones": np.ones((128, 128), np.float32), "cident": np.eye(128, dtype=np.float32),
    }


def kernel(**inputs):
    x = np.asarray(inputs["x"])
    B, SEQ, _ = x.shape
    nc, _ = build_nc(SEQ)
    shared = None
    in_maps = []
    for b in range(B):
        m = host_inputs(x[b], inputs)
        if shared is None:
            shared = m
        else:
            for k in m:
                if k != "xT":
                    m[k] = shared[k]
        in_maps.append(m)
    res = run_bass_kernel_spmd(nc, in_maps, core_ids=list(range(B)))
    out = np.stack([np.asarray(r["outT"]).T for r in res.results], 0)
    return np.ascontiguousarray(out.astype(np.float32))
```

```python
import numpy as np
from contextlib import ExitStack
import concourse.bass as bass
import concourse.mybir as mybir
from concourse.bass_utils import run_bass_kernel_spmd
from concourse.ap import AP

F32 = mybir.dt.float32
BF16 = mybir.dt.bfloat16
AF = mybir.ActivationFunctionType
ALU = mybir.AluOpType

D = 1024
T = 512
EPS = 1e-6
NV = 18
LAYERS = 2
SBUF_LEFT = []


def rev(ap):
    pairs = [list(p) for p in ap.ap]
    step, cnt = pairs[-1]
    pairs[-1] = [-step, cnt]
    return AP(ap.tensor, ap.offset + step * (cnt - 1), pairs)


class _Op:
    __slots__ = ("eng", "fn", "deps", "dma_key", "signal", "val", "idx", "semkey")


class Prog:
    def __init__(self):
        self.ops = []
        self.lastw = {}
        self.rd_eng = {}
        self.rd_dma = {}
        self.last_eng = {}
        self.last_dma = {}
        self.fence = []
        self.passed = set()

    def barrier(self):
        self.fence = list(self.last_eng.values()) + list(self.last_dma.values())
        self.passed = set()

    def add(self, eng, fn, r=(), w=(), dma_key=None):
        op = _Op()
        op.eng, op.fn, op.dma_key = eng, fn, dma_key
        op.signal = dma_key is not None
        op.val = 0
        op.semkey = None
        op.idx = len(self.ops)
        deps = {}
        if self.fence and eng not in self.passed:
            for p in self.fence:
                deps[p.idx] = p
            self.passed.add(eng)
        for k in r:
            p = self.lastw.get(k)
            if p is not None:
                deps[p.idx] = p
        for k in w:
            p = self.lastw.get(k)
            if p is not None:
                deps[p.idx] = p
            for q in self.rd_eng.get(k, {}).values():
                deps[q.idx] = q
            for q in self.rd_dma.get(k, ()):
                deps[q.idx] = q
        op.deps = [p for p in deps.values() if (p.dma_key is not None) or (p.eng != eng) or (eng != "pe")]
        for k in r:
            if dma_key is not None:
                self.rd_dma.setdefault(k, []).append(op)
            else:
                self.rd_eng.setdefault(k, {})[eng] = op
        for k in w:
            self.lastw[k] = op
            self.rd_eng[k] = {}
            self.rd_dma[k] = []
        if dma_key is not None:
            self.last_dma[dma_key] = op
        else:
            self.last_eng[eng] = op
        self.ops.append(op)
        return op

    def emit(self, nc, es):
        for op in self.ops:
            for p in op.deps:
                p.signal = True
        cnt = {}
        for op in self.ops:
            if op.signal:
                key = ("dma", op.dma_key) if op.dma_key is not None else ("eng", op.eng)
                cnt[key] = cnt.get(key, 0) + (16 if op.dma_key is not None else 1)
                op.semkey, op.val = key, cnt[key]
        sems = {}
        for i, key in enumerate(cnt):
            sems[key] = es.enter_context(nc.semaphore("s%d" % i))
        self.nsem = len(sems)
        block = es.enter_context(nc.Block())
        ops = self.ops

        def run(engname, final=False):
            def body(e):
                waited = {}
                for op in ops:
                    if op.eng != engname:
                        continue
                    for p in op.deps:
                        if waited.get(p.semkey, 0) < p.val:
                            e.wait_ge(sems[p.semkey], p.val)
                            waited[p.semkey] = p.val
                    ins = op.fn(e)
                    if op.signal:
                        ins.then_inc(sems[op.semkey], 16 if op.dma_key is not None else 1)
                if final:
                    for key, v in cnt.items():
                        if waited.get(key, 0) < v:
                            e.wait_ge(sems[key], v)
            return body

        block.sync(run("sp", final=True))
        block.tensor(run("pe"))
        block.scalar(run("act"))
        block.vector(run("dve"))
        block.gpsimd(run("pool"))


def build_nc(SEQ, layers=LAYERS):
    NT = SEQ // T
    NCH = SEQ // 128
    BW = max(SEQ, 8 * T)
    nc = bass.Bass("TRN2", target_bir_lowering=False)
    din = lambda n, s, dt=F32: nc.dram_tensor(n, s, dt, kind="ExternalInput").ap()
    xT = din("xT", [D, SEQ])
    w_in0 = din("w_in0", [D, 6144])
    gate_w = din("gate_w", [2, 2, 8, 128, 128])
    w_out0 = din("w_out0", [2048, D])
    w_in1 = din("w_in1", [D, 3104])
    wg = din("wg", [2, 16, 512])
    w_out1 = din("w_out1", [D, D])
    pvec = din("pvec", [128, NV, 8])
    pg = din("pg", [128, 2, 4])
    png = din("png", [128, 2])
    cm01 = din("cm01", [128, T])
    ctri = din("ctri", [128, 2, T])
    cones = din("cones", [128, 128])
    cident = din("cident", [128, 128])
    outT = nc.dram_tensor("outT", [D, SEQ], F32, kind="ExternalOutput").ap()
    mixed0 = nc.dram_tensor("mixed0", [2048, SEQ], BF16, kind="Internal").ap()
    x1T = nc.dram_tensor("x1T", [D, SEQ], F32, kind="Internal").ap()
    qTs = nc.dram_tensor("qTs", [512, SEQ], F32, kind="Internal").ap()
    kTs = nc.dram_tensor("kTs", [512, SEQ], F32, kind="Internal").ap()
    Vs = nc.dram_tensor("Vs", [SEQ, D], BF16, kind="Internal").ap()
    SRs = nc.dram_tensor("SRs", [D, SEQ], BF16, kind="Internal").ap()

    es = ExitStack()
    P = Prog()
    sb = lambda n, s, dt=F32: es.enter_context(nc.sbuf_tensor(n, s, dt))
    ps = lambda n, s, dt=F32: es.enter_context(nc.psum_tensor(n, s, dt))

    hT = sb("hT", [128, 8, SEQ], BF16)
    pv = sb("pv", [128, NV, 8])
    pgt = sb("pgt", [128, 2, 4])
    pngt = sb("pngt", [128, 2])
    m01 = sb("m01", [128, T], BF16)
    tri = sb("tri", [128, 2, T])
    ones = sb("ones", [128, 128], BF16)
    ident = sb("ident", [128, 128], BF16)
    der = sb("der", [128, 12, 8])
    negb = sb("negb", [128, 2, 4])
    gwh = [sb("gwh%d" % i, [128, 4, 128], BF16) for i in range(2)]
    lrw = sb("lrw", [128, 8, 32], BF16)
    wgt = sb("wgt", [16, 2, 512], BF16)
    lrt = [sb("lrt%d" % d, [16, T], BF16) for d in range(2)]
    rs = [sb("rs%d" % i, [128, T]) for i in range(2)]
    wb = [sb("wb%d" % i, [128, 8, 1024], BF16) for i in range(2)]
    wbf = [wb[i][:].rearrange("p k n -> p (k n)").bitcast(F32) for i in range(2)]
    big = [sb("big%d" % i, [128, BW + 4]) for i in range(4)]
    bigh = [sb("bigh%d" % i, [128, BW], BF16) for i in range(2)]
    tmph = [[sb("tmph%d_%d" % (i, j), [128, T], BF16) for j in range(2)] for i in range(3)]
    zero1 = sb("zero1", [128, 1])
    Sst = [sb("Sst%d" % d, [128, 256]) for d in range(2)]
    Sbf = [sb("Sbf%d" % d, [128, 256], BF16) for d in range(2)]
    DEC = [sb("DEC%d" % d, [128, NT, 1]) for d in range(2)]
    KTt = [sb("KTt%d" % d, [128, 4, 128], BF16) for d in range(2)]
    xt = [big[2 + i][:, 0:8 * T].rearrange("p (k t) -> p k t", t=T) for i in range(2)]
    sqs = sb("sqs", [128, 2, T], BF16)

    SBUF_LEFT.append(nc.sbuf_bytes_remaining)
    pb = [ps("pb%d" % i, [128, T]) for i in range(7)]
    pbt = ps("pbt", [128, 1024], BF16)
    pbx = [pb[i][:] for i in range(7)] + [pbt[:].bitcast(F32)]
    pb6h = pb[6][:].bitcast(BF16)

    def kb(i, lo, hi):
        return [("big", i, g) for g in range(lo // T, (hi - 1) // T + 1)]

    def kh(i, lo, hi):
        return [("bigh", i, g) for g in range(lo // T, (hi - 1) // T + 1)]

    def khT(k, lo, hi):
        return [("hT", k, g) for g in range(lo // T, (hi - 1) // T + 1)]

    def bslot(i, n):
        return big[i][:, n * T:(n + 1) * T], [("big", i, n)]

    def wslot(i, n):
        return wbf[i][:, n * T:(n + 1) * T], [("wbf", i, n)]

    def bcast_last(ap, n):
        pairs = [list(p) for p in ap.ap]
        assert pairs[-1][1] == 1
        pairs[-1] = [0, n]
        return AP(ap.tensor, ap.offset, pairs)

    def dma(eng, out, in_, r, w, key):
        P.add(eng, lambda e: e.dma_start(out=out, in_=in_), r=r, w=w, dma_key=key)

    def act(out, in_, func, r, w, bias=0.0, scale=1.0):
        P.add("act", lambda e: e.activation(out=out, in_=in_, func=func, bias=bias, scale=scale), r=r, w=w)

    def tt(eng, out, in0, in1, op, r, w):
        P.add(eng, lambda e: e.tensor_tensor(out=out, in0=in0, in1=in1, op=op), r=r, w=w)

    def ts(eng, out, in0, s1, s2, op0, op1, r, w):
        if op1 is None:
            P.add(eng, lambda e: e.tensor_scalar(out=out, in0=in0, scalar1=s1, scalar2=None, op0=op0), r=r, w=w)
        else:
            P.add(eng, lambda e: e.tensor_scalar(out=out, in0=in0, scalar1=s1, scalar2=s2, op0=op0, op1=op1), r=r, w=w)

    def stt(eng, out, in0, s, in1, op0, op1, r, w):
        P.add(eng, lambda e: e.scalar_tensor_tensor(out=out, in0=in0, scalar=s, in1=in1, op0=op0, op1=op1), r=r, w=w)

    def mm(out, lhsT, rhs, start, stop, r, w):
        P.add("pe", lambda e: e.matmul(out, lhsT=lhsT, rhs=rhs, start=start, stop=stop), r=r, w=w)

    def scan(out, d0, d1, init, reverse, r, w):
        if reverse:
            P.add("dve", lambda e: e.tensor_tensor_scan(out=rev(out), data0=rev(d0), data1=rev(d1), initial=init,
                                                        op0=ALU.mult, op1=ALU.add), r=r, w=w)
        else:
            P.add("dve", lambda e: e.tensor_tensor_scan(out=out, data0=d0, data1=d1, initial=init,
                                                        op0=ALU.mult, op1=ALU.add), r=r, w=w)

    def tsl(j):
        return slice(j * T, (j + 1) * T)

    xtk = [kb(2 + i, 0, 8 * T) for i in range(2)]

    dma("sp", pv[:], pvec[:, :, :], [], ["pv"], "c0")
    dma("sp", pgt[:], pg[:, :, :], [], ["pgt"], "c1")
    dma("sp", pngt[:], png[:, :], [], ["pngt"], "c2")
    dma("pool", m01[:], cm01[:, :], [], ["m01"], "c3")
    dma("sp", tri[:], ctri[:, :, :], [], ["tri"], "c4")
    dma("pool", ones[:], cones[:, :], [], ["ones"], "c5")
    dma("pool", ident[:], cident[:, :], [], ["ident"], "c6")
    P.add("dve", lambda e: e.memset(zero1[:], 0.0), w=["zero1"])
    ts("dve", der[:, 0:4, :], pv[:, 7:11, :], 0.5, None, ALU.mult, None, ["pv"], ["der"])
    act(der[:, 8:10, :], pv[:, 11:13, :], AF.Exp, ["pv"], ["der8"], scale=-1.0)
    act(der[:, 10:12, :], der[:, 8:10, :], AF.Ln, ["der8"], ["der10"], bias=1.0)
    ts("dve", der[:, 4:6, :], der[:, 10:12, :], -4.0, None, ALU.mult, None, ["der10"], ["der"])
    ts("dve", der[:, 6:8, :], der[:, 10:12, :], -8.0, None, ALU.mult, None, ["der10"], ["der"])
    ts("dve", negb[:], pgt[:], -1.0, None, ALU.mult, None, ["pgt"], ["negb"])

    def rstd_from(psum_ap, b, n, mul=1.0):
        act(rs[b][:], psum_ap, AF.Ln, [("pb", 6)], [("rs", b)], bias=EPS, scale=1.0 / n)
        act(rs[b][:], rs[b][:], AF.Exp, [("rs", b)], [("rs", b)], scale=-0.5, bias=float(np.log(mul)))

    def norm_phase(src, gidx, tag):
        srcv = src.rearrange("(k p) t -> p k t", p=128)
        sq = bigh[1][:, 0:8 * T].rearrange("p (k t) -> p k t", t=T)
        sqk = kh(1, 0, 8 * T)
        for j in range(NT):
            b = j % 2
            dma("sp", xt[b], srcv[:, :, tsl(j)], [("src", tag, j)], xtk[b], "xt%d" % b)
            act(sq, xt[b], AF.Square, xtk[b], sqk)
            for k in range(8):
                mm(pb[6][:], ones[:], sq[:, k, :], k == 0, k == 7, ["ones"] + sqk, [("pb", 6)])
            rstd_from(pb[6][:], b, D)
            for k in range(8):
                stt("dve", hT[:, k, tsl(j)], xt[b][:, k, :], pv[:, gidx, k:k + 1], rs[b][:], ALU.mult, ALU.mult,
                    xtk[b] + [("rs", b), "pv"], khT(k, j * T, (j + 1) * T))

    def out_phase(mt_get, nk, wsrc, gidx, resid, dst, tag, sq, sqk):
        wo = wsrc.rearrange("(k p) n -> p k n", p=128)
        for k0 in range(0, nk, 4):
            s, q = k0 // 8, (k0 % 8) // 4
            dma("pool", wb[s][:, q * 4:q * 4 + 4, :], wo[:, k0:k0 + 4, :], [], [("wb", s, q)], "wb%d_%d" % (s, q))
        resv = resid.rearrange("(k p) t -> p k t", p=128)
        dstv = dst.rearrange("(k p) t -> p k t", p=128)
        ysb = [big[i][:, 0:8 * T].rearrange("p (k t) -> p k t", t=T) for i in range(2)]
        for j in range(NT):
            b = j % 2
            ys = ysb[b]
            mget, mkeys = mt_get(j)
            dma("sp", xt[b], resv[:, :, tsl(j)], [("src", tag, j)], xtk[b], "xt%d" % b)
            for o in range(8):
                bank = o % 4
                for k in range(nk):
                    mm(pb[bank][:], wb[k // 8][:, k % 8, o * 128:(o + 1) * 128], mget(k), k == 0, k == nk - 1,
                       [("wb", k // 8, (k % 8) // 4)] + mkeys, [("pb", bank)])
                act(ys[:, o, :], pb[bank][:], AF.Copy, [("pb", bank)], [("big", b, o)])
                act(sq[:, o, :], pb[bank][:], AF.Square, [("pb", bank)], sqk)
            for o in range(8):
                mm(pb[6][:], ones[:], sq[:, o, :], o == 0, o == 7, ["ones"] + sqk, [("pb", 6)])
            rstd_from(pb[6][:], b, D)
            for o in range(8):
                stt("dve", ys[:, o, :], ys[:, o, :], pv[:, gidx, o:o + 1], rs[b][:], ALU.mult, ALU.mult,
                    [("big", b, o), ("rs", b), "pv"], [("big", b, o)])
                tt("pool", ys[:, o, :], ys[:, o, :], xt[b][:, o, :], ALU.add, [("big", b, o)] + xtk[b], [("big", b, o)])
            dma("sp", dstv[:, :, tsl(j)], ys, [("big", b, o) for o in range(8)], [("src", tag + "o", j)], "xo%d" % b)

    norm_phase(xT, 0, "x0")
    w0 = w_in0.rearrange("(k p) n -> p k n", p=128)
    XA, UA, HB, CVb = big[0], big[1], big[2], big[3]
    UAb, SZ = bigh[0], bigh[1]
    P.add("pool", lambda e: e.memset(XA[:, 0:2], 0.0), w=kb(0, 0, 2))
    P.add("pool", lambda e: e.memset(XA[:, SEQ + 2:SEQ + 4], 0.0), w=kb(0, SEQ + 2, SEQ + 4))

    def rgA_front(c):
        s = c % 2
        wA = wb[0][:, :, s * 256:(s + 1) * 256]
        dma("pool", wA[:, :, 0:128], w0[:, :, c * 128:(c + 1) * 128], [], [("wA", s, 0)], "wb%d_0" % s)
        dma("pool", wA[:, :, 128:256], w0[:, :, 1024 + c * 128:1024 + (c + 1) * 128], [], [("wA", s, 1)], "wb%d_1" % s)
        dma("pool", gwh[s][:], gate_w[:, :, c].rearrange("d g i j -> i (d g) j"), [], [("gwh", s)], "gw%d" % s)
        yield
        for j in range(NT):
            bank = 6 + (j % 2)
            for k in range(8):
                mm(pbx[bank], wA[:, k, 0:128], hT[:, k, tsl(j)], k == 0, k == 7, [("wA", s, 0)] + khT(k, j * T, j * T + T), [("pb", bank)])
            yield
            act(XA[:, 2 + j * T:2 + (j + 1) * T], pbx[bank], AF.Copy, [("pb", bank)], kb(0, 2 + j * T, 2 + (j + 1) * T))
            yield

    def rgA_conv(c):
        allXA = kb(0, 0, SEQ + 4)
        allUA = kb(1, 0, SEQ)
        ts("dve", UA[:, 0:SEQ], XA[:, 0:SEQ], pv[:, 2, c:c + 1], pv[:, 6, c:c + 1], ALU.mult, ALU.add, allXA + ["pv"], allUA)
        for kk in range(1, 4):
            stt("dve", UA[:, 0:SEQ], XA[:, kk:kk + SEQ], pv[:, 2 + kk, c:c + 1], UA[:, 0:SEQ], ALU.mult, ALU.add, allXA + allUA + ["pv"], allUA)
        hlf = SEQ // 2
        act(UAb[:, 0:hlf], UA[:, 0:hlf], AF.Copy, kb(1, 0, hlf), kh(0, 0, hlf))
        P.add("pool", lambda e: e.tensor_copy(out=UAb[:, hlf:SEQ], in_=UA[:, hlf:SEQ]), r=kb(1, hlf, SEQ), w=kh(0, hlf, SEQ))

    def rg_dir(d, c):
        s = c % 2
        cs = slice(c * 128, (c + 1) * 128)
        wA = wb[0][:, :, s * 256:(s + 1) * 256]

        def slots(b):
            if d == 1:
                return [bslot(3, i * 2 + b) for i in range(4)]
            return [wslot(1, i * 2 + b) for i in range(4)]

        br, bi = (2, 3) if d == 1 else (0, 1)
        g_r = gwh[s][:, d * 2 + 0, :]
        g_i = gwh[s][:, d * 2 + 1, :]
        prev = None
        pend = []
        for t in range(NT):
            j = t if d == 0 else NT - 1 - t
            t_other = NT - 1 - j if d == 0 else j
            store = (t < t_other) or (t == t_other and d == 1)
            b = j % 2
            uk = kh(0, j * T, j * T + T)
            (THr, k0), (THi, k1), (A, k2), (Ht, k3) = slots(b)
            mm(pbx[br], g_r, UAb[:, tsl(j)], True, True, [("gwh", s)] + uk, [("pb", br)])
            act(THr, pbx[br], AF.Tanh, [("pb", br), "der"], k0, bias=der[:, d * 2 + 0, c:c + 1], scale=0.5)
            mm(pbx[bi], g_i, UAb[:, tsl(j)], True, True, [("gwh", s)] + uk, [("pb", bi)])
            act(THi, pbx[bi], AF.Tanh, [("pb", bi), "der"], k1, bias=der[:, d * 2 + 1, c:c + 1], scale=0.5)
            act(A, THr, AF.Exp, k0 + ["der"], k2, bias=der[:, 4 + d, c:c + 1], scale=der[:, 4 + d, c:c + 1])
            tt("dve", THr, A, A, ALU.mult, k2, k0)
            ts("pool", THr, THr, -1.0 / 16, 1.0 / 16, ALU.mult, ALU.add, k0, k0)
            stt("dve", THi, THi, 1.0, UA[:, tsl(j)], ALU.add, ALU.mult, k1 + kb(1, j * T, j * T + T), k1)
            pend.append((j, store))
            if t % 2 == 0 and t + 1 < NT:
                continue
            yield
            for (jj, _) in pend:
                (THr2, k02) = slots(jj % 2)[0]
                act(THr2, THr2, AF.Sqrt, k02, k02)
            yield
            for (jj, st) in pend:
                bb = jj % 2
                (THr2, k02), (THi2, k12), (A2, k22), (Ht2, k32) = slots(bb)
                hk2 = kb(2, jj * T, jj * T + T)
                tt("pool", THi2, THr2, THi2, ALU.mult, k02 + k12, k12)
                if prev is None:
                    init, rk = zero1[:, 0:1], ["zero1"]
                elif prev[1]:
                    col = prev[0] * T + (T - 1 if d == 0 else 0)
                    init, rk = HB[:, col:col + 1], kb(2, col, col + 1)
                else:
                    pslot, pk = slots(prev[0] % 2)[3]
                    init, rk = (pslot[:, T - 1:T] if d == 0 else pslot[:, 0:1]), pk
                if st:
                    scan(HB[:, tsl(jj)], A2, THi2, init, d == 1, k22 + k12 + rk, hk2)
                else:
                    zb = 4 + bb
                    for k in range(8):
                        mm(pbx[zb], wA[:, k, 128:256], hT[:, k, tsl(jj)], k == 0, k == 7,
                           [("wA", s, 1)] + khT(k, jj * T, jj * T + T), [("pb", zb)])
                    scan(Ht2, A2, THi2, init, d == 1, k22 + k12 + rk, k32)
                    act(THr2, pbx[zb], AF.Tanh, [("pb", zb)], k02, scale=0.5)
                    stt("dve", THr2, THr2, 1.0, pbx[zb], ALU.add, ALU.mult, k02 + [("pb", zb)], k02)
                    tt("dve", A2, Ht2, HB[:, tsl(jj)], ALU.add, k32 + hk2, k22)
                    ob, okk = tmph[d][bb], [("th%d" % d, bb)]
                    tt("pool", ob[:], A2, THr2, ALU.mult, k22 + k02, okk)
                    dma("sp", mixed0[cs, tsl(jj)], ob[:], okk, [("mixed0", c, jj)], "ya%d_%d" % (d, bb))
                prev = (jj, st)
            del pend[:]
            yield

    def run_rr(gens):
        live = list(gens)
        while live:
            for g in list(live):
                try:
                    next(g)
                except StopIteration:
                    live.remove(g)

    run_rr([rgA_front(0)])
    rgA_conv(0)
    for c in range(8):
        gens = [rg_dir(1, c), rg_dir(0, c)]
        if c + 1 < 8:
            gens.append(rgA_front(c + 1))
        run_rr(gens)
        if c + 1 < 8:
            rgA_conv(c + 1)

    Pb, GB = big[0], big[1]
    SZB, YB = bigh[0], bigh[1]
    P.add("pool", lambda e: e.memset(Pb[:, 0:1], 0.0), w=kb(0, 0, 1))
    P.add("pool", lambda e: e.memset(Pb[:, SEQ + 1:SEQ + 2], 0.0), w=kb(0, SEQ + 1, SEQ + 2))
    for c in range(8):
        s = c % 2
        wB = wb[1][:, :, s * 512:(s + 1) * 512]
        for q in range(4):
            dma("pool", wB[:, :, q * 128:(q + 1) * 128], w0[:, :, (2 + q) * 1024 + c * 128:(2 + q) * 1024 + (c + 1) * 128],
                [], [("wB", s, q)] + [("wbf", 1, n) for n in range(8)], "wB%d_%d" % (s, q))
        for j in range(NT):
            b = j % 2
            for q in range(4):
                for k in range(8):
                    mm(pb[q][:], wB[:, k, q * 128:(q + 1) * 128], hT[:, k, tsl(j)], k == 0, k == 7,
                       [("wB", s, q)] + khT(k, j * T, j * T + T), [("pb", q)])
            xbs, xbk = bslot(2, b)
            act(xbs, pb[0][:], AF.Copy, [("pb", 0)], xbk)
            tt("dve", Pb[:, 1 + j * T:1 + (j + 1) * T], pb[2][:], xbs, ALU.mult, [("pb", 2)] + xbk, kb(0, 1 + j * T, 1 + (j + 1) * T))
            act(GB[:, tsl(j)], pb[1][:], AF.Copy, [("pb", 1)], kb(1, j * T, j * T + T))
            szk = kh(0, j * T, j * T + T)
            act(SZB[:, tsl(j)], pb[3][:], AF.Tanh, [("pb", 3)], szk, scale=0.5)
            stt("dve", SZB[:, tsl(j)], SZB[:, tsl(j)], 1.0, pb[3][:], ALU.add, ALU.mult, szk + [("pb", 3)], szk)
            for jj in ([j - 1] if j >= 1 else []) + ([j] if j == NT - 1 else []):
                cvt, cvk = bslot(3, jj % 2)
                lo = jj * T
                ts("dve", cvt, Pb[:, lo:lo + T], pv[:, 13, c:c + 1], None, ALU.mult, None, kb(0, lo, lo + T) + ["pv"], cvk)
                for kk in range(1, 3):
                    stt("dve", cvt, Pb[:, lo + kk:lo + kk + T], pv[:, 13 + kk, c:c + 1], cvt, ALU.mult, ALU.add,
                        kb(0, lo + kk, lo + kk + T) + cvk + ["pv"], cvk)
                stt("dve", cvt, cvt, 0.5, GB[:, tsl(jj)], ALU.mult, ALU.mult, cvk + kb(1, lo, lo + T), cvk)
                tt("pool", YB[:, tsl(jj)], cvt, SZB[:, tsl(jj)], ALU.mult, cvk + kh(0, lo, lo + T), kh(1, lo, lo + T))
        dma("sp", mixed0[1024 + c * 128:1024 + (c + 1) * 128, :], YB[:, 0:SEQ], kh(1, 0, SEQ), [("mixed0", 8 + c, j) for j in range(NT)], "yb")

    P.barrier()
    m0v = mixed0.rearrange("(k p) t -> p k t", p=128)
    mtb = [bigh[i][:, 0:8 * T].rearrange("p (k t) -> p k t", t=T) for i in range(2)]

    def mt_get0(j):
        b = j % 2
        lo = mtb[b]
        hi = hT[:, :, b * T:(b + 1) * T]
        rk = [("mixed0", c, j) for c in range(16)]
        dma("sp", lo, m0v[:, 0:8, tsl(j)], rk, [("mtlo", b)], "mt%d" % b)
        dma("sp", hi, m0v[:, 8:16, tsl(j)], rk, [("mthi", b)], "mth%d" % b)
        return (lambda k: lo[:, k, :] if k < 8 else hi[:, k - 8, :]), [("mtlo", b), ("mthi", b)]

    out_phase(mt_get0, 16, w_out0, 1, xT, x1T if layers > 1 else outT, "x0", hT[:, :, 2 * T:3 * T], ["sqh"])
    if layers == 1:
        P.emit(nc, es)
        es.close()
        return nc, P

    P.barrier()
    norm_phase(x1T, 16, "x0o")
    w1 = w_in1.rearrange("(k p) n -> p k n", p=128)
    LRs = nc.dram_tensor("LRs", [2, 16, SEQ], BF16, kind="Internal").ap()
    dma("pool", lrw[:], w1[:, :, 3072:3104], [], ["lrw"], "c8")
    dma("pool", wgt[:], wg.rearrange("d r k -> r d k"), [], ["wgt"], "c9")

    for q in range(2):
        dma("pool", wb[0][:, :, q * 512:(q + 1) * 512], w1[:, :, q * 512:(q + 1) * 512], [], [("wb", 0, q)], "wb0_%d" % q)
    for oc in range(8):
        dst = qTs if oc < 4 else kTs
        for j in range(NT):
            b = j % 2
            bank = (oc * NT + j) % 4
            for k in range(8):
                mm(pb[bank][:], wb[0][:, k, oc * 128:(oc + 1) * 128], hT[:, k, tsl(j)], k == 0, k == 7,
                   [("wb", 0, oc // 4)] + khT(k, j * T, j * T + T), [("pb", bank)])
            st, stk = bslot(bank, b)
            act(st, pb[bank][:], AF.Copy, [("pb", bank)], stk)
            dma("sp", dst[(oc % 4) * 128:(oc % 4 + 1) * 128, tsl(j)], st, stk, [("qk", oc, j)], "st%d_%d" % (bank, b))
    for q in range(2):
        dma("pool", wb[1][:, :, q * 512:(q + 1) * 512], w1[:, :, 1024 + q * 512:1024 + (q + 1) * 512], [], [("wb", 1, q)], "wb1_%d" % q)
    Vsv = Vs.rearrange("(m p) e -> p m e", p=128)
    Vt = [bigh[i][:, 0:4096].rearrange("p (m e) -> p m e", e=1024) for i in range(2)]
    for j in range(NT):
        b = j % 2
        for m in range(4):
            for half in range(2):
                bank = (m * 2 + half) % 4
                for k in range(8):
                    mm(pb[bank][:], hT[:, k, j * T + m * 128:j * T + (m + 1) * 128], wb[1][:, k, half * 512:(half + 1) * 512],
                       k == 0, k == 7, [("wb", 1, half)] + khT(k, j * T, j * T + T), [("pb", bank)])
                act(Vt[b][:, m, half * 512:(half + 1) * 512], pb[bank][:], AF.Copy, [("pb", bank)], [("Vt", b, m, half)])
        dma("sp", Vsv[:, j * 4:(j + 1) * 4, :], Vt[b], [("Vt", b, m, hf) for m in range(4) for hf in range(2)],
            [("Vs", j)], "vt%d" % b)
    for q in range(2):
        dma("pool", wb[0][:, :, q * 512:(q + 1) * 512], w1[:, :, 2048 + q * 512:2048 + (q + 1) * 512], [], [("wb", 0, q)], "wb0_%d" % q)
    for oc in range(8):
        for j in range(NT):
            b = j % 2
            bank = (oc * NT + j) % 4
            for k in range(8):
                mm(pb[bank][:], wb[0][:, k, oc * 128:(oc + 1) * 128], hT[:, k, tsl(j)], k == 0, k == 7,
                   [("wb", 0, oc // 4)] + khT(k, j * T, j * T + T), [("pb", bank)])
            stg = tmph[bank % 2][b]
            sk = [("th%d" % (bank % 2), b)]
            act(stg[:], pb[bank][:], AF.Tanh, [("pb", bank)], sk, scale=0.5)
            stt("dve", stg[:], stg[:], 1.0, pb[bank][:], ALU.add, ALU.mult, sk + [("pb", bank)], sk)
            dma("sp", SRs[oc * 128:(oc + 1) * 128, tsl(j)], stg[:], sk, [("SRs", oc, j)], "sr%d_%d" % (bank % 2, b))
    for d in range(2):
        for j in range(NT):
            b = j % 2
            bank = 4 + b
            for k in range(8):
                mm(pb[bank][0:16, :], lrw[:, k, d * 16:(d + 1) * 16], hT[:, k, tsl(j)], k == 0, k == 7,
                   ["lrw"] + khT(k, j * T, j * T + T), [("pb", bank)])
            act(tmph[2][b][0:16, :], pb[bank][0:16, :], AF.Copy, [("pb", bank)], [("th2", b)])
            dma("sp", LRs[d, :, tsl(j)], tmph[2][b][0:16, :], [("th2", b)], [("LRs", d, j)], "lrs%d" % b)

    P.barrier()
    SC = 128.0 ** -0.5
    qh, kh_, O = big[0], big[1], [big[2], big[3]]
    HC = NCH // 2
    Vhh = [bigh[i][:, 0:HC * 256].rearrange("p (m e) -> p m e", e=256) for i in range(2)]

    def Vn(n):
        return Vhh[n // HC][:, n % HC, :]

    sq1 = bigh[1][:, 0:8 * T].rearrange("p (k t) -> p k t", t=T)
    for h in range(4):
        dma("sp", qh[:, 0:SEQ], qTs[h * 128:(h + 1) * 128, :], [("qk", h, j) for j in range(NT)], ["qh"], "qh")
        dma("sp", kh_[:, 0:SEQ], kTs[h * 128:(h + 1) * 128, :], [("qk", 4 + h, j) for j in range(NT)], ["kh"], "kh")
        for i in range(2):
            dma("sp", Vhh[i], Vsv[:, i * HC:(i + 1) * HC, h * 256:(h + 1) * 256], [("Vs", j) for j in range(NT)], ["Vh"], "vh%d" % i)
        def gla_dir(d, h=h):
            first = True
            for sidx in range(NT):
                j = sidx if d == 0 else NT - 1 - sidx
                (Lt, k0), (Lc, k1), (E1, k2), (E2, k3) = [wslot(0, i * 2 + d) for i in range(4)]
                QI, KI, KST = tmph[0][d], tmph[1][d], tmph[2][d]
                bz, bo0, bo1, bk = 0 + d, 2 + d, 4 + d, 6 + d
                pz = pbx[bz]
                dma("sp", lrt[d][:], LRs[d, :, tsl(j)], [("LRs", d, j)], [("lrt", d)], "lrt%d" % d)
                mm(pz, wgt[0:16, d, h * 128:(h + 1) * 128], lrt[d][:], True, True, ["wgt", ("lrt", d)], [("pb", bz)])
                yield
                act(Lt, pz, AF.Exp, [("pb", bz), "negb"], k0, bias=negb[:, d, h:h + 1], scale=-1.0)
                act(Lt, Lt, AF.Ln, k0, k0, bias=1.0)
                yield
                scan(Lc, m01[:, :], Lt, 0.0, d == 1, k0 + ["m01"], k1)
                yield
                tot = Lc[:, T - 1:T] if d == 0 else Lc[:, 0:1]
                act(E1, Lc, AF.Exp, k1, k2, scale=-1.0 / 16)
                act(E2, Lc, AF.Exp, k1, k3, scale=1.0 / 16)
                act(DEC[d][:, j, :], tot, AF.Exp, k1, [("DEC", d, j)], scale=-1.0 / 16)
                yield
                stt("dve", QI[:], qh[:, tsl(j)], SC, E1, ALU.mult, ALU.mult, ["qh"] + k2, [("th0", d)])
                tt("pool", KI[:], kh_[:, tsl(j)], E2, ALU.mult, ["kh"] + k3, [("th1", d)])
                yield
                stt("dve", KST[:], kh_[:, tsl(j)], DEC[d][:, j, :], E2, ALU.mult, ALU.mult, ["kh", ("DEC", d, j)] + k3, [("th2", d)])
                yield
                kt_ps = (pb6h if d == 0 else pbt)[:, 0:T]
                kv_ps = pbx[bk][:, 256:512]
                jcs = list(range(4)) if d == 0 else list(range(3, -1, -1))
                SM = [wb[1][:, 2 + (d * 4 + jc) // 2, ((d * 4 + jc) % 2) * T:((d * 4 + jc) % 2 + 1) * T] for jc in range(4)]
                for jc in range(4):
                    P.add("pe", lambda e, o=kt_ps[:, jc * 128:(jc + 1) * 128], i=KST[:, jc * 128:(jc + 1) * 128]:
                          e.transpose(out=o, in_=i, identity=ident[:]), r=[("th2", d), "ident"], w=[("pb", bk)])
                yield
                act(KTt[d][:].rearrange("p c t -> p (c t)"), kt_ps, AF.Copy, [("pb", bk)], [("KT", d)])
                yield
                for jc in jcs:
                    if d == 0:
                        isl, msl = slice(jc * 128, T), tri[:, 0, 0:(4 - jc) * 128]
                    else:
                        isl, msl = slice(0, (jc + 1) * 128), tri[:, 1, (3 - jc) * 128:T]
                    ncol = isl.stop - isl.start
                    sb_, sbank = (pz, bz) if jc % 2 == 0 else (pbx[bk], bk)
                    mm(sb_[:, 0:ncol], KI[:, jc * 128:(jc + 1) * 128], QI[:, isl], True, True, [("th1", d), ("th0", d)], [("pb", sbank)])
                    yield
                    tt("dve", SM[jc][:, 0:ncol], sb_[:, 0:ncol], msl, ALU.mult, [("pb", sbank), "tri"], [("SM", d, jc)])
                    yield
                for jc in range(4):
                    mm(kv_ps, KTt[d][:, jc, :], Vn(j * 4 + jc), jc == 0, jc == 3, [("KT", d), "Vh"], [("pb", bk)])
                yield
                for ec, bo in ((0, bo0), (1, bo1)):
                    for idx, jc in enumerate(jcs):
                        isl = slice(jc * 128, T) if d == 0 else slice(0, (jc + 1) * 128)
                        ncol = isl.stop - isl.start
                        mm(pbx[bo][:, isl], Vn(j * 4 + jc)[:, ec * 128:(ec + 1) * 128], SM[jc][:, 0:ncol], idx == 0, first and idx == 3,
                           ["Vh", ("SM", d, jc)], [("pb", bo)])
                    if not first:
                        mm(pbx[bo], Sbf[d][:, ec * 128:(ec + 1) * 128], QI[:], False, True, [("Sb", d), ("th0", d)], [("pb", bo)])
                yield
                if first:
                    P.add("dve", lambda e, o=Sst[d][:], i=kv_ps: e.tensor_copy(out=o, in_=i), r=[("pb", bk)], w=[("S", d)])
                else:
                    stt("dve", Sst[d][:], Sst[d][:], DEC[d][:, j, :], kv_ps, ALU.mult, ALU.add,
                        [("S", d), ("DEC", d, j), ("pb", bk)], [("S", d)])
                yield
                act(Sbf[d][:], Sst[d][:], AF.Copy, [("S", d)], [("Sb", d)])
                firstO = (d == 0) == (2 * j <= NT - 1)
                okj = [[("O", ec, n) for n in range(j * 4, j * 4 + 4)] for ec in range(2)]
                for ec, bo in ((0, bo0), (1, bo1)):
                    if firstO:
                        act(O[ec][:, tsl(j)], pbx[bo], AF.Copy, [("pb", bo)], okj[ec])
                    else:
                        tt("dve", O[ec][:, tsl(j)], O[ec][:, tsl(j)], pbx[bo], ALU.add, [("pb", bo)] + okj[ec], okj[ec])
                first = False
                yield

        g0, g1 = gla_dir(0), gla_dir(1)
        for _ in range(10):
            next(g0)
        run_rr([g0, g1])
        for j in range(NT):
            b = j % 2
            okeys = [[("O", ec, n) for n in range(j * 4, j * 4 + 4)] for ec in range(2)]
            for ec in range(2):
                dma("sp", tmph[ec][b][:], SRs[(2 * h + ec) * 128:(2 * h + ec + 1) * 128, tsl(j)],
                    [("SRs", 2 * h + ec, j)], [("th%d" % ec, b)], "srl%d_%d" % (ec, b))
                act(sqs[:, ec, :], O[ec][:, tsl(j)], AF.Square, okeys[ec], [("sq1", ec)])
            for ec in range(2):
                mm(pb[6][:], ones[:], sqs[:, ec, :], ec == 0, ec == 1, ["ones", ("sq1", ec)], [("pb", 6)])
            rstd_from(pb[6][:], b, 256, mul=0.5)
            for ec in range(2):
                t4, t4k = wslot(1, ec)
                stt("dve", t4, O[ec][:, tsl(j)], pngt[:, ec:ec + 1], rs[b][:], ALU.mult, ALU.mult,
                    okeys[ec] + [("rs", b), "pngt"], t4k)
                tt("pool", hT[:, 2 * h + ec, tsl(j)], t4, tmph[ec][b][:], ALU.mult,
                   t4k + [("th%d" % ec, b)], khT(2 * h + ec, j * T, j * T + T))

    P.barrier()

    def mt_get1(j):
        return (lambda k: hT[:, k, tsl(j)]), [("hT", k, j) for k in range(8)]

    out_phase(mt_get1, 8, w_out1, 17, x1T, outT, "x0o", sq1, ["sq1all"])
    P.emit(nc, es)
    es.close()
    return nc, P


def host_inputs(x_b, inp):
    f = lambda a: np.ascontiguousarray(np.asarray(a, dtype=np.float32))
    vecs = [inp["even_norm_pre"][0], inp["even_norm_post"][0]] + [inp["rg_conv_w"][0, i] for i in range(4)] + \
           [inp["rg_conv_b"][0]] + [inp["rg_gate_b"][0, d, g].reshape(-1) for d in range(2) for g in range(2)] + \
           [inp["rg_lambda"][0, d] for d in range(2)] + [inp["sc_conv_w"][0, i] for i in range(3)] + \
           [inp["odd_norm_pre"][0], inp["odd_norm_post"][0]]
    pvec = f(np.stack([np.asarray(v) for v in vecs], 0).reshape(NV, 8, 128).transpose(2, 0, 1))
    pg = f(np.asarray(inp["gla_b_gate"][0]).reshape(2, 4, 128).transpose(2, 0, 1))
    png = f(np.asarray(inp["gla_norm_g"][0]).reshape(2, 128).transpose(1, 0))
    cm01 = np.ones((128, T), np.float32)
    jj, ii = np.meshgrid(np.arange(128), np.arange(128), indexing="ij")
    mf = np.ones((128, T), np.float32)
    mf[:, 0:128] = (jj <= ii)
    mb = np.ones((128, T), np.float32)
    mb[:, T - 128:T] = (jj >= ii)
    ctri = f(np.stack([mf, mb], 1))
    return {
        "xT": f(np.asarray(x_b).T), "w_in0": f(inp["even_w_in"][0]), "gate_w": f(inp["rg_gate_w"][0]),
        "w_out0": f(inp["even_w_out"][0]), "w_in1": f(inp["odd_w_in"][0]), "wg": f(inp["gla_w_gate_lr"][0]),
        "w_out1": f(inp["odd_w_out"][0]), "pvec": pvec, "pg": pg, "png": png, "cm01": cm01, "ctri": ctri,
        "cones": np.ones((128, 128), np.float32), "cident": np.eye(128, dtype=np.float32),
    }


def kernel(**inputs):
    x = np.asarray(inputs["x"])
    B, SEQ, _ = x.shape
    nc, _ = build_nc(SEQ)
    shared = None
    in_maps = []
    for b in range(B):
        m = host_inputs(x[b], inputs)
        if shared is None:
            shared = m
        else:
            for k in m:
                if k != "xT":
                    m[k] = shared[k]
        in_maps.append(m)
    res = run_bass_kernel_spmd(nc, in_maps, core_ids=list(range(B)))
    out = np.stack([np.asarray(r["outT"]).T for r in res.results], 0)
    return np.ascontiguousarray(out.astype(np.float32))
```

```python
import numpy as np
from contextlib import ExitStack
import concourse.bass as bass
import concourse.mybir as mybir
from concourse.bass_utils import run_bass_kernel_spmd
from concourse.ap import AP

F32 = mybir.dt.float32
BF16 = mybir.dt.bfloat16
AF = mybir.ActivationFunctionType
ALU = mybir.AluOpType

D = 1024
T = 512
EPS = 1e-6
NV = 18
LAYERS = 2
SBUF_LEFT = []


def rev(ap):
    pairs = [list(p) for p in ap.ap]
    step, cnt = pairs[-1]
    pairs[-1] = [-step, cnt]
    return AP(ap.tensor, ap.offset + step * (cnt - 1), pairs)


class _Op:
    __slots__ = ("eng", "fn", "deps", "dma_key", "signal", "val", "idx", "semkey")


class Prog:
    def __init__(self):
        self.ops = []
        self.lastw = {}
        self.rd_eng = {}
        self.rd_dma = {}
        self.last_eng = {}
        self.last_dma = {}
        self.fence = []
        self.passed = set()

    def barrier(self):
        self.fence = list(self.last_eng.values()) + list(self.last_dma.values())
        self.passed = set()

    def add(self, eng, fn, r=(), w=(), dma_key=None):
        op = _Op()
        op.eng, op.fn, op.dma_key = eng, fn, dma_key
        op.signal = dma_key is not None
        op.val = 0
        op.semkey = None
        op.idx = len(self.ops)
        deps = {}
        if self.fence and eng not in self.passed:
            for p in self.fence:
                deps[p.idx] = p
            self.passed.add(eng)
        for k in r:
            p = self.lastw.get(k)
            if p is not None:
                deps[p.idx] = p
        for k in w:
            p = self.lastw.get(k)
            if p is not None:
                deps[p.idx] = p
            for q in self.rd_eng.get(k, {}).values():
                deps[q.idx] = q
            for q in self.rd_dma.get(k, ()):
                deps[q.idx] = q
        op.deps = [p for p in deps.values() if (p.dma_key is not None) or (p.eng != eng) or (eng != "pe")]
        for k in r:
            if dma_key is not None:
                self.rd_dma.setdefault(k, []).append(op)
            else:
                self.rd_eng.setdefault(k, {})[eng] = op
        for k in w:
            self.lastw[k] = op
            self.rd_eng[k] = {}
            self.rd_dma[k] = []
        if dma_key is not None:
            self.last_dma[dma_key] = op
        else:
            self.last_eng[eng] = op
        self.ops.append(op)
        return op

    def emit(self, nc, es):
        for op in self.ops:
            for p in op.deps:
                p.signal = True
        cnt = {}
        for op in self.ops:
            if op.signal:
                key = ("dma", op.dma_key) if op.dma_key is not None else ("eng", op.eng)
                cnt[key] = cnt.get(key, 0) + (16 if op.dma_key is not None else 1)
                op.semkey, op.val = key, cnt[key]
        sems = {}
        for i, key in enumerate(cnt):
            sems[key] = es.enter_context(nc.semaphore("s%d" % i))
        self.nsem = len(sems)
        block = es.enter_context(nc.Block())
        ops = self.ops

        def run(engname, final=False):
            def body(e):
                waited = {}
                for op in ops:
                    if op.eng != engname:
                        continue
                    for p in op.deps:
                        if waited.get(p.semkey, 0) < p.val:
                            e.wait_ge(sems[p.semkey], p.val)
                            waited[p.semkey] = p.val
                    ins = op.fn(e)
                    if op.signal:
                        ins.then_inc(sems[op.semkey], 16 if op.dma_key is not None else 1)
                if final:
                    for key, v in cnt.items():
                        if waited.get(key, 0) < v:
                            e.wait_ge(sems[key], v)
            return body

        block.sync(run("sp", final=True))
        block.tensor(run("pe"))
        block.scalar(run("act"))
        block.vector(run("dve"))
        block.gpsimd(run("pool"))


def build_nc(SEQ, layers=LAYERS):
    NT = SEQ // T
    NCH = SEQ // 128
    BW = max(SEQ, 8 * T)
    nc = bass.Bass("TRN2", target_bir_lowering=False)
    din = lambda n, s, dt=F32: nc.dram_tensor(n, s, dt, kind="ExternalInput").ap()
    xT = din("xT", [D, SEQ])
    w_in0 = din("w_in0", [D, 6144])
    gate_w = din("gate_w", [2, 2, 8, 128, 128])
    w_out0 = din("w_out0", [2048, D])
    w_in1 = din("w_in1", [D, 3104])
    wg = din("wg", [2, 16, 512])
    w_out1 = din("w_out1", [D, D])
    pvec = din("pvec", [128, NV, 8])
    pg = din("pg", [128, 2, 4])
    png = din("png", [128, 2])
    cm01 = din("cm01", [128, T])
    ctri = din("ctri", [128, 2, T])
    cones = din("cones", [128, 128])
    cident = din("cident", [128, 128])
    outT = nc.dram_tensor("outT", [D, SEQ], F32, kind="ExternalOutput").ap()
    mixed0 = nc.dram_tensor("mixed0", [2048, SEQ], BF16, kind="Internal").ap()
    x1T = nc.dram_tensor("x1T", [D, SEQ], F32, kind="Internal").ap()
    qTs = nc.dram_tensor("qTs", [512, SEQ], F32, kind="Internal").ap()
    kTs = nc.dram_tensor("kTs", [512, SEQ], F32, kind="Internal").ap()
    Vs = nc.dram_tensor("Vs", [SEQ, D], BF16, kind="Internal").ap()
    SRs = nc.dram_tensor("SRs", [D, SEQ], BF16, kind="Internal").ap()

    es = ExitStack()
    P = Prog()
    sb = lambda n, s, dt=F32: es.enter_context(nc.sbuf_tensor(n, s, dt))
    ps = lambda n, s, dt=F32: es.enter_context(nc.psum_tensor(n, s, dt))

    hT = sb("hT", [128, 8, SEQ], BF16)
    pv = sb("pv", [128, NV, 8])
    pgt = sb("pgt", [128, 2, 4])
    pngt = sb("pngt", [128, 2])
    m01 = sb("m01", [128, T], BF16)
    tri = sb("tri", [128, 2, T])
    ones = sb("ones", [128, 128], BF16)
    ident = sb("ident", [128, 128], BF16)
    der = sb("der", [128, 12, 8])
    negb = sb("negb", [128, 2, 4])
    gwh = [sb("gwh%d" % i, [128, 4, 128], BF16) for i in range(2)]
    lrw = sb("lrw", [128, 8, 32], BF16)
    wgt = sb("wgt", [16, 2, 512], BF16)
    lrt = [sb("lrt%d" % d, [16, T], BF16) for d in range(2)]
    rs = [sb("rs%d" % i, [128, T]) for i in range(2)]
    wb = [sb("wb%d" % i, [128, 8, 1024], BF16) for i in range(2)]
    wbf = [wb[i][:].rearrange("p k n -> p (k n)").bitcast(F32) for i in range(2)]
    big = [sb("big%d" % i, [128, BW + 4]) for i in range(4)]
    bigh = [sb("bigh%d" % i, [128, BW], BF16) for i in range(2)]
    tmph = [[sb("tmph%d_%d" % (i, j), [128, T], BF16) for j in range(2)] for i in range(3)]
    zero1 = sb("zero1", [128, 1])
    Sst = [sb("Sst%d" % d, [128, 256]) for d in range(2)]
    Sbf = [sb("Sbf%d" % d, [128, 256], BF16) for d in range(2)]
    DEC = [sb("DEC%d" % d, [128, NT, 1]) for d in range(2)]
    KTt = [sb("KTt%d" % d, [128, 4, 128], BF16) for d in range(2)]
    xt = [big[2 + i][:, 0:8 * T].rearrange("p (k t) -> p k t", t=T) for i in range(2)]
    sqs = sb("sqs", [128, 2, T], BF16)

    SBUF_LEFT.append(nc.sbuf_bytes_remaining)
    pb = [ps("pb%d" % i, [128, T]) for i in range(7)]
    pbt = ps("pbt", [128, 1024], BF16)
    pbx = [pb[i][:] for i in range(7)] + [pbt[:].bitcast(F32)]
    pb6h = pb[6][:].bitcast(BF16)

    def kb(i, lo, hi):
        return [("big", i, g) for g in range(lo // T, (hi - 1) // T + 1)]

    def kh(i, lo, hi):
        return [("bigh", i, g) for g in range(lo // T, (hi - 1) // T + 1)]

    def khT(k, lo, hi):
        return [("hT", k, g) for g in range(lo // T, (hi - 1) // T + 1)]

    def bslot(i, n):
        return big[i][:, n * T:(n + 1) * T], [("big", i, n)]

    def wslot(i, n):
        return wbf[i][:, n * T:(n + 1) * T], [("wbf", i, n)]

    def bcast_last(ap, n):
        pairs = [list(p) for p in ap.ap]
        assert pairs[-1][1] == 1
        pairs[-1] = [0, n]
        return AP(ap.tensor, ap.offset, pairs)

    def dma(eng, out, in_, r, w, key):
        P.add(eng, lambda e: e.dma_start(out=out, in_=in_), r=r, w=w, dma_key=key)

    def act(out, in_, func, r, w, bias=0.0, scale=1.0):
        P.add("act", lambda e: e.activation(out=out, in_=in_, func=func, bias=bias, scale=scale), r=r, w=w)

    def tt(eng, out, in0, in1, op, r, w):
        P.add(eng, lambda e: e.tensor_tensor(out=out, in0=in0, in1=in1, op=op), r=r, w=w)

    def ts(eng, out, in0, s1, s2, op0, op1, r, w):
        if op1 is None:
            P.add(eng, lambda e: e.tensor_scalar(out=out, in0=in0, scalar1=s1, scalar2=None, op0=op0), r=r, w=w)
        else:
            P.add(eng, lambda e: e.tensor_scalar(out=out, in0=in0, scalar1=s1, scalar2=s2, op0=op0, op1=op1), r=r, w=w)

    def stt(eng, out, in0, s, in1, op0, op1, r, w):
        P.add(eng, lambda e: e.scalar_tensor_tensor(out=out, in0=in0, scalar=s, in1=in1, op0=op0, op1=op1), r=r, w=w)

    def mm(out, lhsT, rhs, start, stop, r, w):
        P.add("pe", lambda e: e.matmul(out, lhsT=lhsT, rhs=rhs, start=start, stop=stop), r=r, w=w)

    def scan(out, d0, d1, init, reverse, r, w):
        if reverse:
            P.add("dve", lambda e: e.tensor_tensor_scan(out=rev(out), data0=rev(d0), data1=rev(d1), initial=init,
                                                        op0=ALU.mult, op1=ALU.add), r=r, w=w)
        else:
            P.add("dve", lambda e: e.tensor_tensor_scan(out=out, data0=d0, data1=d1, initial=init,
                                                        op0=ALU.mult, op1=ALU.add), r=r, w=w)

    def tsl(j):
        return slice(j * T, (j + 1) * T)

    xtk = [kb(2 + i, 0, 8 * T) for i in range(2)]

    dma("sp", pv[:], pvec[:, :, :], [], ["pv"], "c0")
    dma("sp", pgt[:], pg[:, :, :], [], ["pgt"], "c1")
    dma("sp", pngt[:], png[:, :], [], ["pngt"], "c2")
    dma("pool", m01[:], cm01[:, :], [], ["m01"], "c3")
    dma("sp", tri[:], ctri[:, :, :], [], ["tri"], "c4")
    dma("pool", ones[:], cones[:, :], [], ["ones"], "c5")
    dma("pool", ident[:], cident[:, :], [], ["ident"], "c6")
    P.add("dve", lambda e: e.memset(zero1[:], 0.0), w=["zero1"])
    ts("dve", der[:, 0:4, :], pv[:, 7:11, :], 0.5, None, ALU.mult, None, ["pv"], ["der"])
    act(der[:, 8:10, :], pv[:, 11:13, :], AF.Exp, ["pv"], ["der8"], scale=-1.0)
    act(der[:, 10:12, :], der[:, 8:10, :], AF.Ln, ["der8"], ["der10"], bias=1.0)
    ts("dve", der[:, 4:6, :], der[:, 10:12, :], -4.0, None, ALU.mult, None, ["der10"], ["der"])
    ts("dve", der[:, 6:8, :], der[:, 10:12, :], -8.0, None, ALU.mult, None, ["der10"], ["der"])
    ts("dve", negb[:], pgt[:], -1.0, None, ALU.mult, None, ["pgt"], ["negb"])

    def rstd_from(psum_ap, b, n, mul=1.0):
        act(rs[b][:], psum_ap, AF.Ln, [("pb", 6)], [("rs", b)], bias=EPS, scale=1.0 / n)
        act(rs[b][:], rs[b][:], AF.Exp, [("rs", b)], [("rs", b)], scale=-0.5, bias=float(np.log(mul)))

    def norm_phase(src, gidx, tag):
        srcv = src.rearrange("(k p) t -> p k t", p=128)
        sq = bigh[1][:, 0:8 * T].rearrange("p (k t) -> p k t", t=T)
        sqk = kh(1, 0, 8 * T)
        for j in range(NT):
            b = j % 2
            dma("sp", xt[b], srcv[:, :, tsl(j)], [("src", tag, j)], xtk[b], "xt%d" % b)
            act(sq, xt[b], AF.Square, xtk[b], sqk)
            for k in range(8):
                mm(pb[6][:], ones[:], sq[:, k, :], k == 0, k == 7, ["ones"] + sqk, [("pb", 6)])
            rstd_from(pb[6][:], b, D)
            for k in range(8):
                stt("dve", hT[:, k, tsl(j)], xt[b][:, k, :], pv[:, gidx, k:k + 1], rs[b][:], ALU.mult, ALU.mult,
                    xtk[b] + [("rs", b), "pv"], khT(k, j * T, (j + 1) * T))

    def out_phase(mt_get, nk, wsrc, gidx, resid, dst, tag, sq, sqk):
        wo = wsrc.rearrange("(k p) n -> p k n", p=128)
        for k0 in range(0, nk, 4):
            s, q = k0 // 8, (k0 % 8) // 4
            dma("pool", wb[s][:, q * 4:q * 4 + 4, :], wo[:, k0:k0 + 4, :], [], [("wb", s, q)], "wb%d_%d" % (s, q))
        resv = resid.rearrange("(k p) t -> p k t", p=128)
        dstv = dst.rearrange("(k p) t -> p k t", p=128)
        ysb = [big[i][:, 0:8 * T].rearrange("p (k t) -> p k t", t=T) for i in range(2)]
        def loads(j):
            bq = j % 2
            r_ = mt_get(j)
            dma("sp", xt[bq], resv[:, :, tsl(j)], [("src", tag, j)], xtk[bq], "xt%d" % bq)
            return r_

        pending = loads(0)
        for j in range(NT):
            b = j % 2
            ys = ysb[b]
            mget, mkeys = pending
            if j + 1 < NT:
                pending = loads(j + 1)
            for o in range(8):
                bank = o % 4
                for k in range(nk):
                    mm(pb[bank][:], wb[k // 8][:, k % 8, o * 128:(o + 1) * 128], mget(k), k == 0, k == nk - 1,
                       [("wb", k // 8, (k % 8) // 4)] + mkeys, [("pb", bank)])
                act(ys[:, o, :], pb[bank][:], AF.Copy, [("pb", bank)], [("big", b, o)])
                act(sq[:, o, :], pb[bank][:], AF.Square, [("pb", bank)], sqk)
            for o in range(8):
                mm(pb[6][:], ones[:], sq[:, o, :], o == 0, o == 7, ["ones"] + sqk, [("pb", 6)])
            rstd_from(pb[6][:], b, D)
            for o in range(8):
                stt("dve", ys[:, o, :], ys[:, o, :], pv[:, gidx, o:o + 1], rs[b][:], ALU.mult, ALU.mult,
                    [("big", b, o), ("rs", b), "pv"], [("big", b, o)])
                tt("pool", ys[:, o, :], ys[:, o, :], xt[b][:, o, :], ALU.add, [("big", b, o)] + xtk[b], [("big", b, o)])
            dma("sp", dstv[:, :, tsl(j)], ys, [("big", b, o) for o in range(8)], [("src", tag + "o", j)], "xo%d" % b)

    norm_phase(xT, 0, "x0")
    w0 = w_in0.rearrange("(k p) n -> p k n", p=128)
    XA, UA, HB, CVb = big[0], big[1], big[2], big[3]
    UAb, SZ = bigh[0], bigh[1]
    P.add("pool", lambda e: e.memset(XA[:, 0:2], 0.0), w=kb(0, 0, 2))
    P.add("pool", lambda e: e.memset(XA[:, SEQ + 2:SEQ + 4], 0.0), w=kb(0, SEQ + 2, SEQ + 4))

    def rgA_front(c):
        s = c % 2
        wA = wb[0][:, :, s * 256:(s + 1) * 256]
        dma("pool", wA[:, :, 0:128], w0[:, :, c * 128:(c + 1) * 128], [], [("wA", s, 0)], "wb%d_0" % s)
        dma("pool", wA[:, :, 128:256], w0[:, :, 1024 + c * 128:1024 + (c + 1) * 128], [], [("wA", s, 1)], "wb%d_1" % s)
        dma("pool", gwh[s][:], gate_w[:, :, c].rearrange("d g i j -> i (d g) j"), [], [("gwh", s)], "gw%d" % s)
        yield
        for j in range(NT):
            bank = 6 + (j % 2)
            for k in range(8):
                mm(pbx[bank], wA[:, k, 0:128], hT[:, k, tsl(j)], k == 0, k == 7, [("wA", s, 0)] + khT(k, j * T, j * T + T), [("pb", bank)])
            yield
            act(XA[:, 2 + j * T:2 + (j + 1) * T], pbx[bank], AF.Copy, [("pb", bank)], kb(0, 2 + j * T, 2 + (j + 1) * T))
            yield

    def rgA_conv(c):
        allXA = kb(0, 0, SEQ + 4)
        allUA = kb(1, 0, SEQ)
        ts("dve", UA[:, 0:SEQ], XA[:, 0:SEQ], pv[:, 2, c:c + 1], pv[:, 6, c:c + 1], ALU.mult, ALU.add, allXA + ["pv"], allUA)
        for kk in range(1, 4):
            stt("dve", UA[:, 0:SEQ], XA[:, kk:kk + SEQ], pv[:, 2 + kk, c:c + 1], UA[:, 0:SEQ], ALU.mult, ALU.add, allXA + allUA + ["pv"], allUA)
        hlf = SEQ // 2
        act(UAb[:, 0:hlf], UA[:, 0:hlf], AF.Copy, kb(1, 0, hlf), kh(0, 0, hlf))
        P.add("pool", lambda e: e.tensor_copy(out=UAb[:, hlf:SEQ], in_=UA[:, hlf:SEQ]), r=kb(1, hlf, SEQ), w=kh(0, hlf, SEQ))

    def rg_dir(d, c):
        s = c % 2
        cs = slice(c * 128, (c + 1) * 128)
        wA = wb[0][:, :, s * 256:(s + 1) * 256]

        def slots(b):
            if d == 1:
                return [bslot(3, i * 2 + b) for i in range(4)]
            return [wslot(1, i * 2 + b) for i in range(4)]

        br, bi = (2, 3) if d == 1 else (0, 1)
        g_r = gwh[s][:, d * 2 + 0, :]
        g_i = gwh[s][:, d * 2 + 1, :]
        prev = None
        pend = []
        for t in range(NT):
            j = t if d == 0 else NT - 1 - t
            t_other = NT - 1 - j if d == 0 else j
            store = (t < t_other) or (t == t_other and d == 1)
            b = j % 2
            uk = kh(0, j * T, j * T + T)
            (THr, k0), (THi, k1), (A, k2), (Ht, k3) = slots(b)
            mm(pbx[br], g_r, UAb[:, tsl(j)], True, True, [("gwh", s)] + uk, [("pb", br)])
            act(THr, pbx[br], AF.Tanh, [("pb", br), "der"], k0, bias=der[:, d * 2 + 0, c:c + 1], scale=0.5)
            mm(pbx[bi], g_i, UAb[:, tsl(j)], True, True, [("gwh", s)] + uk, [("pb", bi)])
            act(THi, pbx[bi], AF.Tanh, [("pb", bi), "der"], k1, bias=der[:, d * 2 + 1, c:c + 1], scale=0.5)
            act(A, THr, AF.Exp, k0 + ["der"], k2, bias=der[:, 4 + d, c:c + 1], scale=der[:, 4 + d, c:c + 1])
            tt("dve", THr, A, A, ALU.mult, k2, k0)
            ts("pool", THr, THr, -1.0 / 16, 1.0 / 16, ALU.mult, ALU.add, k0, k0)
            stt("dve", THi, THi, 1.0, UA[:, tsl(j)], ALU.add, ALU.mult, k1 + kb(1, j * T, j * T + T), k1)
            pend.append((j, store))
            if t % 2 == 0 and t + 1 < NT:
                continue
            yield
            for (jj, _) in pend:
                (THr2, k02) = slots(jj % 2)[0]
                act(THr2, THr2, AF.Sqrt, k02, k02)
            yield
            for (jj, st) in pend:
                bb = jj % 2
                (THr2, k02), (THi2, k12), (A2, k22), (Ht2, k32) = slots(bb)
                hk2 = kb(2, jj * T, jj * T + T)
                tt("pool", THi2, THr2, THi2, ALU.mult, k02 + k12, k12)
                if prev is None:
                    init, rk = zero1[:, 0:1], ["zero1"]
                elif prev[1]:
                    col = prev[0] * T + (T - 1 if d == 0 else 0)
                    init, rk = HB[:, col:col + 1], kb(2, col, col + 1)
                else:
                    pslot, pk = slots(prev[0] % 2)[3]
                    init, rk = (pslot[:, T - 1:T] if d == 0 else pslot[:, 0:1]), pk
                if st:
                    scan(HB[:, tsl(jj)], A2, THi2, init, d == 1, k22 + k12 + rk, hk2)
                else:
                    zb = 4 + bb
                    for k in range(8):
                        mm(pbx[zb], wA[:, k, 128:256], hT[:, k, tsl(jj)], k == 0, k == 7,
                           [("wA", s, 1)] + khT(k, jj * T, jj * T + T), [("pb", zb)])
                    scan(Ht2, A2, THi2, init, d == 1, k22 + k12 + rk, k32)
                    act(THr2, pbx[zb], AF.Tanh, [("pb", zb)], k02, scale=0.5)
                    stt("dve", THr2, THr2, 1.0, pbx[zb], ALU.add, ALU.mult, k02 + [("pb", zb)], k02)
                    tt("dve", A2, Ht2, HB[:, tsl(jj)], ALU.add, k32 + hk2, k22)
                    ob, okk = tmph[d][bb], [("th%d" % d, bb)]
                    tt("pool", ob[:], A2, THr2, ALU.mult, k22 + k02, okk)
                    dma("sp", mixed0[cs, tsl(jj)], ob[:], okk, [("mixed0", c, jj)], "ya%d_%d" % (d, bb))
                prev = (jj, st)
            del pend[:]
            yield

    def run_rr(gens):
        live = list(gens)
        while live:
            for g in list(live):
                try:
                    next(g)
                except StopIteration:
                    live.remove(g)

    run_rr([rgA_front(0)])
    rgA_conv(0)
    for c in range(8):
        gens = [rg_dir(1, c), rg_dir(0, c)]
        if c + 1 < 8:
            gens.append(rgA_front(c + 1))
        run_rr(gens)
        if c + 1 < 8:
            rgA_conv(c + 1)

    Pb, GB = big[0], big[1]
    SZB, YB = bigh[0], bigh[1]
    P.add("pool", lambda e: e.memset(Pb[:, 0:1], 0.0), w=kb(0, 0, 1))
    P.add("pool", lambda e: e.memset(Pb[:, SEQ + 1:SEQ + 2], 0.0), w=kb(0, SEQ + 1, SEQ + 2))
    for c in range(8):
        s = c % 2
        wB = wb[1][:, :, s * 512:(s + 1) * 512]
        for q in range(4):
            dma("pool", wB[:, :, q * 128:(q + 1) * 128], w0[:, :, (2 + q) * 1024 + c * 128:(2 + q) * 1024 + (c + 1) * 128],
                [], [("wB", s, q)] + [("wbf", 1, n) for n in range(8)], "wB%d_%d" % (s, q))
        for j in range(NT):
            b = j % 2
            for q in range(4):
                for k in range(8):
                    mm(pb[q][:], wB[:, k, q * 128:(q + 1) * 128], hT[:, k, tsl(j)], k == 0, k == 7,
                       [("wB", s, q)] + khT(k, j * T, j * T + T), [("pb", q)])
            xbs, xbk = bslot(2, b)
            act(xbs, pb[0][:], AF.Copy, [("pb", 0)], xbk)
            tt("dve", Pb[:, 1 + j * T:1 + (j + 1) * T], pb[2][:], xbs, ALU.mult, [("pb", 2)] + xbk, kb(0, 1 + j * T, 1 + (j + 1) * T))
            act(GB[:, tsl(j)], pb[1][:], AF.Copy, [("pb", 1)], kb(1, j * T, j * T + T))
            szk = kh(0, j * T, j * T + T)
            act(SZB[:, tsl(j)], pb[3][:], AF.Tanh, [("pb", 3)], szk, scale=0.5)
            stt("dve", SZB[:, tsl(j)], SZB[:, tsl(j)], 1.0, pb[3][:], ALU.add, ALU.mult, szk + [("pb", 3)], szk)
            for jj in ([j - 1] if j >= 1 else []) + ([j] if j == NT - 1 else []):
                cvt, cvk = bslot(3, jj % 2)
                lo = jj * T
                ts("dve", cvt, Pb[:, lo:lo + T], pv[:, 13, c:c + 1], None, ALU.mult, None, kb(0, lo, lo + T) + ["pv"], cvk)
                for kk in range(1, 3):
                    stt("dve", cvt, Pb[:, lo + kk:lo + kk + T], pv[:, 13 + kk, c:c + 1], cvt, ALU.mult, ALU.add,
                        kb(0, lo + kk, lo + kk + T) + cvk + ["pv"], cvk)
                stt("dve", cvt, cvt, 0.5, GB[:, tsl(jj)], ALU.mult, ALU.mult, cvk + kb(1, lo, lo + T), cvk)
                tt("pool", YB[:, tsl(jj)], cvt, SZB[:, tsl(jj)], ALU.mult, cvk + kh(0, lo, lo + T), kh(1, lo, lo + T))
        dma("sp", mixed0[1024 + c * 128:1024 + (c + 1) * 128, :], YB[:, 0:SEQ], kh(1, 0, SEQ), [("mixed0", 8 + c, j) for j in range(NT)], "yb")

    P.barrier()
    m0v = mixed0.rearrange("(k p) t -> p k t", p=128)
    mtb = [bigh[i][:, 0:8 * T].rearrange("p (k t) -> p k t", t=T) for i in range(2)]

    def mt_get0(j):
        b = j % 2
        lo = mtb[b]
        hi = hT[:, :, b * T:(b + 1) * T]
        rk = [("mixed0", c, j) for c in range(16)]
        dma("sp", lo, m0v[:, 0:8, tsl(j)], rk, [("mtlo", b)], "mt%d" % b)
        dma("sp", hi, m0v[:, 8:16, tsl(j)], rk, [("mthi", b)], "mth%d" % b)
        return (lambda k: lo[:, k, :] if k < 8 else hi[:, k - 8, :]), [("mtlo", b), ("mthi", b)]

    out_phase(mt_get0, 16, w_out0, 1, xT, x1T if layers > 1 else outT, "x0", hT[:, :, 2 * T:3 * T], ["sqh"])
    if layers == 1:
        P.emit(nc, es)
        es.close()
        return nc, P

    P.barrier()
    norm_phase(x1T, 16, "x0o")
    w1 = w_in1.rearrange("(k p) n -> p k n", p=128)
    LRs = nc.dram_tensor("LRs", [2, 16, SEQ], BF16, kind="Internal").ap()
    dma("pool", lrw[:], w1[:, :, 3072:3104], [], ["lrw"], "c8")
    dma("pool", wgt[:], wg.rearrange("d r k -> r d k"), [], ["wgt"], "c9")

    for q in range(2):
        dma("pool", wb[0][:, :, q * 512:(q + 1) * 512], w1[:, :, q * 512:(q + 1) * 512], [], [("wb", 0, q)], "wb0_%d" % q)
    for oc in range(8):
        dst = qTs if oc < 4 else kTs
        for j in range(NT):
            b = j % 2
            bank = (oc * NT + j) % 4
            for k in range(8):
                mm(pb[bank][:], wb[0][:, k, oc * 128:(oc + 1) * 128], hT[:, k, tsl(j)], k == 0, k == 7,
                   [("wb", 0, oc // 4)] + khT(k, j * T, j * T + T), [("pb", bank)])
            st, stk = bslot(bank, b)
            act(st, pb[bank][:], AF.Copy, [("pb", bank)], stk)
            dma("sp", dst[(oc % 4) * 128:(oc % 4 + 1) * 128, tsl(j)], st, stk, [("qk", oc, j)], "st%d_%d" % (bank, b))
    for q in range(2):
        dma("pool", wb[1][:, :, q * 512:(q + 1) * 512], w1[:, :, 1024 + q * 512:1024 + (q + 1) * 512], [], [("wb", 1, q)], "wb1_%d" % q)
    Vsv = Vs.rearrange("(m p) e -> p m e", p=128)
    Vt = [bigh[i][:, 0:4096].rearrange("p (m e) -> p m e", e=1024) for i in range(2)]
    for j in range(NT):
        b = j % 2
        for m in range(4):
            for half in range(2):
                bank = (m * 2 + half) % 4
                for k in range(8):
                    mm(pb[bank][:], hT[:, k, j * T + m * 128:j * T + (m + 1) * 128], wb[1][:, k, half * 512:(half + 1) * 512],
                       k == 0, k == 7, [("wb", 1, half)] + khT(k, j * T, j * T + T), [("pb", bank)])
                act(Vt[b][:, m, half * 512:(half + 1) * 512], pb[bank][:], AF.Copy, [("pb", bank)], [("Vt", b, m, half)])
        dma("sp", Vsv[:, j * 4:(j + 1) * 4, :], Vt[b], [("Vt", b, m, hf) for m in range(4) for hf in range(2)],
            [("Vs", j)], "vt%d" % b)
    for q in range(2):
        dma("pool", wb[0][:, :, q * 512:(q + 1) * 512], w1[:, :, 2048 + q * 512:2048 + (q + 1) * 512], [], [("wb", 0, q)], "wb0_%d" % q)
    for oc in range(8):
        for j in range(NT):
            b = j % 2
            bank = (oc * NT + j) % 4
            for k in range(8):
                mm(pb[bank][:], wb[0][:, k, oc * 128:(oc + 1) * 128], hT[:, k, tsl(j)], k == 0, k == 7,
                   [("wb", 0, oc // 4)] + khT(k, j * T, j * T + T), [("pb", bank)])
            stg = tmph[bank % 2][b]
            sk = [("th%d" % (bank % 2), b)]
            act(stg[:], pb[bank][:], AF.Tanh, [("pb", bank)], sk, scale=0.5)
            stt("dve", stg[:], stg[:], 1.0, pb[bank][:], ALU.add, ALU.mult, sk + [("pb", bank)], sk)
            dma("sp", SRs[oc * 128:(oc + 1) * 128, tsl(j)], stg[:], sk, [("SRs", oc, j)], "sr%d_%d" % (bank % 2, b))
    for d in range(2):
        for j in range(NT):
            b = j % 2
            bank = 4 + b
            for k in range(8):
                mm(pb[bank][0:16, :], lrw[:, k, d * 16:(d + 1) * 16], hT[:, k, tsl(j)], k == 0, k == 7,
                   ["lrw"] + khT(k, j * T, j * T + T), [("pb", bank)])
            act(tmph[2][b][0:16, :], pb[bank][0:16, :], AF.Copy, [("pb", bank)], [("th2", b)])
            dma("sp", LRs[d, :, tsl(j)], tmph[2][b][0:16, :], [("th2", b)], [("LRs", d, j)], "lrs%d" % b)

    P.barrier()
    SC = 128.0 ** -0.5
    qh, kh_, O = big[0], big[1], [big[2], big[3]]
    HC = NCH // 2
    Vhh = [bigh[i][:, 0:HC * 256].rearrange("p (m e) -> p m e", e=256) for i in range(2)]

    def Vn(n):
        return Vhh[n // HC][:, n % HC, :]

    sq1 = bigh[1][:, 0:8 * T].rearrange("p (k t) -> p k t", t=T)
    for h in range(4):
        dma("sp", qh[:, 0:SEQ], qTs[h * 128:(h + 1) * 128, :], [("qk", h, j) for j in range(NT)], ["qh"], "qh")
        dma("sp", kh_[:, 0:SEQ], kTs[h * 128:(h + 1) * 128, :], [("qk", 4 + h, j) for j in range(NT)], ["kh"], "kh")
        for i in range(2):
            dma("sp", Vhh[i], Vsv[:, i * HC:(i + 1) * HC, h * 256:(h + 1) * 256], [("Vs", j) for j in range(NT)], ["Vh"], "vh%d" % i)
        def gla_dir(d, h=h):
            first = True
            for sidx in range(NT):
                j = sidx if d == 0 else NT - 1 - sidx
                (Lt, k0), (Lc, k1), (E1, k2), (E2, k3) = [wslot(0, i * 2 + d) for i in range(4)]
                QI, KI, KST = tmph[0][d], tmph[1][d], tmph[2][d]
                bz, bo0, bo1, bk = 0 + d, 2 + d, 4 + d, 6 + d
                pz = pbx[bz]
                dma("sp", lrt[d][:], LRs[d, :, tsl(j)], [("LRs", d, j)], [("lrt", d)], "lrt%d" % d)
                mm(pz, wgt[0:16, d, h * 128:(h + 1) * 128], lrt[d][:], True, True, ["wgt", ("lrt", d)], [("pb", bz)])
                yield
                act(Lt, pz, AF.Exp, [("pb", bz), "negb"], k0, bias=negb[:, d, h:h + 1], scale=-1.0)
                act(Lt, Lt, AF.Ln, k0, k0, bias=1.0)
                yield
                scan(Lc, m01[:, :], Lt, 0.0, d == 1, k0 + ["m01"], k1)
                yield
                tot = Lc[:, T - 1:T] if d == 0 else Lc[:, 0:1]
                act(E1, Lc, AF.Exp, k1, k2, scale=-1.0 / 16)
                act(E2, Lc, AF.Exp, k1, k3, scale=1.0 / 16)
                act(DEC[d][:, j, :], tot, AF.Exp, k1, [("DEC", d, j)], scale=-1.0 / 16)
                yield
                stt("dve", QI[:], qh[:, tsl(j)], SC, E1, ALU.mult, ALU.mult, ["qh"] + k2, [("th0", d)])
                tt("pool", KI[:], kh_[:, tsl(j)], E2, ALU.mult, ["kh"] + k3, [("th1", d)])
                yield
                stt("dve", KST[:], kh_[:, tsl(j)], DEC[d][:, j, :], E2, ALU.mult, ALU.mult, ["kh", ("DEC", d, j)] + k3, [("th2", d)])
                yield
                kt_ps = (pb6h if d == 0 else pbt)[:, 0:T]
                kv_ps = pbx[bk][:, 256:512]
                jcs = list(range(4)) if d == 0 else list(range(3, -1, -1))
                SM = [wb[1][:, 2 + (d * 4 + jc) // 2, ((d * 4 + jc) % 2) * T:((d * 4 + jc) % 2 + 1) * T] for jc in range(4)]
                for jc in range(4):
                    P.add("pe", lambda e, o=kt_ps[:, jc * 128:(jc + 1) * 128], i=KST[:, jc * 128:(jc + 1) * 128]:
                          e.transpose(out=o, in_=i, identity=ident[:]), r=[("th2", d), "ident"], w=[("pb", bk)])
                yield
                act(KTt[d][:].rearrange("p c t -> p (c t)"), kt_ps, AF.Copy, [("pb", bk)], [("KT", d)])
                yield
                for jc in jcs:
                    if d == 0:
                        isl, msl = slice(jc * 128, T), tri[:, 0, 0:(4 - jc) * 128]
                    else:
                        isl, msl = slice(0, (jc + 1) * 128), tri[:, 1, (3 - jc) * 128:T]
                    ncol = isl.stop - isl.start
                    sb_, sbank = (pz, bz) if jc % 2 == 0 else (pbx[bk], bk)
                    mm(sb_[:, 0:ncol], KI[:, jc * 128:(jc + 1) * 128], QI[:, isl], True, True, [("th1", d), ("th0", d)], [("pb", sbank)])
                    yield
                    tt("dve", SM[jc][:, 0:ncol], sb_[:, 0:ncol], msl, ALU.mult, [("pb", sbank), "tri"], [("SM", d, jc)])
                    yield
                for jc in range(4):
                    mm(kv_ps, KTt[d][:, jc, :], Vn(j * 4 + jc), jc == 0, jc == 3, [("KT", d), "Vh"], [("pb", bk)])
                yield
                for ec, bo in ((0, bo0), (1, bo1)):
                    for idx, jc in enumerate(jcs):
                        isl = slice(jc * 128, T) if d == 0 else slice(0, (jc + 1) * 128)
                        ncol = isl.stop - isl.start
                        mm(pbx[bo][:, isl], Vn(j * 4 + jc)[:, ec * 128:(ec + 1) * 128], SM[jc][:, 0:ncol], idx == 0, first and idx == 3,
                           ["Vh", ("SM", d, jc)], [("pb", bo)])
                    if not first:
                        mm(pbx[bo], Sbf[d][:, ec * 128:(ec + 1) * 128], QI[:], False, True, [("Sb", d), ("th0", d)], [("pb", bo)])
                yield
                if first:
                    P.add("dve", lambda e, o=Sst[d][:], i=kv_ps: e.tensor_copy(out=o, in_=i), r=[("pb", bk)], w=[("S", d)])
                else:
                    stt("dve", Sst[d][:], Sst[d][:], DEC[d][:, j, :], kv_ps, ALU.mult, ALU.add,
                        [("S", d), ("DEC", d, j), ("pb", bk)], [("S", d)])
                yield
                act(Sbf[d][:], Sst[d][:], AF.Copy, [("S", d)], [("Sb", d)])
                firstO = (d == 0) == (2 * j <= NT - 1)
                okj = [[("O", ec, n) for n in range(j * 4, j * 4 + 4)] for ec in range(2)]
                for ec, bo in ((0, bo0), (1, bo1)):
                    if firstO:
                        act(O[ec][:, tsl(j)], pbx[bo], AF.Copy, [("pb", bo)], okj[ec])
                    else:
                        tt("dve", O[ec][:, tsl(j)], O[ec][:, tsl(j)], pbx[bo], ALU.add, [("pb", bo)] + okj[ec], okj[ec])
                first = False
                yield

        g0, g1 = gla_dir(0), gla_dir(1)
        for _ in range(10):
            next(g0)
        run_rr([g0, g1])
        for j in range(NT):
            b = j % 2
            okeys = [[("O", ec, n) for n in range(j * 4, j * 4 + 4)] for ec in range(2)]
            for ec in range(2):
                dma("sp", tmph[ec][b][:], SRs[(2 * h + ec) * 128:(2 * h + ec + 1) * 128, tsl(j)],
                    [("SRs", 2 * h + ec, j)], [("th%d" % ec, b)], "srl%d_%d" % (ec, b))
                act(sqs[:, ec, :], O[ec][:, tsl(j)], AF.Square, okeys[ec], [("sq1", ec)])
            for ec in range(2):
                mm(pb[6][:], ones[:], sqs[:, ec, :], ec == 0, ec == 1, ["ones", ("sq1", ec)], [("pb", 6)])
            rstd_from(pb[6][:], b, 256, mul=0.5)
            for ec in range(2):
                t4, t4k = wslot(1, ec)
                stt("dve", t4, O[ec][:, tsl(j)], pngt[:, ec:ec + 1], rs[b][:], ALU.mult, ALU.mult,
                    okeys[ec] + [("rs", b), "pngt"], t4k)
                tt("pool", hT[:, 2 * h + ec, tsl(j)], t4, tmph[ec][b][:], ALU.mult,
                   t4k + [("th%d" % ec, b)], khT(2 * h + ec, j * T, j * T + T))

    P.barrier()

    def mt_get1(j):
        return (lambda k: hT[:, k, tsl(j)]), [("hT", k, j) for k in range(8)]

    out_phase(mt_get1, 8, w_out1, 17, x1T, outT, "x0o", sq1, ["sq1all"])
    P.emit(nc, es)
    es.close()
    return nc, P


def host_inputs(x_b, inp):
    f = lambda a: np.ascontiguousarray(np.asarray(a, dtype=np.float32))
    vecs = [inp["even_norm_pre"][0], inp["even_norm_post"][0]] + [inp["rg_conv_w"][0, i] for i in range(4)] + \
           [inp["rg_conv_b"][0]] + [inp["rg_gate_b"][0, d, g].reshape(-1) for d in range(2) for g in range(2)] + \
           [inp["rg_lambda"][0, d] for d in range(2)] + [inp["sc_conv_w"][0, i] for i in range(3)] + \
           [inp["odd_norm_pre"][0], inp["odd_norm_post"][0]]
    pvec = f(np.stack([np.asarray(v) for v in vecs], 0).reshape(NV, 8, 128).transpose(2, 0, 1))
    pg = f(np.asarray(inp["gla_b_gate"][0]).reshape(2, 4, 128).transpose(2, 0, 1))
    png = f(np.asarray(inp["gla_norm_g"][0]).reshape(2, 128).transpose(1, 0))
    cm01 = np.ones((128, T), np.float32)
    jj, ii = np.meshgrid(np.arange(128), np.arange(128), indexing="ij")
    mf = np.ones((128, T), np.float32)
    mf[:, 0:128] = (jj <= ii)
    mb = np.ones((128, T), np.float32)
    mb[:, T - 128:T] = (jj >= ii)
    ctri = f(np.stack([mf, mb], 1))
    return {
        "xT": f(np.asarray(x_b).T), "w_in0": f(inp["even_w_in"][0]), "gate_w": f(inp["rg_gate_w"][0]),
        "w_out0": f(inp["even_w_out"][0]), "w_in1": f(inp["odd_w_in"][0]), "wg": f(inp["gla_w_gate_lr"][0]),
        "w_out1": f(inp["odd_w_out"][0]), "pvec": pvec, "pg": pg, "png": png, "cm01": cm01, "ctri": ctri,
        "cones": np.ones((128, 128), np.float32), "cident": np.eye(128, dtype=np.float32),
    }


def kernel(**inputs):
    x = np.asarray(inputs["x"])
    B, SEQ, _ = x.shape
    nc, _ = build_nc(SEQ)
    shared = None
    in_maps = []
    for b in range(B):
        m = host_inputs(x[b], inputs)
        if shared is None:
            shared = m
        else:
            for k in m:
                if k != "xT":
                    m[k] = shared[k]
        in_maps.append(m)
    res = run_bass_kernel_spmd(nc, in_maps, core_ids=list(range(B)))
    out = np.stack([np.asarray(r["outT"]).T for r in res.results], 0)
    return np.ascontiguousarray(out.astype(np.float32))
```

```python
import numpy as np
from contextlib import ExitStack
import concourse.bass as bass
import concourse.mybir as mybir
from concourse.bass_utils import run_bass_kernel_spmd
from concourse.ap import AP

F32 = mybir.dt.float32
BF16 = mybir.dt.bfloat16
AF = mybir.ActivationFunctionType
ALU = mybir.AluOpType

D = 1024
T = 512
EPS = 1e-6
NV = 18
LAYERS = 2
SBUF_LEFT = []


def rev(ap):
    pairs = [list(p) for p in ap.ap]
    step, cnt = pairs[-1]
    pairs[-1] = [-step, cnt]
    return AP(ap.tensor, ap.offset + step * (cnt - 1), pairs)


class _Op:
    __slots__ = ("eng", "fn", "deps", "dma_key", "signal", "val", "idx", "semkey")


class Prog:
    def __init__(self):
        self.ops = []
        self.lastw = {}
        self.rd_eng = {}
        self.rd_dma = {}
        self.last_eng = {}
        self.last_dma = {}
        self.fence = []
        self.passed = set()

    def barrier(self):
        self.fence = list(self.last_eng.values()) + list(self.last_dma.values())
        self.passed = set()

    def add(self, eng, fn, r=(), w=(), dma_key=None):
        op = _Op()
        op.eng, op.fn, op.dma_key = eng, fn, dma_key
        op.signal = dma_key is not None
        op.val = 0
        op.semkey = None
        op.idx = len(self.ops)
        deps = {}
        if self.fence and eng not in self.passed:
            for p in self.fence:
                deps[p.idx] = p
            self.passed.add(eng)
        for k in r:
            p = self.lastw.get(k)
            if p is not None:
                deps[p.idx] = p
        for k in w:
            p = self.lastw.get(k)
            if p is not None:
                deps[p.idx] = p
            for q in self.rd_eng.get(k, {}).values():
                deps[q.idx] = q
            for q in self.rd_dma.get(k, ()):
                deps[q.idx] = q
        op.deps = [p for p in deps.values() if (p.dma_key is not None) or (p.eng != eng) or (eng != "pe")]
        for k in r:
            if dma_key is not None:
                self.rd_dma.setdefault(k, []).append(op)
            else:
                self.rd_eng.setdefault(k, {})[eng] = op
        for k in w:
            self.lastw[k] = op
            self.rd_eng[k] = {}
            self.rd_dma[k] = []
        if dma_key is not None:
            self.last_dma[dma_key] = op
        else:
            self.last_eng[eng] = op
        self.ops.append(op)
        return op

    def emit(self, nc, es):
        for op in self.ops:
            for p in op.deps:
                p.signal = True
        cnt = {}
        for op in self.ops:
            if op.signal:
                key = ("dma", op.dma_key) if op.dma_key is not None else ("eng", op.eng)
                cnt[key] = cnt.get(key, 0) + (16 if op.dma_key is not None else 1)
                op.semkey, op.val = key, cnt[key]
        sems = {}
        for i, key in enumerate(cnt):
            sems[key] = es.enter_context(nc.semaphore("s%d" % i))
        self.nsem = len(sems)
        block = es.enter_context(nc.Block())
        ops = self.ops

        def run(engname, final=False):
            def body(e):
                waited = {}
                for op in ops:
                    if op.eng != engname:
                        continue
                    for p in op.deps:
                        if waited.get(p.semkey, 0) < p.val:
                            e.wait_ge(sems[p.semkey], p.val)
                            waited[p.semkey] = p.val
                    ins = op.fn(e)
                    if op.signal:
                        ins.then_inc(sems[op.semkey], 16 if op.dma_key is not None else 1)
                if final:
                    for key, v in cnt.items():
                        if waited.get(key, 0) < v:
                            e.wait_ge(sems[key], v)
            return body

        block.sync(run("sp", final=True))
        block.tensor(run("pe"))
        block.scalar(run("act"))
        block.vector(run("dve"))
        block.gpsimd(run("pool"))


def build_nc(SEQ, layers=LAYERS):
    NT = SEQ // T
    NCH = SEQ // 128
    BW = max(SEQ, 8 * T)
    nc = bass.Bass("TRN2", target_bir_lowering=False)
    din = lambda n, s, dt=F32: nc.dram_tensor(n, s, dt, kind="ExternalInput").ap()
    xT = din("xT", [D, SEQ])
    w_in0 = din("w_in0", [D, 6144])
    gate_w = din("gate_w", [2, 2, 8, 128, 128])
    w_out0 = din("w_out0", [2048, D])
    w_in1 = din("w_in1", [D, 3104])
    wg = din("wg", [2, 16, 512])
    w_out1 = din("w_out1", [D, D])
    pvec = din("pvec", [128, NV, 8])
    pg = din("pg", [128, 2, 4])
    png = din("png", [128, 2])
    cm01 = din("cm01", [128, T])
    ctri = din("ctri", [128, 2, T])
    cones = din("cones", [128, 128])
    cident = din("cident", [128, 128])
    outT = nc.dram_tensor("outT", [D, SEQ], F32, kind="ExternalOutput").ap()
    mixed0 = nc.dram_tensor("mixed0", [2048, SEQ], BF16, kind="Internal").ap()
    x1T = nc.dram_tensor("x1T", [D, SEQ], F32, kind="Internal").ap()
    qTs = nc.dram_tensor("qTs", [512, SEQ], F32, kind="Internal").ap()
    kTs = nc.dram_tensor("kTs", [512, SEQ], F32, kind="Internal").ap()
    Vs = nc.dram_tensor("Vs", [SEQ, D], BF16, kind="Internal").ap()
    SRs = nc.dram_tensor("SRs", [D, SEQ], BF16, kind="Internal").ap()

    es = ExitStack()
    P = Prog()
    sb = lambda n, s, dt=F32: es.enter_context(nc.sbuf_tensor(n, s, dt))
    ps = lambda n, s, dt=F32: es.enter_context(nc.psum_tensor(n, s, dt))

    hT = sb("hT", [128, 8, SEQ], BF16)
    pv = sb("pv", [128, NV, 8])
    pgt = sb("pgt", [128, 2, 4])
    pngt = sb("pngt", [128, 2])
    m01 = sb("m01", [128, T], BF16)
    tri = sb("tri", [128, 2, T])
    ones = sb("ones", [128, 128], BF16)
    ident = sb("ident", [128, 128], BF16)
    der = sb("der", [128, 12, 8])
    negb = sb("negb", [128, 2, 4])
    gwh = [sb("gwh%d" % i, [128, 4, 128], BF16) for i in range(2)]
    lrw = sb("lrw", [128, 8, 32], BF16)
    wgt = sb("wgt", [16, 2, 512], BF16)
    lrt = [sb("lrt%d" % d, [16, T], BF16) for d in range(2)]
    rs = [sb("rs%d" % i, [128, T]) for i in range(2)]
    wb = [sb("wb%d" % i, [128, 8, 1024], BF16) for i in range(2)]
    wbf = [wb[i][:].rearrange("p k n -> p (k n)").bitcast(F32) for i in range(2)]
    big = [sb("big%d" % i, [128, BW + 4]) for i in range(4)]
    bigh = [sb("bigh%d" % i, [128, BW], BF16) for i in range(2)]
    tmph = [[sb("tmph%d_%d" % (i, j), [128, T], BF16) for j in range(2)] for i in range(3)]
    zero1 = sb("zero1", [128, 1])
    Sst = [sb("Sst%d" % d, [128, 256]) for d in range(2)]
    Sbf = [sb("Sbf%d" % d, [128, 256], BF16) for d in range(2)]
    DEC = [sb("DEC%d" % d, [128, NT, 1]) for d in range(2)]
    KTt = [sb("KTt%d" % d, [128, 4, 128], BF16) for d in range(2)]
    xt = [big[2 + i][:, 0:8 * T].rearrange("p (k t) -> p k t", t=T) for i in range(2)]
    sqs = sb("sqs", [128, 2, T], BF16)

    SBUF_LEFT.append(nc.sbuf_bytes_remaining)
    pb = [ps("pb%d" % i, [128, T]) for i in range(7)]
    pbt = ps("pbt", [128, 1024], BF16)
    pbx = [pb[i][:] for i in range(7)] + [pbt[:].bitcast(F32)]
    pb6h = pb[6][:].bitcast(BF16)

    def kb(i, lo, hi):
        return [("big", i, g) for g in range(lo // T, (hi - 1) // T + 1)]

    def kh(i, lo, hi):
        return [("bigh", i, g) for g in range(lo // T, (hi - 1) // T + 1)]

    def khT(k, lo, hi):
        return [("hT", k, g) for g in range(lo // T, (hi - 1) // T + 1)]

    def bslot(i, n):
        return big[i][:, n * T:(n + 1) * T], [("big", i, n)]

    def wslot(i, n):
        return wbf[i][:, n * T:(n + 1) * T], [("wbf", i, n)]

    def bcast_last(ap, n):
        pairs = [list(p) for p in ap.ap]
        assert pairs[-1][1] == 1
        pairs[-1] = [0, n]
        return AP(ap.tensor, ap.offset, pairs)

    def dma(eng, out, in_, r, w, key):
        P.add(eng, lambda e: e.dma_start(out=out, in_=in_), r=r, w=w, dma_key=key)

    def act(out, in_, func, r, w, bias=0.0, scale=1.0):
        P.add("act", lambda e: e.activation(out=out, in_=in_, func=func, bias=bias, scale=scale), r=r, w=w)

    def tt(eng, out, in0, in1, op, r, w):
        P.add(eng, lambda e: e.tensor_tensor(out=out, in0=in0, in1=in1, op=op), r=r, w=w)

    def ts(eng, out, in0, s1, s2, op0, op1, r, w):
        if op1 is None:
            P.add(eng, lambda e: e.tensor_scalar(out=out, in0=in0, scalar1=s1, scalar2=None, op0=op0), r=r, w=w)
        else:
            P.add(eng, lambda e: e.tensor_scalar(out=out, in0=in0, scalar1=s1, scalar2=s2, op0=op0, op1=op1), r=r, w=w)

    def stt(eng, out, in0, s, in1, op0, op1, r, w):
        P.add(eng, lambda e: e.scalar_tensor_tensor(out=out, in0=in0, scalar=s, in1=in1, op0=op0, op1=op1), r=r, w=w)

    def mm(out, lhsT, rhs, start, stop, r, w):
        P.add("pe", lambda e: e.matmul(out, lhsT=lhsT, rhs=rhs, start=start, stop=stop), r=r, w=w)

    def scan(out, d0, d1, init, reverse, r, w):
        if reverse:
            P.add("dve", lambda e: e.tensor_tensor_scan(out=rev(out), data0=rev(d0), data1=rev(d1), initial=init,
                                                        op0=ALU.mult, op1=ALU.add), r=r, w=w)
        else:
            P.add("dve", lambda e: e.tensor_tensor_scan(out=out, data0=d0, data1=d1, initial=init,
                                                        op0=ALU.mult, op1=ALU.add), r=r, w=w)

    def tsl(j):
        return slice(j * T, (j + 1) * T)

    xtk = [kb(2 + i, 0, 8 * T) for i in range(2)]

    dma("sp", pv[:], pvec[:, :, :], [], ["pv"], "c0")
    dma("sp", pgt[:], pg[:, :, :], [], ["pgt"], "c1")
    dma("sp", pngt[:], png[:, :], [], ["pngt"], "c2")
    dma("pool", m01[:], cm01[:, :], [], ["m01"], "c3")
    dma("sp", tri[:], ctri[:, :, :], [], ["tri"], "c4")
    dma("pool", ones[:], cones[:, :], [], ["ones"], "c5")
    dma("pool", ident[:], cident[:, :], [], ["ident"], "c6")
    P.add("dve", lambda e: e.memset(zero1[:], 0.0), w=["zero1"])
    ts("dve", der[:, 0:4, :], pv[:, 7:11, :], 0.5, None, ALU.mult, None, ["pv"], ["der"])
    act(der[:, 8:10, :], pv[:, 11:13, :], AF.Exp, ["pv"], ["der8"], scale=-1.0)
    act(der[:, 10:12, :], der[:, 8:10, :], AF.Ln, ["der8"], ["der10"], bias=1.0)
    ts("dve", der[:, 4:6, :], der[:, 10:12, :], -4.0, None, ALU.mult, None, ["der10"], ["der"])
    ts("dve", der[:, 6:8, :], der[:, 10:12, :], -8.0, None, ALU.mult, None, ["der10"], ["der"])
    ts("dve", negb[:], pgt[:], -1.0, None, ALU.mult, None, ["pgt"], ["negb"])

    def rstd_from(psum_ap, b, n, mul=1.0):
        act(rs[b][:], psum_ap, AF.Ln, [("pb", 6)], [("rs", b)], bias=EPS, scale=1.0 / n)
        act(rs[b][:], rs[b][:], AF.Exp, [("rs", b)], [("rs", b)], scale=-0.5, bias=float(np.log(mul)))

    def norm_phase(src, gidx, tag):
        srcv = src.rearrange("(k p) t -> p k t", p=128)
        sq = bigh[1][:, 0:8 * T].rearrange("p (k t) -> p k t", t=T)
        sqk = kh(1, 0, 8 * T)
        for j in range(NT):
            b = j % 2
            dma("sp", xt[b], srcv[:, :, tsl(j)], [("src", tag, j)], xtk[b], "xt%d" % b)
            act(sq, xt[b], AF.Square, xtk[b], sqk)
            for k in range(8):
                mm(pb[6][:], ones[:], sq[:, k, :], k == 0, k == 7, ["ones"] + sqk, [("pb", 6)])
            rstd_from(pb[6][:], b, D)
            for k in range(8):
                stt("dve", hT[:, k, tsl(j)], xt[b][:, k, :], pv[:, gidx, k:k + 1], rs[b][:], ALU.mult, ALU.mult,
                    xtk[b] + [("rs", b), "pv"], khT(k, j * T, (j + 1) * T))

    def out_phase(mt_get, nk, wsrc, gidx, resid, dst, tag, sq, sqk):
        wo = wsrc.rearrange("(k p) n -> p k n", p=128)
        for k0 in range(0, nk, 4):
            s, q = k0 // 8, (k0 % 8) // 4
            dma("pool", wb[s][:, q * 4:q * 4 + 4, :], wo[:, k0:k0 + 4, :], [], [("wb", s, q)], "wb%d_%d" % (s, q))
        resv = resid.rearrange("(k p) t -> p k t", p=128)
        dstv = dst.rearrange("(k p) t -> p k t", p=128)
        ysb = [big[i][:, 0:8 * T].rearrange("p (k t) -> p k t", t=T) for i in range(2)]
        def loads(j):
            bq = j % 2
            r_ = mt_get(j)
            dma("sp", xt[bq], resv[:, :, tsl(j)], [("src", tag, j)], xtk[bq], "xt%d" % bq)
            return r_

        pending = loads(0)
        for j in range(NT):
            b = j % 2
            ys = ysb[b]
            mget, mkeys = pending
            if j + 1 < NT:
                pending = loads(j + 1)
            for o in range(8):
                bank = o % 4
                for k in range(nk):
                    mm(pb[bank][:], wb[k // 8][:, k % 8, o * 128:(o + 1) * 128], mget(k), k == 0, k == nk - 1,
                       [("wb", k // 8, (k % 8) // 4)] + mkeys, [("pb", bank)])
                act(ys[:, o, :], pb[bank][:], AF.Copy, [("pb", bank)], [("big", b, o)])
                act(sq[:, o, :], pb[bank][:], AF.Square, [("pb", bank)], sqk)
            for o in range(8):
                mm(pb[6][:], ones[:], sq[:, o, :], o == 0, o == 7, ["ones"] + sqk, [("pb", 6)])
            rstd_from(pb[6][:], b, D)
            for o in range(8):
                stt("dve", ys[:, o, :], ys[:, o, :], pv[:, gidx, o:o + 1], rs[b][:], ALU.mult, ALU.mult,
                    [("big", b, o), ("rs", b), "pv"], [("big", b, o)])
                tt("pool", ys[:, o, :], ys[:, o, :], xt[b][:, o, :], ALU.add, [("big", b, o)] + xtk[b], [("big", b, o)])
            dma("sp", dstv[:, :, tsl(j)], ys, [("big", b, o) for o in range(8)], [("src", tag + "o", j)], "xo%d" % b)

    norm_phase(xT, 0, "x0")
    w0 = w_in0.rearrange("(k p) n -> p k n", p=128)
    XA, UA, HB, CVb = big[0], big[1], big[2], big[3]
    UAb, SZ = bigh[0], bigh[1]
    P.add("pool", lambda e: e.memset(XA[:, 0:2], 0.0), w=kb(0, 0, 2))
    P.add("pool", lambda e: e.memset(XA[:, SEQ + 2:SEQ + 4], 0.0), w=kb(0, SEQ + 2, SEQ + 4))

    def rgA_front(c):
        s = c % 2
        wA = wb[0][:, :, s * 256:(s + 1) * 256]
        dma("pool", wA[:, :, 0:128], w0[:, :, c * 128:(c + 1) * 128], [], [("wA", s, 0)], "wb%d_0" % s)
        dma("pool", wA[:, :, 128:256], w0[:, :, 1024 + c * 128:1024 + (c + 1) * 128], [], [("wA", s, 1)], "wb%d_1" % s)
        dma("pool", gwh[s][:], gate_w[:, :, c].rearrange("d g i j -> i (d g) j"), [], [("gwh", s)], "gw%d" % s)
        yield
        for j in range(NT):
            bank = 6 + (j % 2)
            for k in range(8):
                mm(pbx[bank], wA[:, k, 0:128], hT[:, k, tsl(j)], k == 0, k == 7, [("wA", s, 0)] + khT(k, j * T, j * T + T), [("pb", bank)])
            yield
            act(XA[:, 2 + j * T:2 + (j + 1) * T], pbx[bank], AF.Copy, [("pb", bank)], kb(0, 2 + j * T, 2 + (j + 1) * T))
            yield

    def rgA_conv(c):
        allXA = kb(0, 0, SEQ + 4)
        allUA = kb(1, 0, SEQ)
        ts("dve", UA[:, 0:SEQ], XA[:, 0:SEQ], pv[:, 2, c:c + 1], pv[:, 6, c:c + 1], ALU.mult, ALU.add, allXA + ["pv"], allUA)
        for kk in range(1, 4):
            stt("dve", UA[:, 0:SEQ], XA[:, kk:kk + SEQ], pv[:, 2 + kk, c:c + 1], UA[:, 0:SEQ], ALU.mult, ALU.add, allXA + allUA + ["pv"], allUA)
        hlf = SEQ // 2
        act(UAb[:, 0:hlf], UA[:, 0:hlf], AF.Copy, kb(1, 0, hlf), kh(0, 0, hlf))
        P.add("pool", lambda e: e.tensor_copy(out=UAb[:, hlf:SEQ], in_=UA[:, hlf:SEQ]), r=kb(1, hlf, SEQ), w=kh(0, hlf, SEQ))

    def rg_dir(d, c):
        s = c % 2
        cs = slice(c * 128, (c + 1) * 128)
        wA = wb[0][:, :, s * 256:(s + 1) * 256]

        def slots(b):
            if d == 1:
                return [bslot(3, i * 2 + b) for i in range(4)]
            return [wslot(1, i * 2 + b) for i in range(4)]

        br, bi = (2, 3) if d == 1 else (0, 1)
        g_r = gwh[s][:, d * 2 + 0, :]
        g_i = gwh[s][:, d * 2 + 1, :]
        prev = None
        pend = []
        for t in range(NT):
            j = t if d == 0 else NT - 1 - t
            t_other = NT - 1 - j if d == 0 else j
            store = (t < t_other) or (t == t_other and d == 1)
            b = j % 2
            uk = kh(0, j * T, j * T + T)
            (THr, k0), (THi, k1), (A, k2), (Ht, k3) = slots(b)
            mm(pbx[br], g_r, UAb[:, tsl(j)], True, True, [("gwh", s)] + uk, [("pb", br)])
            act(THr, pbx[br], AF.Tanh, [("pb", br), "der"], k0, bias=der[:, d * 2 + 0, c:c + 1], scale=0.5)
            mm(pbx[bi], g_i, UAb[:, tsl(j)], True, True, [("gwh", s)] + uk, [("pb", bi)])
            act(THi, pbx[bi], AF.Tanh, [("pb", bi), "der"], k1, bias=der[:, d * 2 + 1, c:c + 1], scale=0.5)
            act(A, THr, AF.Exp, k0 + ["der"], k2, bias=der[:, 4 + d, c:c + 1], scale=der[:, 4 + d, c:c + 1])
            tt("dve", THr, A, A, ALU.mult, k2, k0)
            ts("pool", THr, THr, -1.0 / 16, 1.0 / 16, ALU.mult, ALU.add, k0, k0)
            stt("dve", THi, THi, 1.0, UA[:, tsl(j)], ALU.add, ALU.mult, k1 + kb(1, j * T, j * T + T), k1)
            pend.append((j, store))
            if t % 2 == 0 and t + 1 < NT:
                continue
            yield
            for (jj, _) in pend:
                (THr2, k02) = slots(jj % 2)[0]
                act(THr2, THr2, AF.Sqrt, k02, k02)
            yield
            for (jj, st) in pend:
                bb = jj % 2
                (THr2, k02), (THi2, k12), (A2, k22), (Ht2, k32) = slots(bb)
                hk2 = kb(2, jj * T, jj * T + T)
                tt("pool", THi2, THr2, THi2, ALU.mult, k02 + k12, k12)
                if prev is None:
                    init, rk = zero1[:, 0:1], ["zero1"]
                elif prev[1]:
                    col = prev[0] * T + (T - 1 if d == 0 else 0)
                    init, rk = HB[:, col:col + 1], kb(2, col, col + 1)
                else:
                    pslot, pk = slots(prev[0] % 2)[3]
                    init, rk = (pslot[:, T - 1:T] if d == 0 else pslot[:, 0:1]), pk
                if st:
                    scan(HB[:, tsl(jj)], A2, THi2, init, d == 1, k22 + k12 + rk, hk2)
                else:
                    zb = 4 + bb
                    for k in range(8):
                        mm(pbx[zb], wA[:, k, 128:256], hT[:, k, tsl(jj)], k == 0, k == 7,
                           [("wA", s, 1)] + khT(k, jj * T, jj * T + T), [("pb", zb)])
                    scan(Ht2, A2, THi2, init, d == 1, k22 + k12 + rk, k32)
                    act(THr2, pbx[zb], AF.Tanh, [("pb", zb)], k02, scale=0.5)
                    stt("dve", THr2, THr2, 1.0, pbx[zb], ALU.add, ALU.mult, k02 + [("pb", zb)], k02)
                    tt("dve", A2, Ht2, HB[:, tsl(jj)], ALU.add, k32 + hk2, k22)
                    ob, okk = tmph[d][bb], [("th%d" % d, bb)]
                    tt("pool", ob[:], A2, THr2, ALU.mult, k22 + k02, okk)
                    dma("sp", mixed0[cs, tsl(jj)], ob[:], okk, [("mixed0", c, jj)], "ya%d_%d" % (d, bb))
                prev = (jj, st)
            del pend[:]
            yield

    def run_rr(gens):
        live = list(gens)
        while live:
            for g in list(live):
                try:
                    next(g)
                except StopIteration:
                    live.remove(g)

    run_rr([rgA_front(0)])
    rgA_conv(0)
    for c in range(8):
        gens = [rg_dir(1, c), rg_dir(0, c)]
        if c + 1 < 8:
            gens.append(rgA_front(c + 1))
        run_rr(gens)
        if c + 1 < 8:
            rgA_conv(c + 1)

    Pb, GB = big[0], big[1]
    SZB, YB = bigh[0], bigh[1]
    P.add("pool", lambda e: e.memset(Pb[:, 0:1], 0.0), w=kb(0, 0, 1))
    P.add("pool", lambda e: e.memset(Pb[:, SEQ + 1:SEQ + 2], 0.0), w=kb(0, SEQ + 1, SEQ + 2))
    def wB_load(c):
        s = c % 2
        wB = wb[1][:, :, s * 512:(s + 1) * 512]
        for q in range(4):
            dma("pool", wB[:, :, q * 128:(q + 1) * 128], w0[:, :, (2 + q) * 1024 + c * 128:(2 + q) * 1024 + (c + 1) * 128],
                [], [("wB", s, q)] + [("wbf", 1, n) for n in range(8)], "wB%d_%d" % (s, q))

    wB_load(0)
    for c in range(8):
        s = c % 2
        wB = wb[1][:, :, s * 512:(s + 1) * 512]
        if c + 1 < 8:
            wB_load(c + 1)
        for j in range(NT):
            b = j % 2
            for q in range(4):
                for k in range(8):
                    mm(pb[q][:], wB[:, k, q * 128:(q + 1) * 128], hT[:, k, tsl(j)], k == 0, k == 7,
                       [("wB", s, q)] + khT(k, j * T, j * T + T), [("pb", q)])
            xbs, xbk = bslot(2, b)
            act(xbs, pb[0][:], AF.Copy, [("pb", 0)], xbk)
            tt("dve", Pb[:, 1 + j * T:1 + (j + 1) * T], pb[2][:], xbs, ALU.mult, [("pb", 2)] + xbk, kb(0, 1 + j * T, 1 + (j + 1) * T))
            act(GB[:, tsl(j)], pb[1][:], AF.Copy, [("pb", 1)], kb(1, j * T, j * T + T))
            szk = kh(0, j * T, j * T + T)
            act(SZB[:, tsl(j)], pb[3][:], AF.Tanh, [("pb", 3)], szk, scale=0.5)
            stt("dve", SZB[:, tsl(j)], SZB[:, tsl(j)], 1.0, pb[3][:], ALU.add, ALU.mult, szk + [("pb", 3)], szk)
            for jj in ([j - 1] if j >= 1 else []) + ([j] if j == NT - 1 else []):
                cvt, cvk = bslot(3, jj % 2)
                lo = jj * T
                ts("dve", cvt, Pb[:, lo:lo + T], pv[:, 13, c:c + 1], None, ALU.mult, None, kb(0, lo, lo + T) + ["pv"], cvk)
                for kk in range(1, 3):
                    stt("dve", cvt, Pb[:, lo + kk:lo + kk + T], pv[:, 13 + kk, c:c + 1], cvt, ALU.mult, ALU.add,
                        kb(0, lo + kk, lo + kk + T) + cvk + ["pv"], cvk)
                stt("dve", cvt, cvt, 0.5, GB[:, tsl(jj)], ALU.mult, ALU.mult, cvk + kb(1, lo, lo + T), cvk)
                tt("pool", YB[:, tsl(jj)], cvt, SZB[:, tsl(jj)], ALU.mult, cvk + kh(0, lo, lo + T), kh(1, lo, lo + T))
        dma("sp", mixed0[1024 + c * 128:1024 + (c + 1) * 128, :], YB[:, 0:SEQ], kh(1, 0, SEQ), [("mixed0", 8 + c, j) for j in range(NT)], "yb")

    P.barrier()
    m0v = mixed0.rearrange("(k p) t -> p k t", p=128)
    mtb = [bigh[i][:, 0:8 * T].rearrange("p (k t) -> p k t", t=T) for i in range(2)]

    def mt_get0(j):
        b = j % 2
        lo = mtb[b]
        hi = hT[:, :, b * T:(b + 1) * T]
        rk = [("mixed0", c, j) for c in range(16)]
        dma("sp", lo, m0v[:, 0:8, tsl(j)], rk, [("mtlo", b)], "mt%d" % b)
        dma("sp", hi, m0v[:, 8:16, tsl(j)], rk, [("mthi", b)], "mth%d" % b)
        return (lambda k: lo[:, k, :] if k < 8 else hi[:, k - 8, :]), [("mtlo", b), ("mthi", b)]

    out_phase(mt_get0, 16, w_out0, 1, xT, x1T if layers > 1 else outT, "x0", hT[:, :, 2 * T:3 * T], ["sqh"])
    if layers == 1:
        P.emit(nc, es)
        es.close()
        return nc, P

    P.barrier()
    norm_phase(x1T, 16, "x0o")
    w1 = w_in1.rearrange("(k p) n -> p k n", p=128)
    LRs = nc.dram_tensor("LRs", [2, 16, SEQ], BF16, kind="Internal").ap()
    dma("pool", lrw[:], w1[:, :, 3072:3104], [], ["lrw"], "c8")
    dma("pool", wgt[:], wg.rearrange("d r k -> r d k"), [], ["wgt"], "c9")

    for q in range(2):
        dma("pool", wb[0][:, :, q * 512:(q + 1) * 512], w1[:, :, q * 512:(q + 1) * 512], [], [("wb", 0, q)], "wb0_%d" % q)
    for oc in range(8):
        dst = qTs if oc < 4 else kTs
        for j in range(NT):
            b = j % 2
            bank = (oc * NT + j) % 4
            for k in range(8):
                mm(pb[bank][:], wb[0][:, k, oc * 128:(oc + 1) * 128], hT[:, k, tsl(j)], k == 0, k == 7,
                   [("wb", 0, oc // 4)] + khT(k, j * T, j * T + T), [("pb", bank)])
            st, stk = bslot(bank, b)
            act(st, pb[bank][:], AF.Copy, [("pb", bank)], stk)
            dma("sp", dst[(oc % 4) * 128:(oc % 4 + 1) * 128, tsl(j)], st, stk, [("qk", oc, j)], "st%d_%d" % (bank, b))
    for q in range(2):
        dma("pool", wb[1][:, :, q * 512:(q + 1) * 512], w1[:, :, 1024 + q * 512:1024 + (q + 1) * 512], [], [("wb", 1, q)], "wb1_%d" % q)
    Vsv = Vs.rearrange("(m p) e -> p m e", p=128)
    Vt = [bigh[i][:, 0:4096].rearrange("p (m e) -> p m e", e=1024) for i in range(2)]
    for j in range(NT):
        b = j % 2
        for m in range(4):
            for half in range(2):
                bank = (m * 2 + half) % 4
                for k in range(8):
                    mm(pb[bank][:], hT[:, k, j * T + m * 128:j * T + (m + 1) * 128], wb[1][:, k, half * 512:(half + 1) * 512],
                       k == 0, k == 7, [("wb", 1, half)] + khT(k, j * T, j * T + T), [("pb", bank)])
                act(Vt[b][:, m, half * 512:(half + 1) * 512], pb[bank][:], AF.Copy, [("pb", bank)], [("Vt", b, m, half)])
        dma("sp", Vsv[:, j * 4:(j + 1) * 4, :], Vt[b], [("Vt", b, m, hf) for m in range(4) for hf in range(2)],
            [("Vs", j)], "vt%d" % b)
    for q in range(2):
        dma("pool", wb[0][:, :, q * 512:(q + 1) * 512], w1[:, :, 2048 + q * 512:2048 + (q + 1) * 512], [], [("wb", 0, q)], "wb0_%d" % q)
    for oc in range(8):
        for j in range(NT):
            b = j % 2
            bank = (oc * NT + j) % 4
            for k in range(8):
                mm(pb[bank][:], wb[0][:, k, oc * 128:(oc + 1) * 128], hT[:, k, tsl(j)], k == 0, k == 7,
                   [("wb", 0, oc // 4)] + khT(k, j * T, j * T + T), [("pb", bank)])
            stg = tmph[bank % 2][b]
            sk = [("th%d" % (bank % 2), b)]
            act(stg[:], pb[bank][:], AF.Tanh, [("pb", bank)], sk, scale=0.5)
            stt("dve", stg[:], stg[:], 1.0, pb[bank][:], ALU.add, ALU.mult, sk + [("pb", bank)], sk)
            dma("sp", SRs[oc * 128:(oc + 1) * 128, tsl(j)], stg[:], sk, [("SRs", oc, j)], "sr%d_%d" % (bank % 2, b))
    for d in range(2):
        for j in range(NT):
            b = j % 2
            bank = 4 + b
            for k in range(8):
                mm(pb[bank][0:16, :], lrw[:, k, d * 16:(d + 1) * 16], hT[:, k, tsl(j)], k == 0, k == 7,
                   ["lrw"] + khT(k, j * T, j * T + T), [("pb", bank)])
            act(tmph[2][b][0:16, :], pb[bank][0:16, :], AF.Copy, [("pb", bank)], [("th2", b)])
            dma("sp", LRs[d, :, tsl(j)], tmph[2][b][0:16, :], [("th2", b)], [("LRs", d, j)], "lrs%d" % b)

    P.barrier()
    SC = 128.0 ** -0.5
    qh, kh_, O = big[0], big[1], [big[2], big[3]]
    HC = NCH // 2
    Vhh = [bigh[i][:, 0:HC * 256].rearrange("p (m e) -> p m e", e=256) for i in range(2)]

    def Vn(n):
        return Vhh[n // HC][:, n % HC, :]

    sq1 = bigh[1][:, 0:8 * T].rearrange("p (k t) -> p k t", t=T)
    for h in range(4):
        lorder = []
        for s_ in range(NT):
            for j_ in (s_, NT - 1 - s_):
                if j_ not in lorder:
                    lorder.append(j_)
        for j_ in lorder:
            dma("sp", qh[:, tsl(j_)], qTs[h * 128:(h + 1) * 128, tsl(j_)], [("qk", h, j_)], [("qh", j_)], "qh%d" % j_)
            dma("sp", kh_[:, tsl(j_)], kTs[h * 128:(h + 1) * 128, tsl(j_)], [("qk", 4 + h, j_)], [("kh", j_)], "kh%d" % j_)
            n0 = j_ * 4
            dma("sp", Vhh[n0 // HC][:, n0 % HC:n0 % HC + 4, :], Vsv[:, n0:n0 + 4, h * 256:(h + 1) * 256], [("Vs", j_)], [("Vh", j_)], "vh%d" % j_)
        def gla_dir(d, h=h):
            first = True
            for sidx in range(NT):
                j = sidx if d == 0 else NT - 1 - sidx
                (Lt, k0), (Lc, k1), (E1, k2), (E2, k3) = [wslot(0, i * 2 + d) for i in range(4)]
                QI, KI, KST = tmph[0][d], tmph[1][d], tmph[2][d]
                bz, bo0, bo1, bk = 0 + d, 2 + d, 4 + d, 6 + d
                pz = pbx[bz]
                dma("sp", lrt[d][:], LRs[d, :, tsl(j)], [("LRs", d, j)], [("lrt", d)], "lrt%d" % d)
                mm(pz, wgt[0:16, d, h * 128:(h + 1) * 128], lrt[d][:], True, True, ["wgt", ("lrt", d)], [("pb", bz)])
                yield
                act(Lt, pz, AF.Exp, [("pb", bz), "negb"], k0, bias=negb[:, d, h:h + 1], scale=-1.0)
                act(Lt, Lt, AF.Ln, k0, k0, bias=1.0)
                yield
                scan(Lc, m01[:, :], Lt, 0.0, d == 1, k0 + ["m01"], k1)
                yield
                tot = Lc[:, T - 1:T] if d == 0 else Lc[:, 0:1]
                act(E1, Lc, AF.Exp, k1, k2, scale=-1.0 / 16)
                act(E2, Lc, AF.Exp, k1, k3, scale=1.0 / 16)
                act(DEC[d][:, j, :], tot, AF.Exp, k1, [("DEC", d, j)], scale=-1.0 / 16)
                yield
                stt("dve", QI[:], qh[:, tsl(j)], SC, E1, ALU.mult, ALU.mult, [("qh", j)] + k2, [("th0", d)])
                tt("pool", KI[:], kh_[:, tsl(j)], E2, ALU.mult, [("kh", j)] + k3, [("th1", d)])
                yield
                stt("dve", KST[:], kh_[:, tsl(j)], DEC[d][:, j, :], E2, ALU.mult, ALU.mult, [("kh", j), ("DEC", d, j)] + k3, [("th2", d)])
                yield
                kt_ps = (pb6h if d == 0 else pbt)[:, 0:T]
                kv_ps = pbx[bk][:, 256:512]
                jcs = list(range(4)) if d == 0 else list(range(3, -1, -1))
                SM = [wb[1][:, 2 + (d * 4 + jc) // 2, ((d * 4 + jc) % 2) * T:((d * 4 + jc) % 2 + 1) * T] for jc in range(4)]
                for jc in range(4):
                    P.add("pe", lambda e, o=kt_ps[:, jc * 128:(jc + 1) * 128], i=KST[:, jc * 128:(jc + 1) * 128]:
                          e.transpose(out=o, in_=i, identity=ident[:]), r=[("th2", d), "ident"], w=[("pb", bk)])
                yield
                act(KTt[d][:].rearrange("p c t -> p (c t)"), kt_ps, AF.Copy, [("pb", bk)], [("KT", d)])
                yield
                for jc in jcs:
                    if d == 0:
                        isl, msl = slice(jc * 128, T), tri[:, 0, 0:(4 - jc) * 128]
                    else:
                        isl, msl = slice(0, (jc + 1) * 128), tri[:, 1, (3 - jc) * 128:T]
                    ncol = isl.stop - isl.start
                    sb_, sbank = (pz, bz) if jc % 2 == 0 else (pbx[bk], bk)
                    mm(sb_[:, 0:ncol], KI[:, jc * 128:(jc + 1) * 128], QI[:, isl], True, True, [("th1", d), ("th0", d)], [("pb", sbank)])
                    yield
                    tt("dve", SM[jc][:, 0:ncol], sb_[:, 0:ncol], msl, ALU.mult, [("pb", sbank), "tri"], [("SM", d, jc)])
                    yield
                for jc in range(4):
                    mm(kv_ps, KTt[d][:, jc, :], Vn(j * 4 + jc), jc == 0, jc == 3, [("KT", d), ("Vh", j)], [("pb", bk)])
                yield
                for ec, bo in ((0, bo0), (1, bo1)):
                    for idx, jc in enumerate(jcs):
                        isl = slice(jc * 128, T) if d == 0 else slice(0, (jc + 1) * 128)
                        ncol = isl.stop - isl.start
                        mm(pbx[bo][:, isl], Vn(j * 4 + jc)[:, ec * 128:(ec + 1) * 128], SM[jc][:, 0:ncol], idx == 0, first and idx == 3,
                           [("Vh", j), ("SM", d, jc)], [("pb", bo)])
                    if not first:
                        mm(pbx[bo], Sbf[d][:, ec * 128:(ec + 1) * 128], QI[:], False, True, [("Sb", d), ("th0", d)], [("pb", bo)])
                yield
                if first:
                    P.add("dve", lambda e, o=Sst[d][:], i=kv_ps: e.tensor_copy(out=o, in_=i), r=[("pb", bk)], w=[("S", d)])
                else:
                    stt("dve", Sst[d][:], Sst[d][:], DEC[d][:, j, :], kv_ps, ALU.mult, ALU.add,
                        [("S", d), ("DEC", d, j), ("pb", bk)], [("S", d)])
                yield
                act(Sbf[d][:], Sst[d][:], AF.Copy, [("S", d)], [("Sb", d)])
                firstO = (d == 0) == (2 * j <= NT - 1)
                okj = [[("O", ec, n) for n in range(j * 4, j * 4 + 4)] for ec in range(2)]
                for ec, bo in ((0, bo0), (1, bo1)):
                    if firstO:
                        act(O[ec][:, tsl(j)], pbx[bo], AF.Copy, [("pb", bo)], okj[ec])
                    else:
                        tt("dve", O[ec][:, tsl(j)], O[ec][:, tsl(j)], pbx[bo], ALU.add, [("pb", bo)] + okj[ec], okj[ec])
                first = False
                yield

        g0, g1 = gla_dir(0), gla_dir(1)
        for _ in range(10):
            next(g0)
        run_rr([g0, g1])
        for j in range(NT):
            b = j % 2
            okeys = [[("O", ec, n) for n in range(j * 4, j * 4 + 4)] for ec in range(2)]
            for ec in range(2):
                dma("sp", tmph[ec][b][:], SRs[(2 * h + ec) * 128:(2 * h + ec + 1) * 128, tsl(j)],
                    [("SRs", 2 * h + ec, j)], [("th%d" % ec, b)], "srl%d_%d" % (ec, b))
                act(sqs[:, ec, :], O[ec][:, tsl(j)], AF.Square, okeys[ec], [("sq1", ec)])
            for ec in range(2):
                mm(pb[6][:], ones[:], sqs[:, ec, :], ec == 0, ec == 1, ["ones", ("sq1", ec)], [("pb", 6)])
            rstd_from(pb[6][:], b, 256, mul=0.5)
            for ec in range(2):
                t4, t4k = wslot(1, ec)
                stt("dve", t4, O[ec][:, tsl(j)], pngt[:, ec:ec + 1], rs[b][:], ALU.mult, ALU.mult,
                    okeys[ec] + [("rs", b), "pngt"], t4k)
                tt("pool", hT[:, 2 * h + ec, tsl(j)], t4, tmph[ec][b][:], ALU.mult,
                   t4k + [("th%d" % ec, b)], khT(2 * h + ec, j * T, j * T + T))

    P.barrier()

    def mt_get1(j):
        return (lambda k: hT[:, k, tsl(j)]), [("hT", k, j) for k in range(8)]

    out_phase(mt_get1, 8, w_out1, 17, x1T, outT, "x0o", sq1, ["sq1all"])
    P.emit(nc, es)
    es.close()
    return nc, P


def host_inputs(x_b, inp):
    f = lambda a: np.ascontiguousarray(np.asarray(a, dtype=np.float32))
    vecs = [inp["even_norm_pre"][0], inp["even_norm_post"][0]] + [inp["rg_conv_w"][0, i] for i in range(4)] + \
           [inp["rg_conv_b"][0]] + [inp["rg_gate_b"][0, d, g].reshape(-1) for d in range(2) for g in range(2)] + \
           [inp["rg_lambda"][0, d] for d in range(2)] + [inp["sc_conv_w"][0, i] for i in range(3)] + \
           [inp["odd_norm_pre"][0], inp["odd_norm_post"][0]]
    pvec = f(np.stack([np.asarray(v) for v in vecs], 0).reshape(NV, 8, 128).transpose(2, 0, 1))
    pg = f(np.asarray(inp["gla_b_gate"][0]).reshape(2, 4, 128).transpose(2, 0, 1))
    png = f(np.asarray(inp["gla_norm_g"][0]).reshape(2, 128).transpose(1, 0))
    cm01 = np.ones((128, T), np.float32)
    jj, ii = np.meshgrid(np.arange(128), np.arange(128), indexing="ij")
    mf = np.ones((128, T), np.float32)
    mf[:, 0:128] = (jj <= ii)
    mb = np.ones((128, T), np.float32)
    mb[:, T - 128:T] = (jj >= ii)
    ctri = f(np.stack([mf, mb], 1))
    return {
        "xT": f(np.asarray(x_b).T), "w_in0": f(inp["even_w_in"][0]), "gate_w": f(inp["rg_gate_w"][0]),
        "w_out0": f(inp["even_w_out"][0]), "w_in1": f(inp["odd_w_in"][0]), "wg": f(inp["gla_w_gate_lr"][0]),
        "w_out1": f(inp["odd_w_out"][0]), "pvec": pvec, "pg": pg, "png": png, "cm01": cm01, "ctri": ctri,
        "cones": np.ones((128, 128), np.float32), "cident": np.eye(128, dtype=np.float32),
    }


def kernel(**inputs):
    x = np.asarray(inputs["x"])
    B, SEQ, _ = x.shape
    nc, _ = build_nc(SEQ)
    shared = None
    in_maps = []
    for b in range(B):
        m = host_inputs(x[b], inputs)
        if shared is None:
            shared = m
        else:
            for k in m:
                if k != "xT":
                    m[k] = shared[k]
        in_maps.append(m)
    res = run_bass_kernel_spmd(nc, in_maps, core_ids=list(range(B)))
    out = np.stack([np.asarray(r["outT"]).T for r in res.results], 0)
    return np.ascontiguousarray(out.astype(np.float32))
```

```python
import numpy as np
from contextlib import ExitStack
import concourse.bass as bass
import concourse.mybir as mybir
from concourse.bass_utils import run_bass_kernel_spmd
from concourse.ap import AP

F32 = mybir.dt.float32
BF16 = mybir.dt.bfloat16
AF = mybir.ActivationFunctionType
ALU = mybir.AluOpType

D = 1024
T = 512
EPS = 1e-6
NV = 18
LAYERS = 2
SBUF_LEFT = []


def rev(ap):
    pairs = [list(p) for p in ap.ap]
    step, cnt = pairs[-1]
    pairs[-1] = [-step, cnt]
    return AP(ap.tensor, ap.offset + step * (cnt - 1), pairs)


class _Op:
    __slots__ = ("eng", "fn", "deps", "dma_key", "signal", "val", "idx", "semkey")


class Prog:
    def __init__(self):
        self.ops = []
        self.lastw = {}
        self.rd_eng = {}
        self.rd_dma = {}
        self.last_eng = {}
        self.last_dma = {}
        self.fence = []
        self.passed = set()

    def barrier(self):
        self.fence = list(self.last_eng.values()) + list(self.last_dma.values())
        self.passed = set()

    def add(self, eng, fn, r=(), w=(), dma_key=None):
        op = _Op()
        op.eng, op.fn, op.dma_key = eng, fn, dma_key
        op.signal = dma_key is not None
        op.val = 0
        op.semkey = None
        op.idx = len(self.ops)
        deps = {}
        if self.fence and eng not in self.passed:
            for p in self.fence:
                deps[p.idx] = p
            self.passed.add(eng)
        for k in r:
            p = self.lastw.get(k)
            if p is not None:
                deps[p.idx] = p
        for k in w:
            p = self.lastw.get(k)
            if p is not None:
                deps[p.idx] = p
            for q in self.rd_eng.get(k, {}).values():
                deps[q.idx] = q
            for q in self.rd_dma.get(k, ()):
                deps[q.idx] = q
        op.deps = [p for p in deps.values() if (p.dma_key is not None) or (p.eng != eng) or (eng != "pe")]
        for k in r:
            if dma_key is not None:
                self.rd_dma.setdefault(k, []).append(op)
            else:
                self.rd_eng.setdefault(k, {})[eng] = op
        for k in w:
            self.lastw[k] = op
            self.rd_eng[k] = {}
            self.rd_dma[k] = []
        if dma_key is not None:
            self.last_dma[dma_key] = op
        else:
            self.last_eng[eng] = op
        self.ops.append(op)
        return op

    def emit(self, nc, es):
        for op in self.ops:
            for p in op.deps:
                p.signal = True
        cnt = {}
        for op in self.ops:
            if op.signal:
                key = ("dma", op.dma_key) if op.dma_key is not None else ("eng", op.eng)
                cnt[key] = cnt.get(key, 0) + (16 if op.dma_key is not None else 1)
                op.semkey, op.val = key, cnt[key]
        sems = {}
        for i, key in enumerate(cnt):
            sems[key] = es.enter_context(nc.semaphore("s%d" % i))
        self.nsem = len(sems)
        block = es.enter_context(nc.Block())
        ops = self.ops

        def run(engname, final=False):
            def body(e):
                waited = {}
                for op in ops:
                    if op.eng != engname:
                        continue
                    for p in op.deps:
                        if waited.get(p.semkey, 0) < p.val:
                            e.wait_ge(sems[p.semkey], p.val)
                            waited[p.semkey] = p.val
                    ins = op.fn(e)
                    if op.signal:
                        ins.then_inc(sems[op.semkey], 16 if op.dma_key is not None else 1)
                if final:
                    for key, v in cnt.items():
                        if waited.get(key, 0) < v:
                            e.wait_ge(sems[key], v)
            return body

        block.sync(run("sp", final=True))
        block.tensor(run("pe"))
        block.scalar(run("act"))
        block.vector(run("dve"))
        block.gpsimd(run("pool"))


def build_nc(SEQ, layers=LAYERS):
    NT = SEQ // T
    NCH = SEQ // 128
    BW = max(SEQ, 8 * T)
    nc = bass.Bass("TRN2", target_bir_lowering=False)
    din = lambda n, s, dt=F32: nc.dram_tensor(n, s, dt, kind="ExternalInput").ap()
    xT = din("xT", [D, SEQ])
    w_in0 = din("w_in0", [D, 6144])
    gate_w = din("gate_w", [2, 2, 8, 128, 128])
    w_out0 = din("w_out0", [2048, D])
    w_in1 = din("w_in1", [D, 3104])
    wg = din("wg", [2, 16, 512])
    w_out1 = din("w_out1", [D, D])
    pvec = din("pvec", [128, NV, 8])
    pg = din("pg", [128, 2, 4])
    png = din("png", [128, 2])
    cm01 = din("cm01", [128, T])
    ctri = din("ctri", [128, 2, T])
    cones = din("cones", [128, 128])
    cident = din("cident", [128, 128])
    outT = nc.dram_tensor("outT", [D, SEQ], F32, kind="ExternalOutput").ap()
    mixed0 = nc.dram_tensor("mixed0", [2048, SEQ], BF16, kind="Internal").ap()
    x1T = nc.dram_tensor("x1T", [D, SEQ], F32, kind="Internal").ap()
    qTs = nc.dram_tensor("qTs", [512, SEQ], F32, kind="Internal").ap()
    kTs = nc.dram_tensor("kTs", [512, SEQ], F32, kind="Internal").ap()
    Vs = nc.dram_tensor("Vs", [SEQ, D], BF16, kind="Internal").ap()
    SRs = nc.dram_tensor("SRs", [D, SEQ], BF16, kind="Internal").ap()

    es = ExitStack()
    P = Prog()
    sb = lambda n, s, dt=F32: es.enter_context(nc.sbuf_tensor(n, s, dt))
    ps = lambda n, s, dt=F32: es.enter_context(nc.psum_tensor(n, s, dt))

    hT = sb("hT", [128, 8, SEQ], BF16)
    pv = sb("pv", [128, NV, 8])
    pgt = sb("pgt", [128, 2, 4])
    pngt = sb("pngt", [128, 2])
    m01 = sb("m01", [128, T], BF16)
    tri = sb("tri", [128, 2, T])
    ones = sb("ones", [128, 128], BF16)
    ident = sb("ident", [128, 128], BF16)
    der = sb("der", [128, 12, 8])
    negb = sb("negb", [128, 2, 4])
    gwh = [sb("gwh%d" % i, [128, 4, 128], BF16) for i in range(2)]
    lrw = sb("lrw", [128, 8, 32], BF16)
    wgt = sb("wgt", [16, 2, 512], BF16)
    lrt = [sb("lrt%d" % d, [16, T], BF16) for d in range(2)]
    rs = [sb("rs%d" % i, [128, T]) for i in range(2)]
    wb = [sb("wb%d" % i, [128, 8, 1024], BF16) for i in range(2)]
    wbf = [wb[i][:].rearrange("p k n -> p (k n)").bitcast(F32) for i in range(2)]
    big = [sb("big%d" % i, [128, BW + 4]) for i in range(4)]
    bigh = [sb("bigh%d" % i, [128, BW], BF16) for i in range(2)]
    tmph = [[sb("tmph%d_%d" % (i, j), [128, T], BF16) for j in range(2)] for i in range(3)]
    zero1 = sb("zero1", [128, 1])
    Sst = [sb("Sst%d" % d, [128, 256]) for d in range(2)]
    Sbf = [sb("Sbf%d" % d, [128, 256], BF16) for d in range(2)]
    DEC = [sb("DEC%d" % d, [128, NT, 1]) for d in range(2)]
    KTt = [sb("KTt%d" % d, [128, 4, 128], BF16) for d in range(2)]
    xt = [big[2 + i][:, 0:8 * T].rearrange("p (k t) -> p k t", t=T) for i in range(2)]
    sqs = sb("sqs", [128, 2, T], BF16)

    SBUF_LEFT.append(nc.sbuf_bytes_remaining)
    pb = [ps("pb%d" % i, [128, T]) for i in range(7)]
    pbt = ps("pbt", [128, 1024], BF16)
    pbx = [pb[i][:] for i in range(7)] + [pbt[:].bitcast(F32)]
    pb6h = pb[6][:].bitcast(BF16)

    def kb(i, lo, hi):
        return [("big", i, g) for g in range(lo // T, (hi - 1) // T + 1)]

    def kh(i, lo, hi):
        return [("bigh", i, g) for g in range(lo // T, (hi - 1) // T + 1)]

    def khT(k, lo, hi):
        return [("hT", k, g) for g in range(lo // T, (hi - 1) // T + 1)]

    def bslot(i, n):
        return big[i][:, n * T:(n + 1) * T], [("big", i, n)]

    def wslot(i, n):
        return wbf[i][:, n * T:(n + 1) * T], [("wbf", i, n)]

    def bcast_last(ap, n):
        pairs = [list(p) for p in ap.ap]
        assert pairs[-1][1] == 1
        pairs[-1] = [0, n]
        return AP(ap.tensor, ap.offset, pairs)

    def dma(eng, out, in_, r, w, key):
        P.add(eng, lambda e: e.dma_start(out=out, in_=in_), r=r, w=w, dma_key=key)

    def act(out, in_, func, r, w, bias=0.0, scale=1.0):
        P.add("act", lambda e: e.activation(out=out, in_=in_, func=func, bias=bias, scale=scale), r=r, w=w)

    def tt(eng, out, in0, in1, op, r, w):
        P.add(eng, lambda e: e.tensor_tensor(out=out, in0=in0, in1=in1, op=op), r=r, w=w)

    def ts(eng, out, in0, s1, s2, op0, op1, r, w):
        if op1 is None:
            P.add(eng, lambda e: e.tensor_scalar(out=out, in0=in0, scalar1=s1, scalar2=None, op0=op0), r=r, w=w)
        else:
            P.add(eng, lambda e: e.tensor_scalar(out=out, in0=in0, scalar1=s1, scalar2=s2, op0=op0, op1=op1), r=r, w=w)

    def stt(eng, out, in0, s, in1, op0, op1, r, w):
        P.add(eng, lambda e: e.scalar_tensor_tensor(out=out, in0=in0, scalar=s, in1=in1, op0=op0, op1=op1), r=r, w=w)

    def mm(out, lhsT, rhs, start, stop, r, w):
        P.add("pe", lambda e: e.matmul(out, lhsT=lhsT, rhs=rhs, start=start, stop=stop), r=r, w=w)

    def scan(out, d0, d1, init, reverse, r, w):
        if reverse:
            P.add("dve", lambda e: e.tensor_tensor_scan(out=rev(out), data0=rev(d0), data1=rev(d1), initial=init,
                                                        op0=ALU.mult, op1=ALU.add), r=r, w=w)
        else:
            P.add("dve", lambda e: e.tensor_tensor_scan(out=out, data0=d0, data1=d1, initial=init,
                                                        op0=ALU.mult, op1=ALU.add), r=r, w=w)

    def tsl(j):
        return slice(j * T, (j + 1) * T)

    xtk = [kb(2 + i, 0, 8 * T) for i in range(2)]

    dma("sp", pv[:], pvec[:, :, :], [], ["pv"], "c0")
    dma("sp", pgt[:], pg[:, :, :], [], ["pgt"], "c1")
    dma("sp", pngt[:], png[:, :], [], ["pngt"], "c2")
    dma("pool", m01[:], cm01[:, :], [], ["m01"], "c3")
    dma("sp", tri[:], ctri[:, :, :], [], ["tri"], "c4")
    dma("pool", ones[:], cones[:, :], [], ["ones"], "c5")
    dma("pool", ident[:], cident[:, :], [], ["ident"], "c6")
    P.add("dve", lambda e: e.memset(zero1[:], 0.0), w=["zero1"])
    ts("dve", der[:, 0:4, :], pv[:, 7:11, :], 0.5, None, ALU.mult, None, ["pv"], ["der"])
    act(der[:, 8:10, :], pv[:, 11:13, :], AF.Exp, ["pv"], ["der8"], scale=-1.0)
    act(der[:, 10:12, :], der[:, 8:10, :], AF.Ln, ["der8"], ["der10"], bias=1.0)
    ts("dve", der[:, 4:6, :], der[:, 10:12, :], -4.0, None, ALU.mult, None, ["der10"], ["der"])
    ts("dve", der[:, 6:8, :], der[:, 10:12, :], -8.0, None, ALU.mult, None, ["der10"], ["der"])
    ts("dve", negb[:], pgt[:], -1.0, None, ALU.mult, None, ["pgt"], ["negb"])

    def rstd_from(psum_ap, b, n, mul=1.0):
        act(rs[b][:], psum_ap, AF.Ln, [("pb", 6)], [("rs", b)], bias=EPS, scale=1.0 / n)
        act(rs[b][:], rs[b][:], AF.Exp, [("rs", b)], [("rs", b)], scale=-0.5, bias=float(np.log(mul)))

    def norm_phase(src, gidx, tag):
        srcv = src.rearrange("(k p) t -> p k t", p=128)
        sq = bigh[1][:, 0:8 * T].rearrange("p (k t) -> p k t", t=T)
        sqk = kh(1, 0, 8 * T)
        for j in range(NT):
            b = j % 2
            dma("sp", xt[b], srcv[:, :, tsl(j)], [("src", tag, j)], xtk[b], "xt%d" % b)
            act(sq, xt[b], AF.Square, xtk[b], sqk)
            for k in range(8):
                mm(pb[6][:], ones[:], sq[:, k, :], k == 0, k == 7, ["ones"] + sqk, [("pb", 6)])
            rstd_from(pb[6][:], b, D)
            for k in range(8):
                stt("dve", hT[:, k, tsl(j)], xt[b][:, k, :], pv[:, gidx, k:k + 1], rs[b][:], ALU.mult, ALU.mult,
                    xtk[b] + [("rs", b), "pv"], khT(k, j * T, (j + 1) * T))

    def out_phase(mt_get, nk, wsrc, gidx, resid, dst, tag, sq, sqk, skip=()):
        wo = wsrc.rearrange("(k p) n -> p k n", p=128)
        for k0 in range(0, nk, 4):
            if k0 in skip:
                continue
            s, q = k0 // 8, (k0 % 8) // 4
            dma("pool", wb[s][:, q * 4:q * 4 + 4, :], wo[:, k0:k0 + 4, :], [], [("wb", s, q)], "wb%d_%d" % (s, q))
        resv = resid.rearrange("(k p) t -> p k t", p=128)
        dstv = dst.rearrange("(k p) t -> p k t", p=128)
        ysb = [big[i][:, 0:8 * T].rearrange("p (k t) -> p k t", t=T) for i in range(2)]
        def loads(j):
            bq = j % 2
            r_ = mt_get(j)
            dma("sp", xt[bq], resv[:, :, tsl(j)], [("src", tag, j)], xtk[bq], "xt%d" % bq)
            return r_

        pending = loads(0)
        for j in range(NT):
            b = j % 2
            ys = ysb[b]
            mget, mkeys = pending
            if j + 1 < NT:
                pending = loads(j + 1)
            for o in range(8):
                bank = o % 4
                for k in range(nk):
                    mm(pb[bank][:], wb[k // 8][:, k % 8, o * 128:(o + 1) * 128], mget(k), k == 0, k == nk - 1,
                       [("wb", k // 8, (k % 8) // 4)] + mkeys, [("pb", bank)])
                act(ys[:, o, :], pb[bank][:], AF.Copy, [("pb", bank)], [("big", b, o)])
                act(sq[:, o, :], pb[bank][:], AF.Square, [("pb", bank)], sqk)
            for o in range(8):
                mm(pb[6][:], ones[:], sq[:, o, :], o == 0, o == 7, ["ones"] + sqk, [("pb", 6)])
            rstd_from(pb[6][:], b, D)
            for o in range(8):
                stt("dve", ys[:, o, :], ys[:, o, :], pv[:, gidx, o:o + 1], rs[b][:], ALU.mult, ALU.mult,
                    [("big", b, o), ("rs", b), "pv"], [("big", b, o)])
                tt("pool", ys[:, o, :], ys[:, o, :], xt[b][:, o, :], ALU.add, [("big", b, o)] + xtk[b], [("big", b, o)])
            dma("sp", dstv[:, :, tsl(j)], ys, [("big", b, o) for o in range(8)], [("src", tag + "o", j)], "xo%d" % b)

    norm_phase(xT, 0, "x0")
    w0 = w_in0.rearrange("(k p) n -> p k n", p=128)
    XA, UA, HB, CVb = big[0], big[1], big[2], big[3]
    UAb, SZ = bigh[0], bigh[1]
    P.add("pool", lambda e: e.memset(XA[:, 0:2], 0.0), w=kb(0, 0, 2))
    P.add("pool", lambda e: e.memset(XA[:, SEQ + 2:SEQ + 4], 0.0), w=kb(0, SEQ + 2, SEQ + 4))

    def rgA_front(c):
        s = c % 2
        wA = wb[0][:, :, s * 256:(s + 1) * 256]
        dma("pool", wA[:, :, 0:128], w0[:, :, c * 128:(c + 1) * 128], [], [("wA", s, 0)], "wb%d_0" % s)
        dma("pool", wA[:, :, 128:256], w0[:, :, 1024 + c * 128:1024 + (c + 1) * 128], [], [("wA", s, 1)], "wb%d_1" % s)
        dma("pool", gwh[s][:], gate_w[:, :, c].rearrange("d g i j -> i (d g) j"), [], [("gwh", s)], "gw%d" % s)
        yield
        for j in range(NT):
            bank = 6 + (j % 2)
            for k in range(8):
                mm(pbx[bank], wA[:, k, 0:128], hT[:, k, tsl(j)], k == 0, k == 7, [("wA", s, 0)] + khT(k, j * T, j * T + T), [("pb", bank)])
            yield
            act(XA[:, 2 + j * T:2 + (j + 1) * T], pbx[bank], AF.Copy, [("pb", bank)], kb(0, 2 + j * T, 2 + (j + 1) * T))
            yield

    def rgA_conv(c):
        allXA = kb(0, 0, SEQ + 4)
        allUA = kb(1, 0, SEQ)
        ts("dve", UA[:, 0:SEQ], XA[:, 0:SEQ], pv[:, 2, c:c + 1], pv[:, 6, c:c + 1], ALU.mult, ALU.add, allXA + ["pv"], allUA)
        for kk in range(1, 4):
            stt("dve", UA[:, 0:SEQ], XA[:, kk:kk + SEQ], pv[:, 2 + kk, c:c + 1], UA[:, 0:SEQ], ALU.mult, ALU.add, allXA + allUA + ["pv"], allUA)
        hlf = SEQ // 2
        act(UAb[:, 0:hlf], UA[:, 0:hlf], AF.Copy, kb(1, 0, hlf), kh(0, 0, hlf))
        P.add("pool", lambda e: e.tensor_copy(out=UAb[:, hlf:SEQ], in_=UA[:, hlf:SEQ]), r=kb(1, hlf, SEQ), w=kh(0, hlf, SEQ))

    def rg_dir(d, c):
        s = c % 2
        cs = slice(c * 128, (c + 1) * 128)
        wA = wb[0][:, :, s * 256:(s + 1) * 256]

        def slots(b):
            if d == 1:
                return [bslot(3, i * 2 + b) for i in range(4)]
            return [wslot(1, i * 2 + b) for i in range(4)]

        br, bi = (2, 3) if d == 1 else (0, 1)
        g_r = gwh[s][:, d * 2 + 0, :]
        g_i = gwh[s][:, d * 2 + 1, :]
        prev = None
        pend = []
        for t in range(NT):
            j = t if d == 0 else NT - 1 - t
            t_other = NT - 1 - j if d == 0 else j
            store = (t < t_other) or (t == t_other and d == 1)
            b = j % 2
            uk = kh(0, j * T, j * T + T)
            (THr, k0), (THi, k1), (A, k2), (Ht, k3) = slots(b)
            mm(pbx[br], g_r, UAb[:, tsl(j)], True, True, [("gwh", s)] + uk, [("pb", br)])
            act(THr, pbx[br], AF.Tanh, [("pb", br), "der"], k0, bias=der[:, d * 2 + 0, c:c + 1], scale=0.5)
            mm(pbx[bi], g_i, UAb[:, tsl(j)], True, True, [("gwh", s)] + uk, [("pb", bi)])
            act(THi, pbx[bi], AF.Tanh, [("pb", bi), "der"], k1, bias=der[:, d * 2 + 1, c:c + 1], scale=0.5)
            act(A, THr, AF.Exp, k0 + ["der"], k2, bias=der[:, 4 + d, c:c + 1], scale=der[:, 4 + d, c:c + 1])
            tt("dve", THr, A, A, ALU.mult, k2, k0)
            ts("pool", THr, THr, -1.0 / 16, 1.0 / 16, ALU.mult, ALU.add, k0, k0)
            stt("dve", THi, THi, 1.0, UA[:, tsl(j)], ALU.add, ALU.mult, k1 + kb(1, j * T, j * T + T), k1)
            pend.append((j, store))
            if t % 2 == 0 and t + 1 < NT:
                continue
            yield
            for (jj, _) in pend:
                (THr2, k02) = slots(jj % 2)[0]
                act(THr2, THr2, AF.Sqrt, k02, k02)
            yield
            for (jj, st) in pend:
                bb = jj % 2
                (THr2, k02), (THi2, k12), (A2, k22), (Ht2, k32) = slots(bb)
                hk2 = kb(2, jj * T, jj * T + T)
                tt("pool", THi2, THr2, THi2, ALU.mult, k02 + k12, k12)
                if prev is None:
                    init, rk = zero1[:, 0:1], ["zero1"]
                elif prev[1]:
                    col = prev[0] * T + (T - 1 if d == 0 else 0)
                    init, rk = HB[:, col:col + 1], kb(2, col, col + 1)
                else:
                    pslot, pk = slots(prev[0] % 2)[3]
                    init, rk = (pslot[:, T - 1:T] if d == 0 else pslot[:, 0:1]), pk
                if st:
                    scan(HB[:, tsl(jj)], A2, THi2, init, d == 1, k22 + k12 + rk, hk2)
                else:
                    zb = 4 + bb
                    for k in range(8):
                        mm(pbx[zb], wA[:, k, 128:256], hT[:, k, tsl(jj)], k == 0, k == 7,
                           [("wA", s, 1)] + khT(k, jj * T, jj * T + T), [("pb", zb)])
                    scan(Ht2, A2, THi2, init, d == 1, k22 + k12 + rk, k32)
                    act(THr2, pbx[zb], AF.Tanh, [("pb", zb)], k02, scale=0.5)
                    stt("dve", THr2, THr2, 1.0, pbx[zb], ALU.add, ALU.mult, k02 + [("pb", zb)], k02)
                    tt("dve", A2, Ht2, HB[:, tsl(jj)], ALU.add, k32 + hk2, k22)
                    ob, okk = tmph[d][bb], [("th%d" % d, bb)]
                    tt("pool", ob[:], A2, THr2, ALU.mult, k22 + k02, okk)
                    dma("sp", mixed0[cs, tsl(jj)], ob[:], okk, [("mixed0", c, jj)], "ya%d_%d" % (d, bb))
                prev = (jj, st)
            del pend[:]
            yield

    def run_rr(gens):
        live = list(gens)
        while live:
            for g in list(live):
                try:
                    next(g)
                except StopIteration:
                    live.remove(g)

    run_rr([rgA_front(0)])
    rgA_conv(0)
    for c in range(8):
        gens = [rg_dir(1, c), rg_dir(0, c)]
        if c + 1 < 8:
            gens.append(rgA_front(c + 1))
        run_rr(gens)
        if c + 1 < 8:
            rgA_conv(c + 1)

    Pb, GB = big[0], big[1]
    SZB, YB = bigh[0], bigh[1]
    P.add("pool", lambda e: e.memset(Pb[:, 0:1], 0.0), w=kb(0, 0, 1))
    P.add("pool", lambda e: e.memset(Pb[:, SEQ + 1:SEQ + 2], 0.0), w=kb(0, SEQ + 1, SEQ + 2))
    def wB_load(c):
        s = c % 2
        wB = wb[1][:, :, s * 512:(s + 1) * 512]
        for q in range(4):
            dma("pool", wB[:, :, q * 128:(q + 1) * 128], w0[:, :, (2 + q) * 1024 + c * 128:(2 + q) * 1024 + (c + 1) * 128],
                [], [("wB", s, q)] + [("wbf", 1, n) for n in range(8)], "wB%d_%d" % (s, q))

    wB_load(0)
    wo0 = w_out0.rearrange("(k p) n -> p k n", p=128)
    for k0 in (0, 4):
        q = k0 // 4
        dma("pool", wb[0][:, q * 4:q * 4 + 4, :], wo0[:, k0:k0 + 4, :], [],
            [("wb", 0, q), ("wA", 0, 0), ("wA", 0, 1), ("wA", 1, 0), ("wA", 1, 1)], "wb0_%d" % q)
    for c in range(8):
        s = c % 2
        wB = wb[1][:, :, s * 512:(s + 1) * 512]
        if c + 1 < 8:
            wB_load(c + 1)
        for j in range(NT):
            b = j % 2
            for q in range(4):
                for k in range(8):
                    mm(pb[q][:], wB[:, k, q * 128:(q + 1) * 128], hT[:, k, tsl(j)], k == 0, k == 7,
                       [("wB", s, q)] + khT(k, j * T, j * T + T), [("pb", q)])
            xbs, xbk = bslot(2, b)
            act(xbs, pb[0][:], AF.Copy, [("pb", 0)], xbk)
            tt("dve", Pb[:, 1 + j * T:1 + (j + 1) * T], pb[2][:], xbs, ALU.mult, [("pb", 2)] + xbk, kb(0, 1 + j * T, 1 + (j + 1) * T))
            act(GB[:, tsl(j)], pb[1][:], AF.Copy, [("pb", 1)], kb(1, j * T, j * T + T))
            szk = kh(0, j * T, j * T + T)
            act(SZB[:, tsl(j)], pb[3][:], AF.Tanh, [("pb", 3)], szk, scale=0.5)
            stt("dve", SZB[:, tsl(j)], SZB[:, tsl(j)], 1.0, pb[3][:], ALU.add, ALU.mult, szk + [("pb", 3)], szk)
            for jj in ([j - 1] if j >= 1 else []) + ([j] if j == NT - 1 else []):
                cvt, cvk = bslot(3, jj % 2)
                lo = jj * T
                ts("dve", cvt, Pb[:, lo:lo + T], pv[:, 13, c:c + 1], None, ALU.mult, None, kb(0, lo, lo + T) + ["pv"], cvk)
                for kk in range(1, 3):
                    stt("dve", cvt, Pb[:, lo + kk:lo + kk + T], pv[:, 13 + kk, c:c + 1], cvt, ALU.mult, ALU.add,
                        kb(0, lo + kk, lo + kk + T) + cvk + ["pv"], cvk)
                stt("dve", cvt, cvt, 0.5, GB[:, tsl(jj)], ALU.mult, ALU.mult, cvk + kb(1, lo, lo + T), cvk)
                tt("pool", YB[:, tsl(jj)], cvt, SZB[:, tsl(jj)], ALU.mult, cvk + kh(0, lo, lo + T), kh(1, lo, lo + T))
        dma("sp", mixed0[1024 + c * 128:1024 + (c + 1) * 128, :], YB[:, 0:SEQ], kh(1, 0, SEQ), [("mixed0", 8 + c, j) for j in range(NT)], "yb")

    P.barrier()
    m0v = mixed0.rearrange("(k p) t -> p k t", p=128)
    mtb = [bigh[i][:, 0:8 * T].rearrange("p (k t) -> p k t", t=T) for i in range(2)]

    def mt_get0(j):
        b = j % 2
        lo = mtb[b]
        hi = hT[:, :, b * T:(b + 1) * T]
        rk = [("mixed0", c, j) for c in range(16)]
        dma("sp", lo, m0v[:, 0:8, tsl(j)], rk, [("mtlo", b)], "mt%d" % b)
        dma("sp", hi, m0v[:, 8:16, tsl(j)], rk, [("mthi", b)], "mth%d" % b)
        return (lambda k: lo[:, k, :] if k < 8 else hi[:, k - 8, :]), [("mtlo", b), ("mthi", b)]

    out_phase(mt_get0, 16, w_out0, 1, xT, x1T if layers > 1 else outT, "x0", hT[:, :, 2 * T:3 * T], ["sqh"], skip=(0, 4))
    if layers == 1:
        P.emit(nc, es)
        es.close()
        return nc, P

    P.barrier()
    norm_phase(x1T, 16, "x0o")
    w1 = w_in1.rearrange("(k p) n -> p k n", p=128)
    LRs = nc.dram_tensor("LRs", [2, 16, SEQ], BF16, kind="Internal").ap()
    dma("pool", lrw[:], w1[:, :, 3072:3104], [], ["lrw"], "c8")
    dma("pool", wgt[:], wg.rearrange("d r k -> r d k"), [], ["wgt"], "c9")

    for q in range(2):
        dma("pool", wb[0][:, :, q * 512:(q + 1) * 512], w1[:, :, q * 512:(q + 1) * 512], [], [("wb", 0, q)], "wb0_%d" % q)
    for oc in range(8):
        dst = qTs if oc < 4 else kTs
        for j in range(NT):
            b = j % 2
            bank = (oc * NT + j) % 4
            for k in range(8):
                mm(pb[bank][:], wb[0][:, k, oc * 128:(oc + 1) * 128], hT[:, k, tsl(j)], k == 0, k == 7,
                   [("wb", 0, oc // 4)] + khT(k, j * T, j * T + T), [("pb", bank)])
            st, stk = bslot(bank, b)
            act(st, pb[bank][:], AF.Copy, [("pb", bank)], stk)
            dma("sp", dst[(oc % 4) * 128:(oc % 4 + 1) * 128, tsl(j)], st, stk, [("qk", oc, j)], "st%d_%d" % (bank, b))
    for q in range(2):
        dma("pool", wb[1][:, :, q * 512:(q + 1) * 512], w1[:, :, 1024 + q * 512:1024 + (q + 1) * 512], [], [("wb", 1, q)], "wb1_%d" % q)
    Vsv = Vs.rearrange("(m p) e -> p m e", p=128)
    Vt = [bigh[i][:, 0:4096].rearrange("p (m e) -> p m e", e=1024) for i in range(2)]
    for j in range(NT):
        b = j % 2
        for m in range(4):
            for half in range(2):
                bank = (m * 2 + half) % 4
                for k in range(8):
                    mm(pb[bank][:], hT[:, k, j * T + m * 128:j * T + (m + 1) * 128], wb[1][:, k, half * 512:(half + 1) * 512],
                       k == 0, k == 7, [("wb", 1, half)] + khT(k, j * T, j * T + T), [("pb", bank)])
                act(Vt[b][:, m, half * 512:(half + 1) * 512], pb[bank][:], AF.Copy, [("pb", bank)], [("Vt", b, m, half)])
        dma("sp", Vsv[:, j * 4:(j + 1) * 4, :], Vt[b], [("Vt", b, m, hf) for m in range(4) for hf in range(2)],
            [("Vs", j)], "vt%d" % b)
    for q in range(2):
        dma("pool", wb[0][:, :, q * 512:(q + 1) * 512], w1[:, :, 2048 + q * 512:2048 + (q + 1) * 512], [], [("wb", 0, q)], "wb0_%d" % q)
    for oc in range(8):
        for j in range(NT):
            b = j % 2
            bank = (oc * NT + j) % 4
            for k in range(8):
                mm(pb[bank][:], wb[0][:, k, oc * 128:(oc + 1) * 128], hT[:, k, tsl(j)], k == 0, k == 7,
                   [("wb", 0, oc // 4)] + khT(k, j * T, j * T + T), [("pb", bank)])
            stg = tmph[bank % 2][b]
            sk = [("th%d" % (bank % 2), b)]
            act(stg[:], pb[bank][:], AF.Tanh, [("pb", bank)], sk, scale=0.5)
            stt("dve", stg[:], stg[:], 1.0, pb[bank][:], ALU.add, ALU.mult, sk + [("pb", bank)], sk)
            dma("sp", SRs[oc * 128:(oc + 1) * 128, tsl(j)], stg[:], sk, [("SRs", oc, j)], "sr%d_%d" % (bank % 2, b))
    for d in range(2):
        for j in range(NT):
            b = j % 2
            bank = 4 + b
            for k in range(8):
                mm(pb[bank][0:16, :], lrw[:, k, d * 16:(d + 1) * 16], hT[:, k, tsl(j)], k == 0, k == 7,
                   ["lrw"] + khT(k, j * T, j * T + T), [("pb", bank)])
            act(tmph[2][b][0:16, :], pb[bank][0:16, :], AF.Copy, [("pb", bank)], [("th2", b)])
            dma("sp", LRs[d, :, tsl(j)], tmph[2][b][0:16, :], [("th2", b)], [("LRs", d, j)], "lrs%d" % b)

    P.barrier()
    SC = 128.0 ** -0.5
    qh, kh_, O = big[0], big[1], [big[2], big[3]]
    HC = NCH // 2
    Vhh = [bigh[i][:, 0:HC * 256].rearrange("p (m e) -> p m e", e=256) for i in range(2)]

    def Vn(n):
        return Vhh[n // HC][:, n % HC, :]

    sq1 = bigh[1][:, 0:8 * T].rearrange("p (k t) -> p k t", t=T)
    for h in range(4):
        lorder = []
        for s_ in range(NT):
            for j_ in (s_, NT - 1 - s_):
                if j_ not in lorder:
                    lorder.append(j_)
        for j_ in lorder:
            dma("sp", qh[:, tsl(j_)], qTs[h * 128:(h + 1) * 128, tsl(j_)], [("qk", h, j_)], [("qh", j_)], "qh%d" % j_)
            dma("sp", kh_[:, tsl(j_)], kTs[h * 128:(h + 1) * 128, tsl(j_)], [("qk", 4 + h, j_)], [("kh", j_)], "kh%d" % j_)
            n0 = j_ * 4
            dma("sp", Vhh[n0 // HC][:, n0 % HC:n0 % HC + 4, :], Vsv[:, n0:n0 + 4, h * 256:(h + 1) * 256], [("Vs", j_)], [("Vh", j_)], "vh%d" % j_)
        def gla_dir(d, h=h):
            first = True
            for sidx in range(NT):
                j = sidx if d == 0 else NT - 1 - sidx
                (Lt, k0), (Lc, k1), (E1, k2), (E2, k3) = [wslot(0, i * 2 + d) for i in range(4)]
                QI, KI, KST = tmph[0][d], tmph[1][d], tmph[2][d]
                bz, bo0, bo1, bk = 0 + d, 2 + d, 4 + d, 6 + d
                pz = pbx[bz]
                dma("sp", lrt[d][:], LRs[d, :, tsl(j)], [("LRs", d, j)], [("lrt", d)], "lrt%d" % d)
                mm(pz, wgt[0:16, d, h * 128:(h + 1) * 128], lrt[d][:], True, True, ["wgt", ("lrt", d)], [("pb", bz)])
                yield
                act(Lt, pz, AF.Exp, [("pb", bz), "negb"], k0, bias=negb[:, d, h:h + 1], scale=-1.0)
                act(Lt, Lt, AF.Ln, k0, k0, bias=1.0)
                yield
                scan(Lc, m01[:, :], Lt, 0.0, d == 1, k0 + ["m01"], k1)
                yield
                tot = Lc[:, T - 1:T] if d == 0 else Lc[:, 0:1]
                act(E1, Lc, AF.Exp, k1, k2, scale=-1.0 / 16)
                act(E2, Lc, AF.Exp, k1, k3, scale=1.0 / 16)
                act(DEC[d][:, j, :], tot, AF.Exp, k1, [("DEC", d, j)], scale=-1.0 / 16)
                yield
                stt("dve", QI[:], qh[:, tsl(j)], SC, E1, ALU.mult, ALU.mult, [("qh", j)] + k2, [("th0", d)])
                tt("pool", KI[:], kh_[:, tsl(j)], E2, ALU.mult, [("kh", j)] + k3, [("th1", d)])
                yield
                stt("dve", KST[:], kh_[:, tsl(j)], DEC[d][:, j, :], E2, ALU.mult, ALU.mult, [("kh", j), ("DEC", d, j)] + k3, [("th2", d)])
                yield
                kt_ps = (pb6h if d == 0 else pbt)[:, 0:T]
                kv_ps = pbx[bk][:, 256:512]
                jcs = list(range(4)) if d == 0 else list(range(3, -1, -1))
                SM = [wb[1][:, 2 + (d * 4 + jc) // 2, ((d * 4 + jc) % 2) * T:((d * 4 + jc) % 2 + 1) * T] for jc in range(4)]
                for jc in range(4):
                    P.add("pe", lambda e, o=kt_ps[:, jc * 128:(jc + 1) * 128], i=KST[:, jc * 128:(jc + 1) * 128]:
                          e.transpose(out=o, in_=i, identity=ident[:]), r=[("th2", d), "ident"], w=[("pb", bk)])
                yield
                act(KTt[d][:].rearrange("p c t -> p (c t)"), kt_ps, AF.Copy, [("pb", bk)], [("KT", d)])
                yield
                for jc in jcs:
                    if d == 0:
                        isl, msl = slice(jc * 128, T), tri[:, 0, 0:(4 - jc) * 128]
                    else:
                        isl, msl = slice(0, (jc + 1) * 128), tri[:, 1, (3 - jc) * 128:T]
                    ncol = isl.stop - isl.start
                    sb_, sbank = (pz, bz) if jc % 2 == 0 else (pbx[bk], bk)
                    mm(sb_[:, 0:ncol], KI[:, jc * 128:(jc + 1) * 128], QI[:, isl], True, True, [("th1", d), ("th0", d)], [("pb", sbank)])
                    yield
                    tt("dve", SM[jc][:, 0:ncol], sb_[:, 0:ncol], msl, ALU.mult, [("pb", sbank), "tri"], [("SM", d, jc)])
                    yield
                for jc in range(4):
                    mm(kv_ps, KTt[d][:, jc, :], Vn(j * 4 + jc), jc == 0, jc == 3, [("KT", d), ("Vh", j)], [("pb", bk)])
                yield
                for ec, bo in ((0, bo0), (1, bo1)):
                    for idx, jc in enumerate(jcs):
                        isl = slice(jc * 128, T) if d == 0 else slice(0, (jc + 1) * 128)
                        ncol = isl.stop - isl.start
                        mm(pbx[bo][:, isl], Vn(j * 4 + jc)[:, ec * 128:(ec + 1) * 128], SM[jc][:, 0:ncol], idx == 0, first and idx == 3,
                           [("Vh", j), ("SM", d, jc)], [("pb", bo)])
                    if not first:
                        mm(pbx[bo], Sbf[d][:, ec * 128:(ec + 1) * 128], QI[:], False, True, [("Sb", d), ("th0", d)], [("pb", bo)])
                yield
                if first:
                    P.add("dve", lambda e, o=Sst[d][:], i=kv_ps: e.tensor_copy(out=o, in_=i), r=[("pb", bk)], w=[("S", d)])
                else:
                    stt("dve", Sst[d][:], Sst[d][:], DEC[d][:, j, :], kv_ps, ALU.mult, ALU.add,
                        [("S", d), ("DEC", d, j), ("pb", bk)], [("S", d)])
                yield
                act(Sbf[d][:], Sst[d][:], AF.Copy, [("S", d)], [("Sb", d)])
                firstO = (d == 0) == (2 * j <= NT - 1)
                okj = [[("O", ec, n) for n in range(j * 4, j * 4 + 4)] for ec in range(2)]
                for ec, bo in ((0, bo0), (1, bo1)):
                    if firstO:
                        act(O[ec][:, tsl(j)], pbx[bo], AF.Copy, [("pb", bo)], okj[ec])
                    else:
                        tt("dve", O[ec][:, tsl(j)], O[ec][:, tsl(j)], pbx[bo], ALU.add, [("pb", bo)] + okj[ec], okj[ec])
                first = False
                yield

        g0, g1 = gla_dir(0), gla_dir(1)
        for _ in range(10):
            next(g0)
        run_rr([g0, g1])
        for j in range(NT):
            b = j % 2
            okeys = [[("O", ec, n) for n in range(j * 4, j * 4 + 4)] for ec in range(2)]
            for ec in range(2):
                dma("sp", tmph[ec][b][:], SRs[(2 * h + ec) * 128:(2 * h + ec + 1) * 128, tsl(j)],
                    [("SRs", 2 * h + ec, j)], [("th%d" % ec, b)], "srl%d_%d" % (ec, b))
                act(sqs[:, ec, :], O[ec][:, tsl(j)], AF.Square, okeys[ec], [("sq1", ec)])
            for ec in range(2):
                mm(pb[6][:], ones[:], sqs[:, ec, :], ec == 0, ec == 1, ["ones", ("sq1", ec)], [("pb", 6)])
            rstd_from(pb[6][:], b, 256, mul=0.5)
            for ec in range(2):
                t4, t4k = wslot(1, ec)
                stt("dve", t4, O[ec][:, tsl(j)], pngt[:, ec:ec + 1], rs[b][:], ALU.mult, ALU.mult,
                    okeys[ec] + [("rs", b), "pngt"], t4k)
                tt("pool", hT[:, 2 * h + ec, tsl(j)], t4, tmph[ec][b][:], ALU.mult,
                   t4k + [("th%d" % ec, b)], khT(2 * h + ec, j * T, j * T + T))

    P.barrier()

    def mt_get1(j):
        return (lambda k: hT[:, k, tsl(j)]), [("hT", k, j) for k in range(8)]

    out_phase(mt_get1, 8, w_out1, 17, x1T, outT, "x0o", sq1, ["sq1all"])
    P.emit(nc, es)
    es.close()
    return nc, P


def host_inputs(x_b, inp):
    f = lambda a: np.ascontiguousarray(np.asarray(a, dtype=np.float32))
    vecs = [inp["even_norm_pre"][0], inp["even_norm_post"][0]] + [inp["rg_conv_w"][0, i] for i in range(4)] + \
           [inp["rg_conv_b"][0]] + [inp["rg_gate_b"][0, d, g].reshape(-1) for d in range(2) for g in range(2)] + \
           [inp["rg_lambda"][0, d] for d in range(2)] + [inp["sc_conv_w"][0, i] for i in range(3)] + \
           [inp["odd_norm_pre"][0], inp["odd_norm_post"][0]]
    pvec = f(np.stack([np.asarray(v) for v in vecs], 0).reshape(NV, 8, 128).transpose(2, 0, 1))
    pg = f(np.asarray(inp["gla_b_gate"][0]).reshape(2, 4, 128).transpose(2, 0, 1))
    png = f(np.asarray(inp["gla_norm_g"][0]).reshape(2, 128).transpose(1, 0))
    cm01 = np.ones((128, T), np.float32)
    jj, ii = np.meshgrid(np.arange(128), np.arange(128), indexing="ij")
    mf = np.ones((128, T), np.float32)
    mf[:, 0:128] = (jj <= ii)
    mb = np.ones((128, T), np.float32)
    mb[:, T - 128:T] = (jj >= ii)
    ctri = f(np.stack([mf, mb], 1))
    return {
        "xT": f(np.asarray(x_b).T), "w_in0": f(inp["even_w_in"][0]), "gate_w": f(inp["rg_gate_w"][0]),
        "w_out0": f(inp["even_w_out"][0]), "w_in1": f(inp["odd_w_in"][0]), "wg": f(inp["gla_w_gate_lr"][0]),
        "w_out1": f(inp["odd_w_out"][0]), "pvec": pvec, "pg": pg, "png": png, "cm01": cm01, "ctri": ctri,
        "cones": np.ones((128, 128), np.float32), "cident": np.eye(128, dtype=np.float32),
    }


def kernel(**inputs):
    x = np.asarray(inputs["x"])
    B, SEQ, _ = x.shape
    nc, _ = build_nc(SEQ)
    shared = None
    in_maps = []
    for b in range(B):
        m = host_inputs(x[b], inputs)
        if shared is None:
            shared = m
        else:
            for k in m:
                if k != "xT":
                    m[k] = shared[k]
        in_maps.append(m)
    res = run_bass_kernel_spmd(nc, in_maps, core_ids=list(range(B)))
    out = np.stack([np.asarray(r["outT"]).T for r in res.results], 0)
    return np.ascontiguousarray(out.astype(np.float32))
```
